# Optimizing a Trainium2 kernel written in Bass

```python
import math
import jax, jax.numpy as jnp
from jax import lax
import numpy as np

D_MODEL = 1024
BATCH = 8
SEQ = 8192
DEPTH = 2
DEC_BATCH = 4
DEC_SEQ = 4096
PAST_LEN = 128

N_MIXERS = 2
N_ATTN_LAYERS = (DEPTH + 1) // 2
N_SSM_LAYERS = DEPTH // 2
HEAD_DIM = 64
N_Q_HEADS = D_MODEL // HEAD_DIM
N_KV_HEADS = 4
Q_PER_KV = N_Q_HEADS // N_KV_HEADS
WINDOW = 128
ATTN_BLOCK = 128
ROPE_THETA = 10000.0
QKV_DIM = (N_Q_HEADS + 2 * N_KV_HEADS) * HEAD_DIM
SSM_EXPAND = 2
D_INNER = SSM_EXPAND * D_MODEL
SSM_HEAD_DIM = 64
N_SSM_HEADS = D_INNER // SSM_HEAD_DIM
N_SSM_GROUPS = 4
HEADS_PER_GROUP = N_SSM_HEADS // N_SSM_GROUPS
D_STATE = 128
D_CONV = 5
CONV_PAD = D_CONV // 2
CONV_DIM = D_INNER + 2 * N_SSM_GROUPS * D_STATE
IN_PROJ_DIM = D_INNER + CONV_DIM + 2 * N_SSM_HEADS
CHUNK = 128
D_FF = 4 * D_MODEL
NORM_EPS = 1e-6
GATED_NORM_EPS = 1e-5

kernel_name = 'hybrid_swa_sink_bimamba2_sqrelu_encoder'


def rms_norm(x, w, eps=NORM_EPS):
    x32 = x.astype(jnp.float32)
    y = x32 * lax.rsqrt(jnp.mean(x32 * x32, axis=-1, keepdims=True) + eps)
    return (y * w.astype(jnp.float32)).astype(x.dtype)


def apply_rope(t, seq_len):
    inv_freq = 1.0 / (ROPE_THETA ** (jnp.arange(0, HEAD_DIM, 2, dtype=jnp.float32) / HEAD_DIM))
    ang = jnp.arange(seq_len, dtype=jnp.float32)[:, None] * inv_freq[None, :]
    cos = jnp.cos(ang)[None, :, None, :]
    sin = jnp.sin(ang)[None, :, None, :]
    t32 = t.astype(jnp.float32)
    t1, t2 = jnp.split(t32, 2, axis=-1)
    out = jnp.concatenate([t1 * cos - t2 * sin, t2 * cos + t1 * sin], axis=-1)
    return out.astype(t.dtype)


def banded_keys(t, bsz, nb):
    tp = jnp.pad(t, ((0, 0), (ATTN_BLOCK, ATTN_BLOCK), (0, 0), (0, 0)))
    tp = tp.reshape(bsz, nb + 2, ATTN_BLOCK, N_KV_HEADS, HEAD_DIM)
    return jnp.concatenate([tp[:, :-2], tp[:, 1:-1], tp[:, 2:]], axis=2)


def attn_mixer(u, w_qkv, w_o, sink):
    bsz, l, _ = u.shape
    nb = l // ATTN_BLOCK
    qkv = u @ w_qkv.astype(u.dtype)
    q = qkv[..., :N_Q_HEADS * HEAD_DIM].reshape(bsz, l, N_Q_HEADS, HEAD_DIM)
    k = qkv[..., N_Q_HEADS * HEAD_DIM:(N_Q_HEADS + N_KV_HEADS) * HEAD_DIM].reshape(bsz, l, N_KV_HEADS, HEAD_DIM)
    v = qkv[..., (N_Q_HEADS + N_KV_HEADS) * HEAD_DIM:].reshape(bsz, l, N_KV_HEADS, HEAD_DIM)
    q = apply_rope(q, l)
    k = apply_rope(k, l)
    qb = q.reshape(bsz, nb, ATTN_BLOCK, N_KV_HEADS, Q_PER_KV, HEAD_DIM)
    kb = banded_keys(k, bsz, nb)
    vb = banded_keys(v, bsz, nb)
    s = jnp.einsum('bnqkrd,bnskd->bnkrqs', qb, kb, preferred_element_type=jnp.float32) * (HEAD_DIM ** -0.5)
    qpos = jnp.arange(nb)[:, None] * ATTN_BLOCK + jnp.arange(ATTN_BLOCK)[None, :]
    kpos = jnp.arange(nb)[:, None] * ATTN_BLOCK - ATTN_BLOCK + jnp.arange(3 * ATTN_BLOCK)[None, :]
    valid = (jnp.abs(kpos[:, None, :] - qpos[:, :, None]) <= WINDOW) & (kpos >= 0)[:, None, :] & (kpos < l)[:, None, :]
    s = jnp.where(valid[None, :, None, None], s, -jnp.inf)
    sink_h = sink.astype(jnp.float32).reshape(N_KV_HEADS, Q_PER_KV)[None, None, :, :, None, None]
    m = jnp.maximum(jnp.max(s, axis=-1, keepdims=True), sink_h)
    p = jnp.exp(s - m)
    denom = jnp.sum(p, axis=-1, keepdims=True) + jnp.exp(sink_h - m)
    o = jnp.einsum('bnkrqs,bnskd->bnqkrd', (p / denom).astype(u.dtype), vb)
    return o.reshape(bsz, l, N_Q_HEADS * HEAD_DIM) @ w_o.astype(u.dtype)


def ssd_chunked(x, dt, a, b_mat, c_mat):
    bsz, l = x.shape[0], x.shape[1]
    nc = l // CHUNK
    g, r, p, n = N_SSM_GROUPS, HEADS_PER_GROUP, SSM_HEAD_DIM, D_STATE
    xs = (x * dt[..., None]).reshape(bsz, nc, CHUNK, g, r, p)
    da = (dt * a).reshape(bsz, nc, CHUNK, g, r)
    bc = b_mat.reshape(bsz, nc, CHUNK, g, n)
    cc = c_mat.reshape(bsz, nc, CHUNK, g, n)
    cum = jnp.cumsum(da, axis=2)
    tril = jnp.tril(jnp.ones((CHUNK, CHUNK), dtype=bool))
    diff = cum[:, :, :, None] - cum[:, :, None, :]
    decay = jnp.exp(jnp.where(tril[:, :, None, None], diff, -jnp.inf))
    cb = jnp.einsum('bclgn,bcsgn->bclsg', cc, bc)
    y_diag = jnp.einsum('bclsgr,bcsgrp->bclgrp', cb[..., None] * decay, xs)
    decay_states = jnp.exp(cum[:, :, -1:] - cum)
    states = jnp.einsum('bclgn,bclgrp->bcgrpn', bc, xs * decay_states[..., None])
    chunk_decay = jnp.exp(cum[:, :, -1])

    def step(h, inp):
        st, dec = inp
        return h * dec[..., None, None] + st, h

    h0 = jnp.zeros((bsz, g, r, p, n), jnp.float32)
    _, prev = lax.scan(step, h0, (jnp.moveaxis(states, 1, 0), jnp.moveaxis(chunk_decay, 1, 0)))
    prev = jnp.moveaxis(prev, 0, 1)
    y_off = jnp.einsum('bclgn,bcgrpn->bclgrp', cc, prev) * jnp.exp(cum)[..., None]
    return (y_diag + y_off).reshape(bsz, l, N_SSM_HEADS, p)


def depthwise_conv(x, w, b):
    y = lax.conv_general_dilated(x, w.astype(x.dtype)[:, None, :], window_strides=(1,),
                                 padding=[(CONV_PAD, CONV_PAD)],
                                 dimension_numbers=('NWC', 'WIO', 'NWC'),
                                 feature_group_count=x.shape[-1])
    return y + b.astype(x.dtype)


def ssm_mixer(u, w_in, conv_w, conv_b, dt_bias, a_log, d_skip, norm_w, w_out):
    bsz, l, _ = u.shape
    proj = u @ w_in.astype(u.dtype)
    z = proj[..., :D_INNER]
    xbc = proj[..., D_INNER:D_INNER + CONV_DIM]
    dt_raw = proj[..., D_INNER + CONV_DIM:]
    xbc = jax.nn.silu(depthwise_conv(xbc, conv_w, conv_b)).astype(jnp.float32)
    xh = xbc[..., :D_INNER].reshape(bsz, l, N_SSM_HEADS, SSM_HEAD_DIM)
    bm = xbc[..., D_INNER:D_INNER + N_SSM_GROUPS * D_STATE].reshape(bsz, l, N_SSM_GROUPS, D_STATE)
    cm = xbc[..., D_INNER + N_SSM_GROUPS * D_STATE:].reshape(bsz, l, N_SSM_GROUPS, D_STATE)
    dt = jax.nn.softplus(dt_raw.astype(jnp.float32).reshape(bsz, l, 2, N_SSM_HEADS) + dt_bias.astype(jnp.float32))
    a = -jnp.exp(a_log.astype(jnp.float32))
    y_fwd = ssd_chunked(xh, dt[:, :, 0], a[0], bm, cm)
    flip = lambda t: jnp.flip(t, axis=1)
    y_bwd = flip(ssd_chunked(flip(xh), flip(dt[:, :, 1]), a[1], flip(bm), flip(cm)))
    y = y_fwd + y_bwd + d_skip.astype(jnp.float32)[:, None] * xh
    y = y.reshape(bsz, l, D_INNER) * jax.nn.silu(z.astype(jnp.float32))
    y = rms_norm(y, norm_w, GATED_NORM_EPS)
    return y.astype(u.dtype) @ w_out.astype(u.dtype)


def trunk(x, attn_w_qkv, attn_w_o, attn_sink, ssm_w_in, ssm_conv_w, ssm_conv_b, ssm_dt_bias,
          ssm_a_log, ssm_d, ssm_norm_w, ssm_w_out, norm_mix_pre, norm_mix_post,
          norm_ffn_pre, norm_ffn_post, mlp_w_up, mlp_w_down):
    for i in range(DEPTH):
        j = i // N_MIXERS
        h = rms_norm(x, norm_mix_pre[i])
        if i % N_MIXERS == 0:
            h = attn_mixer(h, attn_w_qkv[j], attn_w_o[j], attn_sink[j])
        else:
            h = ssm_mixer(h, ssm_w_in[j], ssm_conv_w[j], ssm_conv_b[j], ssm_dt_bias[j],
                          ssm_a_log[j], ssm_d[j], ssm_norm_w[j], ssm_w_out[j])
        x = x + rms_norm(h, norm_mix_post[i])
        h = rms_norm(x, norm_ffn_pre[i])
        h = jnp.square(jax.nn.relu(h @ mlp_w_up[i].astype(x.dtype))) @ mlp_w_down[i].astype(x.dtype)
        x = x + rms_norm(h, norm_ffn_post[i])
    return x


def setup_inputs(seed: int = 0) -> dict:
    key = jax.random.key(seed)
    ks = jax.random.split(key, 20)
    f32 = jnp.float32

    def nrm(k, shape, scale):
        return jax.random.normal(k, shape, f32) * scale

    dt0 = jnp.exp(jax.random.uniform(ks[9], (N_SSM_LAYERS, 2, N_SSM_HEADS), f32,
                                     minval=math.log(1e-3), maxval=math.log(1e-1)))
    return {
        'x_prompt': nrm(ks[0], (BATCH, SEQ, D_MODEL), 1.0),
        'x_sample': nrm(ks[1], (DEC_BATCH, DEC_SEQ, D_MODEL), 1.0),
        'attn_w_qkv': nrm(ks[2], (N_ATTN_LAYERS, D_MODEL, QKV_DIM), D_MODEL ** -0.5),
        'attn_w_o': nrm(ks[3], (N_ATTN_LAYERS, N_Q_HEADS * HEAD_DIM, D_MODEL), (N_Q_HEADS * HEAD_DIM) ** -0.5),
        'attn_sink': nrm(ks[4], (N_ATTN_LAYERS, N_Q_HEADS), 0.5),
        'ssm_w_in': nrm(ks[5], (N_SSM_LAYERS, D_MODEL, IN_PROJ_DIM), D_MODEL ** -0.5),
        'ssm_conv_w': nrm(ks[6], (N_SSM_LAYERS, D_CONV, CONV_DIM), D_CONV ** -0.5),
        'ssm_conv_b': nrm(ks[7], (N_SSM_LAYERS, CONV_DIM), 0.01),
        'ssm_dt_bias': dt0 + jnp.log(-jnp.expm1(-dt0)),
        'ssm_a_log': jnp.log(jax.random.uniform(ks[10], (N_SSM_LAYERS, 2, N_SSM_HEADS), f32, minval=1.0, maxval=16.0)),
        'ssm_d': 1.0 + nrm(ks[11], (N_SSM_LAYERS, N_SSM_HEADS), 0.1),
        'ssm_norm_w': 1.0 + nrm(ks[12], (N_SSM_LAYERS, D_INNER), 0.1),
        'ssm_w_out': nrm(ks[13], (N_SSM_LAYERS, D_INNER, D_MODEL), D_INNER ** -0.5),
        'norm_mix_pre': 1.0 + nrm(ks[14], (DEPTH, D_MODEL), 0.1),
        'norm_mix_post': 1.0 + nrm(ks[15], (DEPTH, D_MODEL), 0.1),
        'norm_ffn_pre': 1.0 + nrm(ks[16], (DEPTH, D_MODEL), 0.1),
        'norm_ffn_post': 1.0 + nrm(ks[17], (DEPTH, D_MODEL), 0.1),
        'mlp_w_up': nrm(ks[18], (DEPTH, D_MODEL, D_FF), D_MODEL ** -0.5),
        'mlp_w_down': nrm(ks[19], (DEPTH, D_FF, D_MODEL), D_FF ** -0.5),
    }


def reference(x_prompt, x_sample, attn_w_qkv, attn_w_o, attn_sink, ssm_w_in, ssm_conv_w,
              ssm_conv_b, ssm_dt_bias, ssm_a_log, ssm_d, ssm_norm_w, ssm_w_out,
              norm_mix_pre, norm_mix_post, norm_ffn_pre, norm_ffn_post, mlp_w_up, mlp_w_down):
    weights = (attn_w_qkv, attn_w_o, attn_sink, ssm_w_in, ssm_conv_w, ssm_conv_b, ssm_dt_bias,
               ssm_a_log, ssm_d, ssm_norm_w, ssm_w_out, norm_mix_pre, norm_mix_post,
               norm_ffn_pre, norm_ffn_post, mlp_w_up, mlp_w_down)
    y_prompt = trunk(x_prompt, *weights)
    y_sample = trunk(x_sample, *weights)
    return (y_prompt, y_sample)
```

```python
import math
from contextlib import ExitStack
import numpy as np
import ml_dtypes
import concourse.bass as bass
import concourse.mybir as mybir
from concourse.bass_utils import run_bass_kernel_spmd

F32 = mybir.dt.float32
BF16 = mybir.dt.bfloat16
ALU = mybir.AluOpType
AF = mybir.ActivationFunctionType

D = 1024
T = 512
NQH, NKV, HD = 16, 4, 64
DFF = 4096
DIN = 2048
NH = 32
NG = 4
DS = 128
CONVD = 3072
INP = 5184
EPS = 1e-6
GEPS = 1e-5
NRING = 4


class Buf:
    __slots__ = ("name", "w", "r", "excl")

    def __init__(self, name=""):
        self.name = name
        self.w = None
        self.r = {}
        self.excl = False


class Ctx:
    def __init__(self, nc):
        self.nc = nc
        self.eng = {"pe": nc.tensor, "act": nc.scalar, "dve": nc.vector,
                    "pool": nc.gpsimd, "sp": nc.sync}
        self.sems = {}
        self.cnt = {}
        for k in self.eng:
            self.sems[k] = nc.alloc_semaphore("s_" + k)
            self.cnt[k] = 0
        self.waited = {}
        self.n_dma_sem = 0
        self.ninst = 0
        self.nwait = 0

    def new_dma_sem(self):
        k = "dma%d" % self.n_dma_sem
        self.n_dma_sem += 1
        self.sems[k] = self.nc.alloc_semaphore("s_" + k)
        self.cnt[k] = 0
        return k

    def _wait(self, e, dep):
        if dep is None:
            return
        key, val = dep
        if self.waited.get((e, key), 0) >= val:
            return
        if key == e and e == "pe":
            return
        self.eng[e].wait_ge(self.sems[key], val)
        self.waited[(e, key)] = val
        self.nwait += 1

    def deps(self, e, reads, writes):
        for b in reads:
            self._wait(e, b.w)
            if b.excl:
                for k, v in list(b.r.items()):
                    if k != e:
                        self._wait(e, (k, v))
        for b in writes:
            self._wait(e, b.w)
            for k, v in list(b.r.items()):
                self._wait(e, (k, v))

    def op(self, e, fn, reads=(), writes=(), signal=True):
        self.deps(e, reads, writes)
        ins = fn(self.eng[e])
        self.ninst += 1
        if signal:
            self.cnt[e] += 1
            ins.then_inc(self.sems[e], 1)
            v = self.cnt[e]
        else:
            v = self.cnt[e] + 1
        for b in writes:
            b.w = (e, v)
            b.r = {}
        for b in reads:
            if b.r.get(e, 0) < v:
                b.r[e] = v
        return ins

    def dma(self, q, semk, out, in_, reads=(), writes=()):
        self.deps(q, reads, writes)
        if self.cnt[semk] > 0:
            self._wait(q, (semk, self.cnt[semk]))
        ins = self.eng[q].dma_start(out=out, in_=in_)
        self.cnt[semk] += 16
        ins.then_inc(self.sems[semk], 16)
        v = self.cnt[semk]
        self.ninst += 1
        for b in writes:
            b.w = (semk, v)
            b.r = {}
        for b in reads:
            if b.r.get(semk, 0) < v:
                b.r[semk] = v
        return ins

    def barrier(self):
        for e in self.eng:
            for k, v in self.cnt.items():
                if v > 0 and k != e:
                    self._wait(e, (k, v))

    def final_wait(self, e="sp"):
        for k, v in self.cnt.items():
            if v > 0 and k != e:
                self._wait(e, (k, v))


class TB:
    def __init__(self, t, n=1):
        self.t = t
        self.b = [Buf() for _ in range(n)]


def _consts(lmax):
    inv = 1.0 / (10000.0 ** (np.arange(0, HD, 2, dtype=np.float32) / HD))
    ang = np.arange(lmax, dtype=np.float32)[:, None] * inv[None, :].astype(np.float32)
    ang = ang.astype(np.float32)
    cos = np.cos(ang).astype(np.float32).T
    sin = np.sin(ang).astype(np.float32).T
    cosT = np.concatenate([cos, cos, cos, cos], 0)
    sinT = np.concatenate([-sin, sin, -sin, sin], 0)
    j = np.arange(128)[:, None]
    i = np.arange(128)[None, :]
    mprev = (j >= i).astype(np.float32)
    mnext = (j <= i).astype(np.float32)
    return {
        "c_cos": np.ascontiguousarray(cosT), "c_sin": np.ascontiguousarray(sinT),
        "c_ident": np.eye(128, dtype=np.float32),
        "c_mprev": mprev.astype(ml_dtypes.bfloat16), "c_mnext": mnext.astype(ml_dtypes.bfloat16),
        "c_tle": mnext.astype(np.float32), "c_tge": mprev.astype(np.float32),
    }


class _Stop(Exception):
    pass


def build(seq_lens, phases=3, debug=False, stop=0):
    ntok = sum(seq_lens)
    lmax = max(seq_lens)
    nc = bass.Bass("TRN2", target_bir_lowering=False)
    c = Ctx(nc)

    def din(name, shape, dt=F32):
        return nc.dram_tensor(name, list(shape), dt, kind="ExternalInput").ap()

    def dscr(name, shape, dt):
        return nc.dram_tensor(name, list(shape), dt, kind="Internal").ap()

    x_d = din("x", [ntok, D])
    out_d = nc.dram_tensor("y", [ntok, D], F32, kind="ExternalOutput").ap()
    wqkv_d = din("attn_w_qkv", [D, 1536])
    wo_d = din("attn_w_o", [D, D])
    sink_d = din("attn_sink", [1, NQH])
    win_d = din("ssm_w_in", [D, INP])
    convw_d = din("ssm_conv_w", [5, CONVD])
    convb_d = din("ssm_conv_b", [1, CONVD])
    dtb_d = din("ssm_dt_bias", [1, 64])
    alog_d = din("ssm_a_log", [1, 64])
    dsk_d = din("ssm_d", [1, NH])
    gnw_d = din("ssm_norm_w", [1, DIN])
    wout_d = din("ssm_w_out", [DIN, D])
    nmpre_d = din("norm_mix_pre", [2, D])
    nmpost_d = din("norm_mix_post", [2, D])
    nfpre_d = din("norm_ffn_pre", [2, D])
    nfpost_d = din("norm_ffn_post", [2, D])
    wup_d = din("mlp_w_up", [2, D, DFF])
    wdn_d = din("mlp_w_down", [2, DFF, D])
    cos_d = din("c_cos", [128, lmax])
    sin_d = din("c_sin", [128, lmax])
    ident_d = din("c_ident", [128, 128])
    mprev_d = din("c_mprev", [128, 128], BF16)
    mnext_d = din("c_mnext", [128, 128], BF16)
    tle_d = din("c_tle", [128, 128])
    tge_d = din("c_tge", [128, 128])

    s_qk = dscr("s_qk", [D, 2560], BF16)
    s_v = dscr("s_v", [D, 256], BF16)
    s_wo = dscr("s_wo", [D, D], BF16)
    s_up = [dscr("s_up%d" % i, [D, DFF], BF16) for i in range(2)]
    s_dn = [dscr("s_dn%d" % i, [DFF, D], BF16) for i in range(2)]
    s_x1 = dscr("s_x1", [ntok // T, 128, 8 * T], F32)
    s_hn = dscr("s_hn", [ntok // T, 128, 8 * T], BF16)
    s_bT = dscr("s_bT", [ntok // T, 3, 64, T], BF16)
    s_cT = dscr("s_cT", [ntok // T, 3, 64, T], BF16)
    s_wfm = dscr("s_wfm", [D, CONVD], BF16)
    s_wtm = dscr("s_wtm", [D, 2560], BF16)
    s_wout = dscr("s_wout", [DIN, D], BF16)
    NCH = ntok // 128
    s_yp = dscr("s_yp", [ntok, DIN], F32)
    s_z = dscr("s_z", [ntok, DIN], F32)
    s_ct = dscr("s_ct", [ntok // T, 128, 4 * T], BF16)
    s_ecb = dscr("s_ecb", [ntok, NH], F32)
    s_sb = dscr("s_sb", [NCH, 128, DIN], F32)
    s_cdb = dscr("s_cdb", [NCH, 128, NH], F32)

    es = ExitStack()

    def sb(name, shape, dt=F32, n=1):
        return TB(es.enter_context(nc.sbuf_tensor(name, list(shape), dt)), n)

    ident = sb("ident", [128, 128])
    identb = sb("identb", [128, 128], BF16)
    ones = sb("ones", [128, 128], BF16)
    mprev = sb("mprev", [128, 128], BF16)
    mnext = sb("mnext", [128, 128], BF16)
    esink = sb("esink", [128, NQH])
    gvec = sb("gvec", [128, 8, 8])
    epsc = sb("epsc", [128, 4])
    tle = sb("tle", [128, 128])
    tge = sb("tge", [128, 128])
    onesf = sb("onesf", [128, 128])
    convw = sb("convw", [128, 24, 5])
    convb = sb("convb", [128, 24])
    dtb_bc = sb("dtb_bc", [128, 64])
    a_bc = sb("a_bc", [128, 64])
    d32 = sb("d32", [128, NH])
    gnw = sb("gnw", [128, 16])
    psum = [TB(es.enter_context(nc.psum_tensor("ps%d" % i, [128, 512], F32))) for i in range(8)]
    for p_ in psum:
        p_.b[0].excl = True
    pools = {"all": list(range(8)), "A": [0, 1, 2, 3, 4, 5], "B": [6, 7]}
    cur_pool = ["all"]
    pctrs = {"all": 0, "A": 0, "B": 0}

    def ps():
        k = cur_pool[0]
        lst = pools[k]
        p = psum[lst[pctrs[k] % len(lst)]]
        pctrs[k] += 1
        return p

    dumped = {}

    def dump(name, ap, shape, dt, bufs):
        if not debug or name in dumped:
            return
        d = nc.dram_tensor("dbg_" + name, list(shape), dt, kind="ExternalOutput").ap()
        dumped[name] = d
        c.dma("sp", sem_dbg, d, ap, reads=bufs)

    sem_dbg = c.new_dma_sem()
    sem_c = c.new_dma_sem()
    sem_ld = [c.new_dma_sem() for _ in range(8)]
    ld_ctr = [0]

    def ldsem():
        ld_ctr[0] += 1
        return sem_ld[ld_ctr[0] % 8]

    sem_st = [c.new_dma_sem() for _ in range(8)]
    st_ctr = [0]

    def stsem():
        st_ctr[0] += 1
        return sem_st[st_ctr[0] % 8]

    blk = es.enter_context(nc.Block())

    c.dma("sp", sem_c, ident.t[:], ident_d, writes=ident.b)
    c.dma("sp", sem_c, mprev.t[:], mprev_d, writes=mprev.b)
    c.dma("sp", sem_c, mnext.t[:], mnext_d, writes=mnext.b)
    c.dma("sp", sem_c, esink.t[:], sink_d.partition_broadcast(128), writes=esink.b)
    c.op("act", lambda e: e.activation(out=esink.t[:], in_=esink.t[:], func=AF.Exp), reads=esink.b, writes=esink.b)
    c.op("dve", lambda e: e.tensor_copy(out=identb.t[:], in_=ident.t[:]), reads=ident.b, writes=identb.b)
    c.op("pool", lambda e: e.memset(ones.t[:], 1.0), writes=ones.b)
    c.op("pool", lambda e: e.memset(epsc.t[:, 0:1], EPS), writes=epsc.b)
    c.op("pool", lambda e: e.memset(epsc.t[:, 1:2], GEPS), writes=epsc.b)
    c.op("pool", lambda e: e.memset(epsc.t[:, 2:3], 1.0), writes=epsc.b)
    c.op("pool", lambda e: e.memset(epsc.t[:, 3:4], 0.0), writes=epsc.b)
    c.op("pool", lambda e: e.memset(onesf.t[:], 1.0), writes=onesf.b)
    c.dma("sp", sem_c, tle.t[:], tle_d, writes=tle.b)
    c.dma("sp", sem_c, tge.t[:], tge_d, writes=tge.b)
    with nc.allow_non_contiguous_dma(reason="tiny param vectors"):
        for k5 in range(5):
            c.dma("sp", sem_c, convw.t[:, :, k5], convw_d[k5].rearrange("(m p) -> p m", p=128), writes=convw.b)
        c.dma("sp", sem_c, convb.t[:], convb_d[0].rearrange("(m p) -> p m", p=128), writes=convb.b)
        c.dma("sp", sem_c, gnw.t[:], gnw_d[0].rearrange("(k p) -> p k", p=128), writes=gnw.b)
    c.dma("sp", sem_c, dtb_bc.t[:], dtb_d.partition_broadcast(128), writes=dtb_bc.b)
    c.dma("sp", sem_c, a_bc.t[:], alog_d.partition_broadcast(128), writes=a_bc.b)
    c.dma("sp", sem_c, d32.t[:], dsk_d.partition_broadcast(128), writes=d32.b)
    c.op("act", lambda e: e.activation(out=a_bc.t[:], in_=a_bc.t[:], func=AF.Exp), reads=a_bc.b, writes=a_bc.b)
    c.op("dve", lambda e: e.tensor_scalar(out=a_bc.t[:], in0=a_bc.t[:], scalar1=-1.0, scalar2=None, op0=ALU.mult), reads=a_bc.b, writes=a_bc.b)
    for li in range(2):
        for wi, src in enumerate((nmpre_d, nmpost_d, nfpre_d, nfpost_d)):
            with nc.allow_non_contiguous_dma(reason="tiny gain vectors"):
                c.dma("sp", sem_c, gvec.t[:, li * 4 + wi, :], src[li].rearrange("(k p) -> p k", p=128), writes=gvec.b)

    with ExitStack() as es0:
        NSTG = 6
        stg = [TB(es0.enter_context(nc.sbuf_tensor("stg%d" % i, [128, 2048], F32))) for i in range(NSTG)]
        stb = [TB(es0.enter_context(nc.sbuf_tensor("stb%d" % i, [128, 2048], BF16))) for i in range(NSTG)]
        cv = [0]

        tasks = []

        def conv_piece(src_ap, ncols, scale_ap, dsts):
            i = cv[0] % NSTG
            cv[0] += 1
            use_act = (cv[0] % 2 == 1)

            def load():
                c.dma("sp", ldsem(), stg[i].t[:, 0:ncols], src_ap, writes=stg[i].b)

            def cast():
                if scale_ap is None:
                    if use_act:
                        c.op("act", lambda e: e.activation(out=stb[i].t[:, 0:ncols], in_=stg[i].t[:, 0:ncols], func=AF.Copy), reads=stg[i].b, writes=stb[i].b)
                    else:
                        c.op("dve", lambda e: e.tensor_copy(out=stb[i].t[:, 0:ncols], in_=stg[i].t[:, 0:ncols]), reads=stg[i].b, writes=stb[i].b)
                else:
                    if use_act:
                        c.op("act", lambda e: e.activation(out=stb[i].t[:, 0:ncols], in_=stg[i].t[:, 0:ncols], func=AF.Copy, scale=scale_ap), reads=stg[i].b + gvec.b + gnw.b, writes=stb[i].b)
                    else:
                        c.op("dve", lambda e: e.tensor_scalar(out=stb[i].t[:, 0:ncols], in0=stg[i].t[:, 0:ncols], scalar1=scale_ap, scalar2=None, op0=ALU.mult), reads=stg[i].b + gvec.b + gnw.b, writes=stb[i].b)

            def store():
                for (dap, c0, n) in dsts:
                    c.dma("sp", stsem(), dap, stb[i].t[:, c0:c0 + n], reads=stb[i].b)
            tasks.append((load, cast, store))

        stq = [TB(es0.enter_context(nc.sbuf_tensor("stq%d" % i, [128, 2816], BF16))) for i in range(2)]

        def conv_perm(src_ap, ncols, scale_ap, pieces, outs, idx):
            i = cv[0] % NSTG
            cv[0] += 1
            q_ = stq[idx % 2]

            def load():
                c.dma("sp", ldsem(), stg[i].t[:, 0:ncols], src_ap, writes=stg[i].b)

            def cast():
                for pi_, (dc, sc, n) in enumerate(pieces):
                    if pi_ % 2 == 0:
                        c.op("act", lambda e: e.activation(out=q_.t[:, dc:dc + n], in_=stg[i].t[:, sc:sc + n], func=AF.Copy, scale=scale_ap), reads=stg[i].b + gvec.b, writes=q_.b)
                    else:
                        c.op("dve", lambda e: e.tensor_scalar(out=q_.t[:, dc:dc + n], in0=stg[i].t[:, sc:sc + n], scalar1=scale_ap, scalar2=None, op0=ALU.mult), reads=stg[i].b + gvec.b, writes=q_.b)

            def store():
                for (dap, c0, n) in outs:
                    c.dma("sp", stsem(), dap, q_.t[:, c0:c0 + n], reads=q_.b)
            tasks.append((load, cast, store))

        class _Col:
            def __init__(self):
                pass

        for k in range(8):
            rows = slice(k * 128, (k + 1) * 128)
            g = gvec.t[:, 0, k:k + 1]
            dsts = []
            for j in range(2):
                for r in range(4):
                    cch = 4 * j + r
                    blkq, pos = cch // 2, cch % 2
                    base = blkq * 512 + pos * 256
                    for e2 in range(2):
                        h = 8 * j + 4 * e2 + r
                        dsts.append((base + e2 * 64, h * 64, 64))
                        dsts.append((base + 128 + e2 * 64, h * 64 + 32, 32))
                        dsts.append((base + 128 + e2 * 64 + 32, h * 64, 32))
            for kc in range(2):
                base = 2048 + kc * 256
                for e2 in range(2):
                    gk = 2 * kc + e2
                    dsts.append((base + e2 * 64, 1024 + gk * 64, 64))
                    dsts.append((base + 128 + e2 * 64, 1024 + gk * 64 + 32, 32))
                    dsts.append((base + 128 + e2 * 64 + 32, 1024 + gk * 64, 32))
            dsts.append((2560, 1280, 256))
            conv_perm(wqkv_d[rows, :], 1536, g, dsts, [(s_qk[rows, :], 0, 2560), (s_v[rows, :], 2560, 256)], k)
            conv_piece(wo_d[rows, :], 1024, None, [(s_wo[rows, :], 0, 1024)])
            for li in range(2 if phases >= 3 else 1):
                g2 = gvec.t[:, li * 4 + 2, k:k + 1]
                for hh in range(2):
                    conv_piece(wup_d[li, rows, hh * 2048:(hh + 1) * 2048], 2048, g2, [(s_up[li][rows, hh * 2048:(hh + 1) * 2048], 0, 2048)])
        for li in range(2 if phases >= 3 else 1):
            for k in range(32):
                rows = slice(k * 128, (k + 1) * 128)
                conv_piece(wdn_d[li, rows, :], 1024, None, [(s_dn[li][rows, :], 0, 1024)])
        if phases >= 2:
            for k in range(8):
                rows = slice(k * 128, (k + 1) * 128)
                g = gvec.t[:, 4, k:k + 1]
                conv_piece(win_d[rows, 0:2048], 2048, g, [(s_wtm[rows, 0:2048], 0, 2048)])
                conv_piece(win_d[rows, 2048:4096], 2048, g, [(s_wfm[rows, 0:2048], 0, 2048)])
                conv_piece(win_d[rows, 4096:5184], 1088, g, [(s_wfm[rows, 2048:3072], 0, 1024), (s_wtm[rows, 2048:2112], 1024, 64)])
            for k in range(16):
                rows = slice(k * 128, (k + 1) * 128)
                conv_piece(wout_d[rows, :], 1024, gnw.t[:, k:k + 1], [(s_wout[rows, :], 0, 1024)])

        DEPTH = NSTG - 1
        for i_ in range(len(tasks) + DEPTH):
            if i_ < len(tasks):
                tasks[i_][0]()
            if i_ - DEPTH >= 0:
                tasks[i_ - DEPTH][1]()
                tasks[i_ - DEPTH][2]()
    c.barrier()
    tiles = []
    off = 0
    for si, L in enumerate(seq_lens):
        nt = L // T
        for ti in range(nt):
            tiles.append(dict(seq=si, pos0=ti * T, row0=off + ti * T, first=(ti == 0), last=(ti == nt - 1), idx=len(tiles)))
        off += L
    NT = len(tiles)

    def wblock(scr, kb, nb, ncols=512):
        return scr[kb * 1024:(kb + 1) * 1024, nb * ncols:(nb + 1) * ncols].rearrange("(kk p) n -> p kk n", p=128), ncols

    def sched_p1():
        for t in range(NT + 1):
            if t < NT:
                for b in range(5):
                    yield wblock(s_qk, 0, b)
                yield wblock(s_v, 0, 0, 256)
            if t >= 1:
                for b in range(2):
                    yield wblock(s_wo, 0, b)
                for b in range(8):
                    yield wblock(s_up[0], 0, b)
                for nb in range(2):
                    for kb in range(4):
                        yield wblock(s_dn[0], kb, nb)

    class WStream:
        def __init__(self, gen, ring):
            self.gen = gen
            self.ring = ring
            self.sem = [c.new_dma_sem() for _ in ring]
            self.issued = 0
            self.used = 0
            self.done = False

        def _issue(self):
            try:
                ap, ncols = next(self.gen)
            except StopIteration:
                self.done = True
                return
            s = self.issued % len(self.ring)
            r = self.ring[s]
            c.dma("sp", self.sem[s], r.t[:, :, 0:ncols], ap, writes=r.b)
            self.issued += 1

        def next(self):
            while not self.done and self.issued < self.used + len(self.ring):
                self._issue()
            r = self.ring[self.used % len(self.ring)]
            self.used += 1
            return r

    def rms_stats(src_fn, nchunk, rstd, dim, eps, sqbuf, sq_eng="act", c0=0, n=T):
        p = ps()
        for k in range(nchunk):
            ap, bufs = src_fn(k)
            c.op("act", lambda e: e.activation(out=sqbuf.t[:, k, c0:c0 + n], in_=ap, func=AF.Square), reads=bufs, writes=[sqbuf.b[k]])
        for k in range(nchunk):
            c.op("pe", lambda e: e.matmul(p.t[:, 0:n], lhsT=ones.t[:], rhs=sqbuf.t[:, k, c0:c0 + n], start=(k == 0), stop=(k == nchunk - 1)),
                 reads=[sqbuf.b[k]] + ones.b, writes=p.b, signal=(k == nchunk - 1))
        c.op("act", lambda e: e.activation(out=rstd.t[:, c0:c0 + n], in_=p.t[:, 0:n], func=AF.Ln, scale=1.0 / dim, bias=epsc.t[:, 0:1] if eps == EPS else epsc.t[:, 1:2]), reads=p.b + epsc.b, writes=rstd.b)
        c.op("act", lambda e: e.activation(out=rstd.t[:, c0:c0 + n], in_=rstd.t[:, c0:c0 + n], func=AF.Exp, scale=-0.5), reads=rstd.b, writes=rstd.b)

    def linear_g(ws, act_fn, kchunks, nblocks, kblocks, epilogue):
        for nb in range(nblocks):
            banks = [ps() for _ in range(4)]
            for kb in range(kblocks):
                w = ws.next()
                for m in range(4):
                    for kk in range(8):
                        kidx = kb * 8 + kk
                        ap, bufs = act_fn(kidx)
                        first = (kb == 0 and kk == 0)
                        last = (kb == kblocks - 1 and kk == 7)
                        c.op("pe", lambda e: e.matmul(banks[m].t[:], lhsT=w.t[:, kk, m * 128:(m + 1) * 128], rhs=ap, start=first, stop=last),
                             reads=w.b + bufs, writes=banks[m].b, signal=(kk == 7))
                if kb < kblocks - 1:
                    yield
            for m in range(4):
                epilogue(nb, m, banks[m])
            yield

    def linear(ws, act_fn, kchunks, nblocks, kblocks, epilogue):
        for _ in linear_g(ws, act_fn, kchunks, nblocks, kblocks, epilogue):
            pass

    def post_norm_residual(xT, asb, sqbuf, rstd, gidx, tmp):
        rms_stats(lambda k: (asb.t[:, k * T:(k + 1) * T], [asb.b[k]]), 8, rstd, D, EPS, sqbuf)
        dump("rstd2", rstd.t[:], [128, T], F32, rstd.b)
        for k in range(8):
            tb = tmp[k % 2]
            c.op("dve", lambda e: e.scalar_tensor_tensor(out=tb.t[:], in0=asb.t[:, k * T:(k + 1) * T], scalar=gvec.t[:, gidx, k:k + 1], in1=rstd.t[:], op0=ALU.mult, op1=ALU.mult),
                 reads=[asb.b[k]] + rstd.b + gvec.b, writes=tb.b)
            c.op("pool" if k % 2 == 0 else "dve", lambda e: e.tensor_tensor(out=xT.t[:, k, :], in0=xT.t[:, k, :], in1=tb.t[:], op=ALU.add), reads=tb.b + [xT.b[k]], writes=[xT.b[k]])

    def pre_norm(xT, hn, sqbuf, rstd):
        rms_stats(lambda k: (xT.t[:, k, :], [xT.b[k]]), 8, rstd, D, EPS, sqbuf)
        for k in range(8):
            eng = "dve" if k % 2 == 0 else "pool"
            c.op(eng, lambda e: e.tensor_tensor(out=hn.t[:, k, :], in0=xT.t[:, k, :], in1=rstd.t[:], op=ALU.mult), reads=[xT.b[k]] + rstd.b, writes=[hn.b[k]])

    def mlp_g(ws, li, xT, hn, h1T, asb, sqbuf, rstd, tmp):
        pre_norm(xT, hn, sqbuf, rstd)
        yield

        def ep_up(nb, m, p):
            j = nb * 4 + m
            tb = tmp[j % 2]
            c.op("act", lambda e: e.activation(out=tb.t[:], in_=p.t[:], func=AF.Square), reads=p.b, writes=tb.b)
            c.op("dve", lambda e: e.scalar_tensor_tensor(out=h1T.t[:, j, :], in0=p.t[:], scalar=0.0, in1=tb.t[:], op0=ALU.is_gt, op1=ALU.mult), reads=p.b + tb.b, writes=[h1T.b[j]])
        yield from linear_g(ws, lambda k: (hn.t[:, k, :], [hn.b[k]]), 8, 8, 1, ep_up)

        def ep_dn(nb, m, p):
            j = nb * 4 + m
            c.op("act", lambda e: e.activation(out=asb.t[:, j * T:(j + 1) * T], in_=p.t[:], func=AF.Copy), reads=p.b, writes=[asb.b[j]])
        yield from linear_g(ws, lambda k: (h1T.t[:, k, :], [h1T.b[k]]), 32, 2, 4, ep_dn)
        post_norm_residual(xT, asb, sqbuf, rstd, li * 4 + 3, tmp)
        yield

    def mlp(ws, li, xT, hn, h1T, asb, sqbuf, rstd, tmp):
        for _ in mlp_g(ws, li, xT, hn, h1T, asb, sqbuf, rstd, tmp):
            pass

    def store_tokmajor(xT, row0, stage):
        for b in range(4):
            for half in range(2):
                p = ps()
                for kk in range(4):
                    k = half * 4 + kk
                    c.op("pe", lambda e: e.transpose(out=p.t[:, kk * 128:(kk + 1) * 128], in_=xT.t[:, k, b * 128:(b + 1) * 128], identity=ident.t[:]),
                         reads=[xT.b[k]] + ident.b, writes=p.b, signal=(kk == 3))
                eng = "act" if half == 0 else "dve"
                if eng == "act":
                    c.op("act", lambda e: e.activation(out=stage.t[:, b, half * 512:(half + 1) * 512], in_=p.t[:], func=AF.Copy), reads=p.b, writes=[stage.b[b]])
                else:
                    c.op("dve", lambda e: e.tensor_copy(out=stage.t[:, b, half * 512:(half + 1) * 512], in_=p.t[:]), reads=p.b, writes=[stage.b[b]])
            c.dma("sp", stsem(), out_d[row0 + b * 128: row0 + (b + 1) * 128, :], stage.t[:, b, :], reads=[stage.b[b]])

    def store_tokmajor_alias(x, row0, xin_):
        for b in range(4):
            for half in range(2):
                p = ps()
                for kk in range(4):
                    k = half * 4 + kk
                    c.op("pe", lambda e: e.transpose(out=p.t[:, kk * 128:(kk + 1) * 128], in_=x.t[:, k, b * 128:(b + 1) * 128], identity=ident.t[:]),
                         reads=[x.b[k]] + ident.b, writes=p.b, signal=(kk == 3))
                bb = xin_.b[2 * b + half]
                if half == 0:
                    c.op("act", lambda e: e.activation(out=xin_.t[:, b * 1024 + half * 512: b * 1024 + (half + 1) * 512], in_=p.t[:], func=AF.Copy), reads=p.b, writes=[bb])
                else:
                    c.op("dve", lambda e: e.tensor_copy(out=xin_.t[:, b * 1024 + half * 512: b * 1024 + (half + 1) * 512], in_=p.t[:]), reads=p.b, writes=[bb])
            c.dma("sp", stsem(), out_d[row0 + b * 128: row0 + (b + 1) * 128, :], xin_.t[:, b * 1024:(b + 1) * 1024], reads=[xin_.b[2 * b], xin_.b[2 * b + 1]])


    with ExitStack() as es1:
        def sb1(name, shape, dt=F32, n=1):
            return TB(es1.enter_context(nc.sbuf_tensor(name, list(shape), dt)), n)
        ring = [sb1("ring%d" % i, [128, 8, 512], BF16) for i in range(NRING)]
        ws = WStream(sched_p1(), ring)
        xin = sb1("xin", [128, 4096], F32, 8)
        asb1 = sb1("asb1", [128, 4096], F32, 8)
        xT = [sb1("xT%d" % i, [128, 8, T], F32, 8) for i in range(2)]
        sqb = sb1("sqb", [128, 8, T], BF16, 8)
        rstd = sb1("rstd", [128, T])
        hn = sb1("hn", [128, 8, T], BF16, 8)
        qT = [sb1("qT%d" % i, [128, 8, T], BF16, 8) for i in range(2)]
        kT = [sb1("kT%d" % i, [128, 2, T], BF16, 2) for i in range(3)]
        Va = [sb1("Va%d" % i, [128, 4, 4, 66], BF16, 4) for i in range(3)]
        cs = sb1("cos", [128, T])
        sn = sb1("sin", [128, T])
        tmp = [sb1("tmp%d" % i, [128, T]) for i in range(2)]
        PT = [[sb1("PT%d_%d" % (i, j), [128, 512], BF16) for j in range(3)] for i in range(2)]
        Osb = sb1("Osb", [128, 1024], BF16)
        den = sb1("den", [128, 8])
        OT = sb1("OT", [128, 8, T], BF16, 8)
        h1T = sb1("h1T", [128, 32, T], BF16, 32)
        for v in Va:
            c.op("pool", lambda e: e.memset(v.t[:], 1.0), writes=v.b)

        def stageA(t):
            tl = tiles[t]
            x = xT[t % 2]
            q = qT[t % 2]
            kk_ = kT[t % 3]
            va = Va[t % 3]
            for b in range(4):
                c.dma("sp", ldsem(), xin.t[:, b * 1024:(b + 1) * 1024], x_d[tl["row0"] + b * 128: tl["row0"] + (b + 1) * 128, :], writes=[xin.b[2 * b], xin.b[2 * b + 1]])
            c.dma("sp", ldsem(), cs.t[:], cos_d[:, tl["pos0"]:tl["pos0"] + T], writes=cs.b)
            c.dma("sp", ldsem(), sn.t[:], sin_d[:, tl["pos0"]:tl["pos0"] + T], writes=sn.b)
            banks = [ps() for _ in range(8)]
            for k in range(8):
                for b in range(4):
                    c.op("pe", lambda e: e.transpose(out=banks[k].t[:, b * 128:(b + 1) * 128], in_=xin.t[:, b * 1024 + k * 128: b * 1024 + (k + 1) * 128], identity=ident.t[:]),
                         reads=[xin.b[2 * b], xin.b[2 * b + 1]] + ident.b, writes=banks[k].b, signal=(b == 3))
                if k % 2 == 0:
                    c.op("act", lambda e: e.activation(out=x.t[:, k, :], in_=banks[k].t[:], func=AF.Copy), reads=banks[k].b, writes=[x.b[k]])
                else:
                    c.op("dve", lambda e: e.tensor_copy(out=x.t[:, k, :], in_=banks[k].t[:]), reads=banks[k].b, writes=[x.b[k]])
            dump("ones", ones.t[:], [128, 128], BF16, ones.b)
            dump("epsc", epsc.t[:], [128, 2], F32, epsc.b)
            dump("gvec", gvec.t[:].rearrange("p a b -> p (a b)"), [128, 64], F32, gvec.b)
            dump("esink", esink.t[:], [128, 16], F32, esink.b)
            dump("xT", x.t[:].rearrange("p k n -> p (k n)"), [128, 8 * T], F32, x.b)
            pre_norm(x, hn, sqb, rstd)
            dump("rstd", rstd.t[:], [128, T], F32, rstd.b)
            dump("hn", hn.t[:].rearrange("p k n -> p (k n)"), [128, 8 * T], BF16, hn.b)

            def ep_qk(nb, m, p, hold={}):
                if m % 2 == 0:
                    hold["p"] = p
                    return
                pq = hold["p"]
                cch = nb * 2 + m // 2
                dst, dbuf = (q.t[:, cch, :], q.b[cch]) if nb < 4 else (kk_.t[:, m // 2, :], kk_.b[m // 2])
                c.op("dve", lambda e: e.tensor_tensor(out=tmp[0].t[:], in0=p.t[:], in1=sn.t[:], op=ALU.mult), reads=p.b + sn.b, writes=tmp[0].b)
                c.op("dve", lambda e: e.tensor_tensor(out=tmp[1].t[:], in0=pq.t[:], in1=cs.t[:], op=ALU.mult), reads=pq.b + cs.b, writes=tmp[1].b)
                c.op("pool", lambda e: e.tensor_tensor(out=dst, in0=tmp[0].t[:], in1=tmp[1].t[:], op=ALU.add), reads=tmp[0].b + tmp[1].b, writes=[dbuf])
            linear(ws, lambda k: (hn.t[:, k, :], [hn.b[k]]), 8, 5, 1, ep_qk)
            dump("qT", q.t[:].rearrange("p k n -> p (k n)"), [128, 8 * T], BF16, q.b)
            dump("kT", kk_.t[:].rearrange("p k n -> p (k n)"), [128, 2 * T], BF16, kk_.b)
            w = ws.next()
            for b in range(4):
                p = ps()
                for k in range(8):
                    c.op("pe", lambda e: e.matmul(p.t[:, 0:256], lhsT=hn.t[:, k, b * 128:(b + 1) * 128], rhs=w.t[:, k, 0:256], start=(k == 0), stop=(k == 7)),
                         reads=[hn.b[k]] + w.b, writes=p.b, signal=(k == 7))
                c.op("act", lambda e: e.activation(out=va.t[:, b, :, 0:64], in_=p.t[:, 0:256].rearrange("p (g d) -> p g d", g=4), func=AF.Copy), reads=p.b, writes=[va.b[b]])

        def stageB(t):
            tl = tiles[t]
            x = xT[t % 2]
            q = qT[t % 2]

            def keyblocks(b):
                kbs = []
                if b > 0:
                    kbs.append((t, b - 1, mprev))
                elif not tl["first"]:
                    kbs.append((t - 1, 3, mprev))
                kbs.append((t, b, None))
                if b < 3:
                    kbs.append((t, b + 1, mnext))
                elif not tl["last"]:
                    kbs.append((t + 1, 0, mnext))
                return kbs

            def part1(b, g):
                kbs = keyblocks(b)
                j, e2 = g // 2, g % 2
                rows = slice(e2 * 64, (e2 + 1) * 64)
                pts = PT[g % 2]
                for ci, (kt, kb, msk) in enumerate(kbs):
                    p = ps()
                    kt_ = kT[kt % 3]
                    c.op("pe", lambda e: e.matmul(p.t[:], lhsT=kt_.t[rows, j, kb * 128:(kb + 1) * 128], rhs=q.t[rows, 4 * j:4 * j + 4, b * 128:(b + 1) * 128], start=True, stop=True),
                         reads=[kt_.b[j]] + q.b[4 * j:4 * j + 4], writes=p.b)
                    c.op("act", lambda e: e.activation(out=pts[ci].t[:], in_=p.t[:], func=AF.Exp, scale=HD ** -0.5), reads=p.b, writes=pts[ci].b)
                    if msk is not None:
                        c.op("pool", lambda e: e.tensor_tensor(out=pts[ci].t[:].rearrange("p (r i) -> p r i", r=4), in0=pts[ci].t[:].rearrange("p (r i) -> p r i", r=4),
                                                               in1=msk.t[:].unsqueeze(1).to_broadcast([128, 4, 128]), op=ALU.mult), reads=pts[ci].b + msk.b, writes=pts[ci].b)

            def part2(b, g):
                kbs = keyblocks(b)
                pts = PT[g % 2]
                po = ps()
                pov = po.t[:, 0:260].rearrange("p (r d) -> p r d", r=4)
                for r in range(4):
                    for ci, (kt, kb, msk) in enumerate(kbs):
                        va = Va[kt % 3]
                        c.op("pe", lambda e: e.matmul(pov[:, r, :], lhsT=pts[ci].t[:, r * 128:(r + 1) * 128], rhs=va.t[:, kb, g, 0:65], start=(ci == 0), stop=(ci == len(kbs) - 1)),
                             reads=pts[ci].b + [va.b[kb]], writes=po.b, signal=(r == 3 and ci == len(kbs) - 1))
                c.op("dve", lambda e: e.tensor_tensor(out=den.t[:, 0:4], in0=pov[:, :, 64], in1=esink.t[:, 4 * g:4 * g + 4], op=ALU.add), reads=po.b + esink.b, writes=den.b)
                c.op("dve", lambda e: e.reciprocal(out=den.t[:, 4:8], in_=den.t[:, 0:4]), reads=den.b, writes=den.b)
                c.op("dve", lambda e: e.tensor_tensor(out=Osb.t[:, g * 256:(g + 1) * 256].rearrange("p (r d) -> p r d", r=4), in0=pov[:, :, 0:64],
                                                     in1=den.t[:, 4:8].unsqueeze(2).to_broadcast([128, 4, 64]), op=ALU.mult), reads=po.b + den.b, writes=Osb.b)

            def part3(b):
                p = ps()
                pb = p.t[:].bitcast(BF16)
                for k in range(8):
                    c.op("pe", lambda e: e.transpose(out=pb[:, k * 128:(k + 1) * 128], in_=Osb.t[:, k * 128:(k + 1) * 128], identity=identb.t[:]),
                         reads=Osb.b + identb.b, writes=p.b, signal=(k == 7))
                c.op("act", lambda e: e.activation(out=OT.t[:, :, b * 128:(b + 1) * 128], in_=pb.rearrange("p (k i) -> p k i", k=8), func=AF.Copy), reads=p.b, writes=OT.b)

            its = [(b, g) for b in range(4) for g in range(4)]
            part1(*its[0])
            for ii, (b, g) in enumerate(its):
                if ii + 1 < len(its):
                    part1(*its[ii + 1])
                part2(b, g)
                if g == 3:
                    part3(b)

            dump("Va", Va[t % 3].t[:].rearrange("p a g d -> p (a g d)"), [128, 4 * 4 * 66], BF16, Va[t % 3].b)
            dump("OT", OT.t[:].rearrange("p k n -> p (k n)"), [128, 8 * T], BF16, OT.b)

            def ep_wo(nb, m, p):
                jj = nb * 4 + m
                c.op("act", lambda e: e.activation(out=asb1.t[:, jj * T:(jj + 1) * T], in_=p.t[:], func=AF.Copy), reads=p.b, writes=[asb1.b[jj]])
            linear(ws, lambda k: (OT.t[:, k, :], OT.b), 8, 2, 1, ep_wo)
            dump("ao", asb1.t[:], [128, 4096], F32, asb1.b)
            post_norm_residual(x, asb1, sqb, rstd, 1, tmp)
            dump("x_mid", x.t[:].rearrange("p k n -> p (k n)"), [128, 8 * T], F32, x.b)
            mlp(ws, 0, x, hn, h1T, asb1, sqb, rstd, tmp)
            if phases == 1:
                store_tokmajor_alias(x, tl["row0"], asb1)
            else:
                c.dma("sp", stsem(), s_x1[t], x.t[:].rearrange("p k n -> p (k n)"), reads=x.b)
                pre_norm(x, hn, sqb, rstd)
                c.dma("sp", stsem(), s_hn[t], hn.t[:].rearrange("p k n -> p (k n)"), reads=hn.b)

        for t in range(NT + 1):
            if t < NT:
                stageA(t)
            if t >= 1:
                stageB(t - 1)

    def sched_p2():
        for t in range(NT):
            for nb in range(6):
                yield wblock(s_wfm, 0, nb)
                if nb < 4:
                    yield wblock(s_wtm, 0, nb)
                elif nb == 4:
                    yield (s_wtm[0:1024, 2048:2112].rearrange("(kk p) n -> p kk n", p=128), 64)

    if phases >= 2:
      c.barrier()
      with ExitStack() as es2:
        def sb2(name, shape, dt=F32, n=1):
            return TB(es2.enter_context(nc.sbuf_tensor(name, list(shape), dt)), n)
        ring = [sb2("ringb%d" % i, [128, 8, 512], BF16) for i in range(3)]
        ws = WStream(sched_p2(), ring)
        hn2 = sb2("hn2", [128, 8, 516], BF16, 8)
        cin = [sb2("cin%d" % i, [128, 516]) for i in range(4)]
        acc = [sb2("acc%d" % i, [128, 512]) for i in range(4)]
        xcT = [sb2("xcT%d" % i, [128, 512], BF16) for i in range(8)]
        x_tm = sb2("x_tm", [128, 4, DIN], BF16, 4)
        BT = sb2("BT", [128, 4, T], BF16, 4)
        B_tm = sb2("B_tm", [128, 4, 512], BF16, 4)
        CT = sb2("CT", [128, 4, T], BF16, 4)
        zst = [sb2("zst%d" % i, [128, 512]) for i in range(2)]
        dtx = sb2("dtx", [128, 4, 64])
        dtt = sb2("dtt", [128, 4, 64])
        dtv = sb2("dtv", [128, 4, 64])
        lndt = sb2("lndt", [128, 4, 64])
        da = sb2("da", [128, 4, 64])
        cumx = sb2("cumx", [128, 4, 64])
        biasx = sb2("biasx", [128, 4, 64])
        ecum = sb2("ecum", [128, 4, 64])
        wst = sb2("wst", [128, 4, 64])
        cdec = sb2("cdec", [128, 4, 64])
        cbm = [sb2("cbm%d" % i, [128, 512]) for i in range(2)]
        Ep = [sb2("Ep%d" % i, [128, 8, 128]) for i in range(2)]
        Mt = [sb2("Mt%d" % i, [128, 8, 128], BF16) for i in range(4)]
        Ysb = sb2("Ysb", [128, DIN], F32, 4)
        t1s = [sb2("t1_%d" % i, [128, 512]) for i in range(2)]
        t1c = [0]
        xw = [sb2("xw%d" % i, [128, DIN], BF16, 4) for i in range(2)]
        hstf = sb2("hstf", [128, DIN], F32, 4)
        hstf_bf = sb2("hstf_bf", [128, DIN], BF16, 4)
        Sbsb = sb2("Sbsb", [128, DIN], F32, 4)
        zc = [0]
        itc = [0]
        rowsA = [sb2("rowsA%d" % i, [128, 16, 128], BF16) for i in range(2)]
        rowsB = [sb2("rowsB%d" % i, [128, 16, 128], BF16) for i in range(2)]
        spl = [[sb2("spl%d_%d" % (a, b), [64, T], BF16) for b in range(3)] for a in range(2)]
        for rr in rowsA + rowsB:
            c.op("pool", lambda e: e.memset(rr.t[:], 1.0), writes=rr.b)
        bT = sb2("bT", [64, T])
        cT = sb2("cT", [64, T])
        scrb = Buf()
        Dd = sb2("Dd", [128, NH, 128], BF16)
        c.op("dve", lambda e: e.tensor_tensor(out=Dd.t[:], in0=identb.t[:].unsqueeze(1).to_broadcast([128, NH, 128]), in1=d32.t[:].unsqueeze(2).to_broadcast([128, NH, 128]), op=ALU.mult),
             reads=d32.b + identb.b, writes=Dd.b)

        def chk(n):
            if stop == n:
                raise _Stop()

        def p2_tile(t):
            tl = tiles[t]
            ch0 = tl["row0"] // 128
            c.dma("sp", ldsem(), hn2.t[:, :, 0:512], s_hn[t].rearrange("p (k n) -> p k n", k=8), writes=hn2.b)
            with nc.allow_non_contiguous_dma(reason="2-token conv halos"):
                if tl["first"]:
                    c.op("pool", lambda e: e.memset(hn2.t[:, :, 512:514], 0.0), writes=hn2.b)
                else:
                    c.dma("sp", ldsem(), hn2.t[:, :, 512:514], s_hn[t - 1].rearrange("p (k n) -> p k n", k=8)[:, :, 510:512], writes=hn2.b)
                if tl["last"]:
                    c.op("pool", lambda e: e.memset(hn2.t[:, :, 514:516], 0.0), writes=hn2.b)
                else:
                    c.dma("sp", ldsem(), hn2.t[:, :, 514:516], s_hn[t + 1].rearrange("p (k n) -> p k n", k=8)[:, :, 0:2], writes=hn2.b)
            if tl["first"]:
                c.op("pool", lambda e: e.memset(hstf.t[:], 0.0), writes=hstf.b)
                c.op("pool", lambda e: e.memset(hstf_bf.t[:], 0.0), writes=hstf_bf.b)
            chk(1)
            if t == 1:
                chk(12)
            def tm_block(nb):
                w = ws.next()
                ncol = 512 if nb < 4 else 64
                for j in range(4):
                    p = ps()
                    for kk in range(8):
                        c.op("pe", lambda e: e.matmul(p.t[:, 0:ncol], lhsT=hn2.t[:, kk, j * 128:(j + 1) * 128], rhs=w.t[:, kk, 0:ncol], start=(kk == 0), stop=(kk == 7)),
                             reads=w.b + [hn2.b[kk]], writes=p.b, signal=(kk == 7))
                    if nb < 4:
                        zb = zst[zc[0] % 2]
                        zc[0] += 1
                        if zc[0] % 2 == 0:
                            c.op("act", lambda e: e.activation(out=zb.t[:], in_=p.t[:], func=AF.Copy), reads=p.b, writes=zb.b)
                        else:
                            c.op("dve", lambda e: e.tensor_copy(out=zb.t[:], in_=p.t[:]), reads=p.b, writes=zb.b)
                        r0 = tl["row0"] + j * 128
                        c.dma("sp", stsem(), s_z[r0:r0 + 128, nb * 512:(nb + 1) * 512], zb.t[:], reads=zb.b)
                    else:
                        c.op("dve", lambda e: e.tensor_tensor(out=dtx.t[:, j, :], in0=p.t[:, 0:64], in1=dtb_bc.t[:], op=ALU.add), reads=p.b + dtb_bc.b, writes=dtx.b)

            pending_T = []
            for nb in range(6):
                w = ws.next()
                banks = [ps() for _ in range(4)]
                ph = ps()
                for m in range(4):
                    for kk in range(8):
                        c.op("pe", lambda e: e.matmul(banks[m].t[:], lhsT=w.t[:, kk, m * 128:(m + 1) * 128], rhs=hn2.t[:, kk, 0:512], start=(kk == 0), stop=(kk == 7)),
                             reads=w.b + [hn2.b[kk]], writes=banks[m].b, signal=(kk == 7))
                    for kk in range(8):
                        c.op("pe", lambda e: e.matmul(ph.t[:, m * 4:m * 4 + 4], lhsT=w.t[:, kk, m * 128:(m + 1) * 128], rhs=hn2.t[:, kk, 512:516], start=(kk == 0), stop=(kk == 7)),
                             reads=w.b + [hn2.b[kk]], writes=ph.b, signal=(kk == 7))
                mcs = [nb * 4 + m for m in range(4)]
                for m in range(4):
                    c.op("act", lambda e: e.activation(out=cin[m].t[:, 2:514], in_=banks[m].t[:], func=AF.Copy), reads=banks[m].b, writes=cin[m].b)
                for m in range(4):
                    c.op("dve", lambda e: e.tensor_copy(out=cin[m].t[:, 0:2], in_=ph.t[:, m * 4:m * 4 + 2]), reads=ph.b, writes=cin[m].b)
                    c.op("dve", lambda e: e.tensor_copy(out=cin[m].t[:, 514:516], in_=ph.t[:, m * 4 + 2:m * 4 + 4]), reads=ph.b, writes=cin[m].b)
                if nb < 5:
                    tm_block(nb)
                while pending_T:
                    pending_T.pop(0)()
                for m in range(4):
                    mc = mcs[m]
                    c.op("act", lambda e: e.activation(out=acc[m].t[:], in_=cin[m].t[:, 0:512], func=AF.Identity, scale=convw.t[:, mc, 0:1], bias=convb.t[:, mc:mc + 1]),
                         reads=cin[m].b + convw.b + convb.b, writes=acc[m].b)
                for k5 in range(1, 5):
                    for m in range(4):
                        mc = mcs[m]
                        c.op("dve", lambda e: e.scalar_tensor_tensor(out=acc[m].t[:], in0=cin[m].t[:, k5:k5 + 512], scalar=convw.t[:, mc, k5:k5 + 1], in1=acc[m].t[:], op0=ALU.mult, op1=ALU.add),
                             reads=cin[m].b + acc[m].b + convw.b, writes=acc[m].b)
                dsts = []
                for m in range(4):
                    mc = mcs[m]
                    if mc < 16:
                        dst, dbufs = xcT[(nb % 2) * 4 + m].t[:], xcT[(nb % 2) * 4 + m].b
                    elif mc < 20:
                        dst, dbufs = BT.t[:, mc - 16, :], [BT.b[mc - 16]]
                    else:
                        dst, dbufs = CT.t[:, mc - 20, :], [CT.b[mc - 20]]
                    dsts.append((dst, dbufs))
                    c.op("act", lambda e: e.activation(out=dst, in_=acc[m].t[:], func=AF.Silu), reads=acc[m].b, writes=dbufs)
                if nb < 5:
                    def do_T(dsts=dsts, mcs=mcs):
                        pps = []
                        for m in range(4):
                            dst, dbufs = dsts[m]
                            p = ps()
                            pps.append(p)
                            pb = p.t[:].bitcast(BF16)
                            for j in range(4):
                                c.op("pe", lambda e: e.transpose(out=pb[:, j * 128:(j + 1) * 128], in_=dst[:, j * 128:(j + 1) * 128], identity=identb.t[:]),
                                     reads=dbufs + identb.b, writes=p.b, signal=(j == 3))
                        for m in range(4):
                            mc = mcs[m]
                            pb = pps[m].t[:].bitcast(BF16)
                            if mc < 16:
                                c.op("act", lambda e: e.activation(out=x_tm.t[:, :, mc * 128:(mc + 1) * 128], in_=pb[:, 0:512].rearrange("p (j i) -> p j i", j=4), func=AF.Copy), reads=pps[m].b, writes=x_tm.b)
                            else:
                                g = mc - 16
                                c.op("dve", lambda e: e.tensor_copy(out=B_tm.t[:, :, g * 128:(g + 1) * 128], in_=pb[:, 0:512].rearrange("p (j i) -> p j i", j=4)), reads=pps[m].b, writes=B_tm.b)
                    pending_T.append(do_T)
            while pending_T:
                pending_T.pop(0)()
            chk(2)
            if t == 1:
                chk(14)
            c.dma("sp", stsem(), s_ct[t], CT.t[:].rearrange("p g n -> p (g n)"), reads=CT.b)
            chk(3)
            if t == 1:
                chk(15)
            c.op("act", lambda e: e.activation(out=dtt.t[:], in_=dtx.t[:], func=AF.Abs), reads=dtx.b, writes=dtt.b)
            c.op("act", lambda e: e.activation(out=dtt.t[:], in_=dtt.t[:], func=AF.Exp, scale=-1.0), reads=dtt.b, writes=dtt.b)
            c.op("act", lambda e: e.activation(out=dtt.t[:], in_=dtt.t[:], func=AF.Ln, bias=epsc.t[:, 2:3]), reads=dtt.b + epsc.b, writes=dtt.b)
            c.op("dve", lambda e: e.scalar_tensor_tensor(out=dtv.t[:], in0=dtx.t[:], scalar=0.0, in1=dtt.t[:], op0=ALU.max, op1=ALU.add), reads=dtx.b + dtt.b, writes=dtv.b)
            c.op("act", lambda e: e.activation(out=lndt.t[:], in_=dtv.t[:], func=AF.Ln), reads=dtv.b, writes=lndt.b)
            c.op("dve", lambda e: e.tensor_tensor(out=da.t[:], in0=dtv.t[:], in1=a_bc.t[:].unsqueeze(1).to_broadcast([128, 4, 64]), op=ALU.mult), reads=dtv.b + a_bc.b, writes=da.b)
            pd = ps()
            pdv = pd.t[:].rearrange("p (j c) -> p j c", j=4)
            for j in range(4):
                c.op("pe", lambda e: e.matmul(pdv[:, j, 0:32], lhsT=tle.t[:], rhs=da.t[:, j, 0:32], start=True, stop=True), reads=tle.b + da.b, writes=pd.b, signal=False)
                c.op("pe", lambda e: e.matmul(pdv[:, j, 32:64], lhsT=tge.t[:], rhs=da.t[:, j, 32:64], start=True, stop=True), reads=tge.b + da.b, writes=pd.b, signal=False)
                c.op("pe", lambda e: e.matmul(pdv[:, j, 64:128], lhsT=onesf.t[:], rhs=da.t[:, j, 0:64], start=True, stop=True), reads=onesf.b + da.b, writes=pd.b, signal=(j == 3))
            c.op("act", lambda e: e.activation(out=cumx.t[:], in_=pdv[:, :, 0:64], func=AF.Copy), reads=pd.b, writes=cumx.b)
            c.op("dve", lambda e: e.tensor_tensor(out=biasx.t[:], in0=lndt.t[:], in1=pdv[:, :, 0:64], op=ALU.subtract), reads=lndt.b + pd.b, writes=biasx.b)
            c.op("act", lambda e: e.activation(out=ecum.t[:], in_=pdv[:, :, 0:64], func=AF.Exp), reads=pd.b, writes=ecum.b)
            c.op("dve", lambda e: e.tensor_tensor(out=wst.t[:], in0=biasx.t[:], in1=pdv[:, :, 64:128], op=ALU.add), reads=biasx.b + pd.b, writes=wst.b)
            c.op("act", lambda e: e.activation(out=wst.t[:], in_=wst.t[:], func=AF.Exp), reads=wst.b, writes=wst.b)
            c.op("act", lambda e: e.activation(out=cdec.t[:], in_=pdv[:, :, 64:128], func=AF.Exp), reads=pd.b, writes=cdec.b)
            for qi_, (src_, dst_, scr_) in enumerate(((biasx, bT, s_bT), (cumx, cT, s_cT))):
                pt_ = ps()
                for j in range(4):
                    c.op("pe", lambda e: e.transpose(out=pt_.t[0:64, j * 128:(j + 1) * 128], in_=src_.t[:, j, :], identity=ident.t[:]),
                         reads=src_.b + ident.b, writes=pt_.b, signal=(j == 3))
                c.op("dve", lambda e: e.tensor_copy(out=dst_.t[:], in_=pt_.t[0:64, :]), reads=pt_.b, writes=dst_.b)
                for part in range(3):
                    sp_ = spl[qi_][part]
                    c.op("dve", lambda e: e.tensor_copy(out=sp_.t[:], in_=dst_.t[:]), reads=dst_.b, writes=sp_.b)
                    if part < 2:
                        c.op("dve", lambda e: e.tensor_tensor(out=dst_.t[:], in0=dst_.t[:], in1=sp_.t[:], op=ALU.subtract), reads=dst_.b + sp_.b, writes=dst_.b)
                    c.dma("sp", stsem(), scr_[t, part], sp_.t[:], reads=sp_.b, writes=[scrb])
            chk(4)
            r0t = tl["row0"]
            with nc.allow_non_contiguous_dma(reason="small per-token rows"):
                for j in range(4):
                    c.dma("sp", stsem(), s_ecb[r0t + j * 128:r0t + (j + 1) * 128, :], ecum.t[:, j, 32:64], reads=ecum.b)
                    c.dma("sp", stsem(), s_cdb[ch0 + j], cdec.t[:, j, 32:64], reads=cdec.b)
            chk(5)
            if t == 1:
                chk(16)
            for j in range(4):
                cols = slice(j * 128, (j + 1) * 128)
                r0 = tl["row0"] + j * 128
                pcb = psum[6]
                for g in range(4):
                    c.op("pe", lambda e: e.matmul(pcb.t[:, g * 128:(g + 1) * 128], lhsT=BT.t[:, g, cols], rhs=CT.t[:, g, cols], start=True, stop=True),
                         reads=[BT.b[g], CT.b[g]], writes=pcb.b, signal=(g == 3))
                c.op("dve", lambda e: e.tensor_tensor(out=cbm[0].t[:].rearrange("p (g l) -> p g l", g=4), in0=pcb.t[:].rearrange("p (g l) -> p g l", g=4),
                                                     in1=tle.t[:].unsqueeze(1).to_broadcast([128, 4, 128]), op=ALU.mult), reads=pcb.b + tle.b, writes=cbm[0].b)
                c.op("dve", lambda e: e.tensor_tensor(out=cbm[1].t[:].rearrange("p (g l) -> p g l", g=4), in0=pcb.t[:].rearrange("p (g l) -> p g l", g=4),
                                                     in1=tge.t[:].unsqueeze(1).to_broadcast([128, 4, 128]), op=ALU.mult), reads=pcb.b + tge.b, writes=cbm[1].b)
                for d in range(2):
                    xwd = xw[d]
                    for g in range(4):
                        c.op("pool", lambda e: e.tensor_tensor(out=xwd.t[:, g * 512:(g + 1) * 512].rearrange("p (h d) -> p h d", h=8), in0=x_tm.t[:, j, g * 512:(g + 1) * 512].rearrange("p (h d) -> p h d", h=8),
                                                              in1=wst.t[:, j, d * 32 + g * 8:d * 32 + (g + 1) * 8].unsqueeze(2).to_broadcast([128, 8, 64]), op=ALU.mult),
                             reads=[x_tm.b[j]] + wst.b, writes=[xwd.b[g]])
                chk(6)
                rA, rB = rowsA[j % 2], rowsB[j % 2]
                for q in range(4):
                    c.dma("sp", ldsem(), rA.t[32 * q + 3:32 * q + 6, :, :], s_bT[t][:, 16 * q:16 * q + 16, cols], reads=[scrb], writes=rA.b)
                    c.dma("sp", ldsem(), rB.t[32 * q:32 * q + 3, :, :], s_cT[t][:, 16 * q:16 * q + 16, cols], reads=[scrb], writes=rB.b)

                def emit_T(g):
                    info = []
                    for d in range(2):
                        it = itc[0]
                        itc[0] += 1
                        rb = [psum[4 + 2 * (it % 2)], psum[5 + 2 * (it % 2)]]
                        for hh in range(8):
                            col = d * 32 + g * 8 + hh
                            q, h16 = col // 16, col % 16
                            pr = rb[hh // 4]
                            c.op("pe", lambda e: e.matmul(pr.t[:, (hh % 4) * 128:(hh % 4 + 1) * 128], lhsT=rA.t[32 * q:32 * q + 6, h16, :], rhs=rB.t[32 * q:32 * q + 6, h16, :],
                                                          start=True, stop=True, tile_position=(32 * q, 0)),
                                 reads=rA.b + rB.b, writes=pr.b, signal=(hh % 4 == 3))
                        info.append((it, rb))
                    return info

                nxt = emit_T(0)
                for g in range(4):
                    cur = nxt
                    mts = []
                    for d in range(2):
                        it, rb = cur[d]
                        ep = Ep[it % 2]
                        mt = Mt[it % 4]
                        mts.append(mt)
                        for hb in range(2):
                            pr = rb[hb]
                            c.op("act", lambda e: e.activation(out=ep.t[:, hb * 4:(hb + 1) * 4, :], in_=pr.t[:].rearrange("p (h l) -> p h l", h=4), func=AF.Exp), reads=pr.b, writes=ep.b)
                        c.op("dve", lambda e: e.scalar_tensor_tensor(out=mt.t[:], in0=ep.t[:], scalar=1e30, in1=cbm[d].t[:, g * 128:(g + 1) * 128].unsqueeze(1).to_broadcast([128, 8, 128]),
                                                                    op0=ALU.min, op1=ALU.mult), reads=ep.b + cbm[d].b, writes=mt.b)
                    if g < 3:
                        nxt = emit_T(g + 1)
                    for hh in range(8):
                        h = g * 8 + hh
                        xs_ = x_tm.t[:, j, h * 64:(h + 1) * 64]
                        c.op("pe", lambda e: e.matmul(psum[g].t[:, hh * 64:(hh + 1) * 64], lhsT=mts[0].t[:, hh, :], rhs=xs_, start=True, stop=False),
                             reads=mts[0].b + [x_tm.b[j]], writes=psum[g].b, signal=False)
                        c.op("pe", lambda e: e.matmul(psum[g].t[:, hh * 64:(hh + 1) * 64], lhsT=mts[1].t[:, hh, :], rhs=xs_, start=False, stop=False),
                             reads=mts[1].b + [x_tm.b[j]], writes=psum[g].b, signal=False)
                        c.op("pe", lambda e: e.matmul(psum[g].t[:, hh * 64:(hh + 1) * 64], lhsT=Dd.t[:, h, :], rhs=xs_, start=False, stop=True),
                             reads=Dd.b + [x_tm.b[j]], writes=psum[g].b, signal=(hh == 7))
                chk(7)
                for g in range(4):
                    c.op("act", lambda e: e.activation(out=Ysb.t[:, g * 512:(g + 1) * 512], in_=psum[g].t[:], func=AF.Copy), reads=psum[g].b, writes=[Ysb.b[g]])
                for g in range(4):
                    c.op("pe", lambda e: e.matmul(psum[g].t[:], lhsT=CT.t[:, g, cols], rhs=hstf_bf.t[:, g * 512:(g + 1) * 512], start=True, stop=True),
                         reads=[CT.b[g], hstf_bf.b[g]], writes=psum[g].b)
                for g in range(4):
                    t1 = t1s[t1c[0] % 2]
                    t1c[0] += 1
                    c.op("dve", lambda e: e.tensor_tensor(out=t1.t[:].rearrange("p (h d) -> p h d", h=8), in0=psum[g].t[:].rearrange("p (h d) -> p h d", h=8),
                                                         in1=ecum.t[:, j, g * 8:(g + 1) * 8].unsqueeze(2).to_broadcast([128, 8, 64]), op=ALU.mult), reads=psum[g].b + ecum.b, writes=t1.b)
                    c.op("pool", lambda e: e.tensor_tensor(out=Ysb.t[:, g * 512:(g + 1) * 512], in0=Ysb.t[:, g * 512:(g + 1) * 512], in1=t1.t[:], op=ALU.add),
                         reads=t1.b + [Ysb.b[g]], writes=[Ysb.b[g]])
                c.dma("sp", stsem(), s_yp[r0:r0 + 128, :], Ysb.t[:], reads=Ysb.b)
                chk(8)
                for g in range(4):
                    c.op("pe", lambda e: e.matmul(psum[g].t[:], lhsT=B_tm.t[:, j, g * 128:(g + 1) * 128], rhs=xw[0].t[:, g * 512:(g + 1) * 512], start=True, stop=True),
                         reads=[B_tm.b[j], xw[0].b[g]], writes=psum[g].b)
                for g in range(4):
                    c.op("dve", lambda e: e.tensor_tensor(out=hstf.t[:, g * 512:(g + 1) * 512].rearrange("p (h d) -> p h d", h=8), in0=hstf.t[:, g * 512:(g + 1) * 512].rearrange("p (h d) -> p h d", h=8),
                                                         in1=cdec.t[:, j, g * 8:(g + 1) * 8].unsqueeze(2).to_broadcast([128, 8, 64]), op=ALU.mult), reads=[hstf.b[g]] + cdec.b, writes=[hstf.b[g]])
                for g in range(4):
                    c.op("dve", lambda e: e.tensor_tensor(out=hstf.t[:, g * 512:(g + 1) * 512], in0=hstf.t[:, g * 512:(g + 1) * 512], in1=psum[g].t[:], op=ALU.add),
                         reads=[hstf.b[g]] + psum[g].b, writes=[hstf.b[g]])
                for g in range(4):
                    c.op("pool", lambda e: e.tensor_copy(out=hstf_bf.t[:, g * 512:(g + 1) * 512], in_=hstf.t[:, g * 512:(g + 1) * 512]), reads=[hstf.b[g]], writes=[hstf_bf.b[g]])
                for g in range(4):
                    c.op("pe", lambda e: e.matmul(psum[g].t[:], lhsT=B_tm.t[:, j, g * 128:(g + 1) * 128], rhs=xw[1].t[:, g * 512:(g + 1) * 512], start=True, stop=True),
                         reads=[B_tm.b[j], xw[1].b[g]], writes=psum[g].b)
                for g in range(4):
                    c.op("act", lambda e: e.activation(out=Sbsb.t[:, g * 512:(g + 1) * 512], in_=psum[g].t[:], func=AF.Copy), reads=psum[g].b, writes=[Sbsb.b[g]])
                c.dma("sp", stsem(), s_sb[ch0 + j], Sbsb.t[:], reads=Sbsb.b)
                chk(10)
                if j == 3:
                    chk(11)

        try:
            for t in range(NT):
                p2_tile(t)
        except _Stop:
            pass

    def sched_p3():
        for t in range(NT):
            for nb in range(2):
                for kb in range(2):
                    yield wblock(s_wout, kb, nb)
            for b in range(8):
                yield wblock(s_up[1], 0, b)
            for nb in range(2):
                for kb in range(4):
                    yield wblock(s_dn[1], kb, nb)

    if phases >= 3:
      c.barrier()
      with ExitStack() as es3:
        def sb3(name, shape, dt=F32, n=1):
            return TB(es3.enter_context(nc.sbuf_tensor(name, list(shape), dt)), n)
        ring = [sb3("ringc%d" % i, [128, 8, 512], BF16) for i in range(NRING)]
        ws = WStream(sched_p3(), ring)
        x1T = sb3("x1T", [128, 8, T], F32, 8)
        CT3 = sb3("CT3", [128, 4, T], BF16)
        yps = [sb3("yp%d" % i, [128, DIN], F32, 4) for i in range(2)]
        zzs = [sb3("zz%d" % i, [128, DIN], F32, 4) for i in range(2)]
        ecbs = [sb3("ecb%d" % i, [128, NH]) for i in range(2)]
        Sb3s = [sb3("Sb3_%d" % i, [128, DIN], F32, 4) for i in range(2)]
        cdbs = [sb3("cdb%d" % i, [128, NH]) for i in range(2)]
        hstb = sb3("hstb", [128, DIN], F32, 4)
        hstb_bf = sb3("hstb_bf", [128, DIN], BF16, 4)
        t3s = [sb3("t3_%d" % i, [128, 512]) for i in range(2)]
        ss = sb3("ss", [128, 8])
        mhalf = sb3("mhalf", [128, 1])
        c.op("pool", lambda e: e.memset(mhalf.t[:], -0.5), writes=mhalf.b)
        Gn = sb3("Gn", [128, DIN], BF16)
        gT = sb3("gT", [128, 16, T], BF16, 16)
        asb3 = sb3("asb3", [128, 4096], F32, 8)
        rstd3 = sb3("rstd3", [128, T])
        hn3 = sb3("hn3", [128, 8, T], BF16, 8)
        h1T3 = sb3("h1T3", [128, 32, T], BF16, 32)
        sqb3 = TB(h1T3.t[:, 0:8, :], 0)
        sqb3.b = h1T3.b[0:8]
        tmp3 = [sb3("tmp3_%d" % i, [128, T]) for i in range(2)]
        order = []
        for si in range(len(seq_lens)):
            ts_ = [tt for tt in range(NT) if tiles[tt]["seq"] == si]
            order += ts_[::-1]
        chunks = [(t, j) for t in order for j in (3, 2, 1, 0)]

        def prefetch(i):
            if i >= len(chunks):
                return
            t, j = chunks[i]
            tl = tiles[t]
            r0 = tl["row0"] + j * 128
            chn = tl["row0"] // 128 + j
            pi = i % 2
            c.dma("sp", ldsem(), yps[pi].t[:], s_yp[r0:r0 + 128, :], writes=yps[pi].b)
            c.dma("sp", ldsem(), zzs[pi].t[:], s_z[r0:r0 + 128, :], writes=zzs[pi].b)
            c.dma("sp", ldsem(), Sb3s[pi].t[:], s_sb[chn], writes=Sb3s[pi].b)
            with nc.allow_non_contiguous_dma(reason="small per-token rows"):
                c.dma("sp", ldsem(), ecbs[pi].t[:], s_ecb[r0:r0 + 128, :], writes=ecbs[pi].b)
                c.dma("sp", ldsem(), cdbs[pi].t[:], s_cdb[chn], writes=cdbs[pi].b)

        prefetch(0)
        ci = [0]
        t3c = [0]

        def CL(t):
            tl = tiles[t]
            c.dma("sp", ldsem(), CT3.t[:], s_ct[t].rearrange("p (g n) -> p g n", g=4), writes=CT3.b)
            if tl["last"]:
                c.op("pool", lambda e: e.memset(hstb.t[:], 0.0), writes=hstb.b)
                c.op("pool", lambda e: e.memset(hstb_bf.t[:], 0.0), writes=hstb_bf.b)
            for j in (3, 2, 1, 0):
                cols = slice(j * 128, (j + 1) * 128)
                i = ci[0]
                ci[0] += 1
                assert chunks[i] == (t, j)
                prefetch(i + 1)
                yp, zz, ecb, Sb3, cdb = yps[i % 2], zzs[i % 2], ecbs[i % 2], Sb3s[i % 2], cdbs[i % 2]
                G4 = [slice(g * 512, (g + 1) * 512) for g in range(4)]
                for g in range(4):
                    t3 = t3s[t3c[0] % 2]
                    t3c[0] += 1
                    py = ps()
                    c.op("pe", lambda e: e.matmul(py.t[:], lhsT=CT3.t[:, g, cols], rhs=hstb_bf.t[:, G4[g]], start=True, stop=True), reads=CT3.b + [hstb_bf.b[g]], writes=py.b)
                    c.op("dve", lambda e: e.tensor_tensor(out=t3.t[:].rearrange("p (h d) -> p h d", h=8), in0=py.t[:].rearrange("p (h d) -> p h d", h=8),
                                                         in1=ecb.t[:, g * 8:(g + 1) * 8].unsqueeze(2).to_broadcast([128, 8, 64]), op=ALU.mult), reads=py.b + ecb.b, writes=t3.b)
                    c.op("pool", lambda e: e.tensor_tensor(out=yp.t[:, G4[g]], in0=yp.t[:, G4[g]], in1=t3.t[:], op=ALU.add), reads=t3.b + [yp.b[g]], writes=[yp.b[g]])
                yield
                for g in range(4):
                    c.op("dve", lambda e: e.tensor_tensor(out=hstb.t[:, G4[g]].rearrange("p (h d) -> p h d", h=8), in0=hstb.t[:, G4[g]].rearrange("p (h d) -> p h d", h=8),
                                                         in1=cdb.t[:, g * 8:(g + 1) * 8].unsqueeze(2).to_broadcast([128, 8, 64]), op=ALU.mult), reads=[hstb.b[g]] + cdb.b, writes=[hstb.b[g]])
                for g in range(4):
                    eng = "pool" if g < 2 else "dve"
                    c.op(eng, lambda e: e.tensor_tensor(out=hstb.t[:, G4[g]], in0=hstb.t[:, G4[g]], in1=Sb3.t[:, G4[g]], op=ALU.add), reads=[hstb.b[g], Sb3.b[g]], writes=[hstb.b[g]])
                for g in range(4):
                    c.op("pool", lambda e: e.tensor_copy(out=hstb_bf.t[:, G4[g]], in_=hstb.t[:, G4[g]]), reads=[hstb.b[g]], writes=[hstb_bf.b[g]])
                yield
                for g in range(4):
                    c.op("act", lambda e: e.activation(out=zz.t[:, G4[g]], in_=zz.t[:, G4[g]], func=AF.Silu), reads=[zz.b[g]], writes=[zz.b[g]])
                for g in range(4):
                    c.op("dve", lambda e: e.tensor_tensor(out=yp.t[:, G4[g]], in0=yp.t[:, G4[g]], in1=zz.t[:, G4[g]], op=ALU.mult), reads=[yp.b[g], zz.b[g]], writes=[yp.b[g]])
                for g in range(4):
                    c.op("act", lambda e: e.activation(out=zz.t[:, G4[g]], in_=yp.t[:, G4[g]], func=AF.Square, accum_out=ss.t[:, g:g + 1]), reads=[yp.b[g]], writes=[zz.b[g]] + ss.b)
                yield
                c.op("dve", lambda e: e.tensor_reduce(out=ss.t[:, 4:5], in_=ss.t[:, 0:4], op=ALU.add, axis=mybir.AxisListType.X), reads=ss.b, writes=ss.b)
                c.op("pool", lambda e: e.tensor_scalar(out=ss.t[:, 5:6], in0=ss.t[:, 4:5], scalar1=1.0 / DIN, scalar2=GEPS, op0=ALU.mult, op1=ALU.add), reads=ss.b, writes=ss.b)
                c.op("pool", lambda e: e.tensor_tensor(out=ss.t[:, 4:5], in0=ss.t[:, 5:6], in1=mhalf.t[:, 0:1], op=ALU.pow), reads=ss.b + mhalf.b, writes=ss.b)
                c.op("dve", lambda e: e.tensor_scalar(out=Gn.t[:], in0=yp.t[:], scalar1=ss.t[:, 4:5], scalar2=None, op0=ALU.mult), reads=yp.b + ss.b, writes=Gn.b)
                yield
                for half in range(2):
                    p = ps()
                    pb = p.t[:].bitcast(BF16)
                    for kk in range(8):
                        k = half * 8 + kk
                        c.op("pe", lambda e: e.transpose(out=pb[:, kk * 128:(kk + 1) * 128], in_=Gn.t[:, k * 128:(k + 1) * 128], identity=identb.t[:]),
                             reads=Gn.b + identb.b, writes=p.b, signal=(kk == 7))
                    c.op("act", lambda e: e.activation(out=gT.t[:, half * 8:(half + 1) * 8, cols], in_=pb.rearrange("p (k i) -> p k i", k=8), func=AF.Copy), reads=p.b, writes=gT.b[half * 8:(half + 1) * 8])
                yield

        def HEAD(t):
            tl = tiles[t]
            c.dma("sp", ldsem(), x1T.t[:], s_x1[t].rearrange("p (k n) -> p k n", k=8), writes=x1T.b)

            def ep_out(nb, m, p):
                jj = nb * 4 + m
                c.op("act", lambda e: e.activation(out=asb3.t[:, jj * T:(jj + 1) * T], in_=p.t[:], func=AF.Copy), reads=p.b, writes=[asb3.b[jj]])
            yield from linear_g(ws, lambda k: (gT.t[:, k, :], [gT.b[k]]), 16, 2, 2, ep_out)
            post_norm_residual(x1T, asb3, sqb3, rstd3, 5, tmp3)
            yield
            yield from mlp_g(ws, 1, x1T, hn3, h1T3, asb3, sqb3, rstd3, tmp3)
            store_tokmajor_alias(x1T, tl["row0"], asb3)
            yield

        def drive(gh, gc, lead):
            nh = 0
            dh = gh is None
            dc = gc is None
            while not (dh and dc):
                if not dh:
                    cur_pool[0] = "A"
                    try:
                        next(gh)
                    except StopIteration:
                        dh = True
                    nh += 1
                if not dc and (dh or nh > lead):
                    cur_pool[0] = "B"
                    try:
                        next(gc)
                    except StopIteration:
                        dc = True
            cur_pool[0] = "all"

        drive(None, CL(order[0]), 0)
        for oi, t in enumerate(order):
            nxt = CL(order[oi + 1]) if oi + 1 < len(order) else None
            drive(HEAD(t), nxt, 4)

    c.final_wait("sp")
    es.close()
    return nc, c


WNAMES = ["attn_w_qkv", "attn_w_o", "attn_sink", "ssm_w_in", "ssm_conv_w", "ssm_conv_b", "ssm_dt_bias", "ssm_a_log",
          "ssm_d", "ssm_norm_w", "ssm_w_out", "norm_mix_pre", "norm_mix_post", "norm_ffn_pre", "norm_ffn_post",
          "mlp_w_up", "mlp_w_down"]


def _wmap(inputs, lmax):
    m = {}
    f = lambda a: np.ascontiguousarray(np.asarray(a, dtype=np.float32))
    m["attn_w_qkv"] = f(inputs["attn_w_qkv"])[0]
    m["attn_w_o"] = f(inputs["attn_w_o"])[0]
    m["attn_sink"] = f(inputs["attn_sink"]).reshape(1, NQH)
    m["ssm_w_in"] = f(inputs["ssm_w_in"])[0]
    m["ssm_conv_w"] = f(inputs["ssm_conv_w"])[0]
    m["ssm_conv_b"] = f(inputs["ssm_conv_b"]).reshape(1, CONVD)
    m["ssm_dt_bias"] = f(inputs["ssm_dt_bias"]).reshape(1, 64)
    m["ssm_a_log"] = f(inputs["ssm_a_log"]).reshape(1, 64)
    m["ssm_d"] = f(inputs["ssm_d"]).reshape(1, NH)
    m["ssm_norm_w"] = f(inputs["ssm_norm_w"]).reshape(1, DIN)
    m["ssm_w_out"] = f(inputs["ssm_w_out"])[0]
    for n in ("norm_mix_pre", "norm_mix_post", "norm_ffn_pre", "norm_ffn_post", "mlp_w_up", "mlp_w_down"):
        m[n] = f(inputs[n])
    m.update(_consts(lmax))
    return m


def kernel(**inputs):
    xp = np.asarray(inputs["x_prompt"], dtype=np.float32)
    xs = np.asarray(inputs["x_sample"], dtype=np.float32)
    B, L, _ = xp.shape
    Bs, Ls, _ = xs.shape
    nc, _ = build([L, Ls])
    wm = _wmap(inputs, max(L, Ls))
    in_maps = []
    for i in range(8):
        m = dict(wm)
        m["x"] = np.ascontiguousarray(np.concatenate([xp[i], xs[i % Bs]], axis=0))
        in_maps.append(m)
    res = run_bass_kernel_spmd(nc, in_maps, core_ids=list(range(8)))
    yp = np.stack([res.results[i]["y"][:L] for i in range(8)], 0)
    ys = np.stack([res.results[i]["y"][L:] for i in range(Bs)], 0)
    return (yp.astype(np.float32), ys.astype(np.float32))
```

```python
import math
from contextlib import ExitStack
import numpy as np
import ml_dtypes
import concourse.bass as bass
import concourse.mybir as mybir
from concourse.bass_utils import run_bass_kernel_spmd

F32 = mybir.dt.float32
BF16 = mybir.dt.bfloat16
ALU = mybir.AluOpType
AF = mybir.ActivationFunctionType

D = 1024
T = 512
NQH, NKV, HD = 16, 4, 64
DFF = 4096
DIN = 2048
NH = 32
NG = 4
DS = 128
CONVD = 3072
INP = 5184
EPS = 1e-6
GEPS = 1e-5
NRING = 4


class Buf:
    __slots__ = ("name", "w", "r", "excl")

    def __init__(self, name=""):
        self.name = name
        self.w = None
        self.r = {}
        self.excl = False


class Ctx:
    def __init__(self, nc):
        self.nc = nc
        self.eng = {"pe": nc.tensor, "act": nc.scalar, "dve": nc.vector,
                    "pool": nc.gpsimd, "sp": nc.sync}
        self.sems = {}
        self.cnt = {}
        for k in self.eng:
            self.sems[k] = nc.alloc_semaphore("s_" + k)
            self.cnt[k] = 0
        self.waited = {}
        self.n_dma_sem = 0
        self.ninst = 0
        self.nwait = 0

    def new_dma_sem(self):
        k = "dma%d" % self.n_dma_sem
        self.n_dma_sem += 1
        self.sems[k] = self.nc.alloc_semaphore("s_" + k)
        self.cnt[k] = 0
        return k

    def _wait(self, e, dep):
        if dep is None:
            return
        key, val = dep
        if self.waited.get((e, key), 0) >= val:
            return
        if key == e and e == "pe":
            return
        self.eng[e].wait_ge(self.sems[key], val)
        self.waited[(e, key)] = val
        self.nwait += 1

    def deps(self, e, reads, writes):
        for b in reads:
            self._wait(e, b.w)
            if b.excl:
                for k, v in list(b.r.items()):
                    if k != e:
                        self._wait(e, (k, v))
        for b in writes:
            self._wait(e, b.w)
            for k, v in list(b.r.items()):
                self._wait(e, (k, v))

    def op(self, e, fn, reads=(), writes=(), signal=True):
        self.deps(e, reads, writes)
        ins = fn(self.eng[e])
        self.ninst += 1
        if signal:
            self.cnt[e] += 1
            ins.then_inc(self.sems[e], 1)
            v = self.cnt[e]
        else:
            v = self.cnt[e] + 1
        for b in writes:
            b.w = (e, v)
            b.r = {}
        for b in reads:
            if b.r.get(e, 0) < v:
                b.r[e] = v
        return ins

    def dma(self, q, semk, out, in_, reads=(), writes=()):
        self.deps(q, reads, writes)
        if self.cnt[semk] > 0:
            self._wait(q, (semk, self.cnt[semk]))
        ins = self.eng[q].dma_start(out=out, in_=in_)
        self.cnt[semk] += 16
        ins.then_inc(self.sems[semk], 16)
        v = self.cnt[semk]
        self.ninst += 1
        for b in writes:
            b.w = (semk, v)
            b.r = {}
        for b in reads:
            if b.r.get(semk, 0) < v:
                b.r[semk] = v
        return ins

    def barrier(self):
        for e in self.eng:
            for k, v in self.cnt.items():
                if v > 0 and k != e:
                    self._wait(e, (k, v))

    def final_wait(self, e="sp"):
        for k, v in self.cnt.items():
            if v > 0 and k != e:
                self._wait(e, (k, v))


class TB:
    def __init__(self, t, n=1):
        self.t = t
        self.b = [Buf() for _ in range(n)]


def _consts(lmax):
    inv = 1.0 / (10000.0 ** (np.arange(0, HD, 2, dtype=np.float32) / HD))
    ang = np.arange(lmax, dtype=np.float32)[:, None] * inv[None, :].astype(np.float32)
    ang = ang.astype(np.float32)
    cos = np.cos(ang).astype(np.float32).T
    sin = np.sin(ang).astype(np.float32).T
    cosT = np.concatenate([cos, cos, cos, cos], 0)
    sinT = np.concatenate([-sin, sin, -sin, sin], 0)
    j = np.arange(128)[:, None]
    i = np.arange(128)[None, :]
    mprev = (j >= i).astype(np.float32)
    mnext = (j <= i).astype(np.float32)
    return {
        "c_cos": np.ascontiguousarray(cosT), "c_sin": np.ascontiguousarray(sinT),
        "c_ident": np.eye(128, dtype=np.float32),
        "c_mprev": mprev.astype(ml_dtypes.bfloat16), "c_mnext": mnext.astype(ml_dtypes.bfloat16),
        "c_tle": mnext.astype(np.float32), "c_tge": mprev.astype(np.float32),
    }


class _Stop(Exception):
    pass


def build(seq_lens, phases=3, debug=False, stop=0):
    ntok = sum(seq_lens)
    lmax = max(seq_lens)
    nc = bass.Bass("TRN2", target_bir_lowering=False)
    c = Ctx(nc)

    def din(name, shape, dt=F32):
        return nc.dram_tensor(name, list(shape), dt, kind="ExternalInput").ap()

    def dscr(name, shape, dt):
        return nc.dram_tensor(name, list(shape), dt, kind="Internal").ap()

    x_d = din("x", [ntok, D])
    out_d = nc.dram_tensor("y", [ntok, D], F32, kind="ExternalOutput").ap()
    wqkv_d = din("attn_w_qkv", [D, 1536])
    wo_d = din("attn_w_o", [D, D])
    sink_d = din("attn_sink", [1, NQH])
    win_d = din("ssm_w_in", [D, INP])
    convw_d = din("ssm_conv_w", [5, CONVD])
    convb_d = din("ssm_conv_b", [1, CONVD])
    dtb_d = din("ssm_dt_bias", [1, 64])
    alog_d = din("ssm_a_log", [1, 64])
    dsk_d = din("ssm_d", [1, NH])
    gnw_d = din("ssm_norm_w", [1, DIN])
    wout_d = din("ssm_w_out", [DIN, D])
    nmpre_d = din("norm_mix_pre", [2, D])
    nmpost_d = din("norm_mix_post", [2, D])
    nfpre_d = din("norm_ffn_pre", [2, D])
    nfpost_d = din("norm_ffn_post", [2, D])
    wup_d = din("mlp_w_up", [2, D, DFF])
    wdn_d = din("mlp_w_down", [2, DFF, D])
    cos_d = din("c_cos", [128, lmax])
    sin_d = din("c_sin", [128, lmax])
    ident_d = din("c_ident", [128, 128])
    mprev_d = din("c_mprev", [128, 128], BF16)
    mnext_d = din("c_mnext", [128, 128], BF16)
    tle_d = din("c_tle", [128, 128])
    tge_d = din("c_tge", [128, 128])

    s_qk = dscr("s_qk", [D, 2560], BF16)
    s_v = dscr("s_v", [D, 256], BF16)
    s_wo = dscr("s_wo", [D, D], BF16)
    s_up = [dscr("s_up%d" % i, [D, DFF], BF16) for i in range(2)]
    s_dn = [dscr("s_dn%d" % i, [DFF, D], BF16) for i in range(2)]
    s_x1 = dscr("s_x1", [ntok // T, 128, 8 * T], F32)
    s_hn = dscr("s_hn", [ntok // T, 128, 8 * T], BF16)
    s_bT = dscr("s_bT", [ntok // T, 3, 64, T], BF16)
    s_cT = dscr("s_cT", [ntok // T, 3, 64, T], BF16)
    s_wfm = dscr("s_wfm", [D, CONVD], BF16)
    s_wtm = dscr("s_wtm", [D, 2560], BF16)
    s_wout = dscr("s_wout", [DIN, D], BF16)
    NCH = ntok // 128
    s_yp = dscr("s_yp", [ntok, DIN], F32)
    s_z = dscr("s_z", [ntok, DIN], F32)
    s_ct = dscr("s_ct", [ntok // T, 128, 4 * T], BF16)
    s_ecb = dscr("s_ecb", [ntok, NH], F32)
    s_sb = dscr("s_sb", [NCH, 128, DIN], F32)
    s_cdb = dscr("s_cdb", [NCH, 128, NH], F32)

    es = ExitStack()

    def sb(name, shape, dt=F32, n=1):
        return TB(es.enter_context(nc.sbuf_tensor(name, list(shape), dt)), n)

    ident = sb("ident", [128, 128])
    identb = sb("identb", [128, 128], BF16)
    ones = sb("ones", [128, 128], BF16)
    mprev = sb("mprev", [128, 128], BF16)
    mnext = sb("mnext", [128, 128], BF16)
    esink = sb("esink", [128, NQH])
    gvec = sb("gvec", [128, 8, 8])
    epsc = sb("epsc", [128, 4])
    tle = sb("tle", [128, 128])
    tge = sb("tge", [128, 128])
    onesf = sb("onesf", [128, 128])
    convw = sb("convw", [128, 24, 5])
    convb = sb("convb", [128, 24])
    dtb_bc = sb("dtb_bc", [128, 64])
    a_bc = sb("a_bc", [128, 64])
    d32 = sb("d32", [128, NH])
    gnw = sb("gnw", [128, 16])
    psum = [TB(es.enter_context(nc.psum_tensor("ps%d" % i, [128, 512], F32))) for i in range(8)]
    for p_ in psum:
        p_.b[0].excl = True
    pools = {"all": list(range(8)), "A": [0, 1, 2, 3, 4, 5], "B": [6, 7]}
    cur_pool = ["all"]
    pctrs = {"all": 0, "A": 0, "B": 0}

    def ps():
        k = cur_pool[0]
        lst = pools[k]
        p = psum[lst[pctrs[k] % len(lst)]]
        pctrs[k] += 1
        return p

    dumped = {}

    def dump(name, ap, shape, dt, bufs):
        if not debug or name in dumped:
            return
        d = nc.dram_tensor("dbg_" + name, list(shape), dt, kind="ExternalOutput").ap()
        dumped[name] = d
        c.dma("sp", sem_dbg, d, ap, reads=bufs)

    sem_dbg = c.new_dma_sem()
    sem_c = c.new_dma_sem()
    sem_ld = [c.new_dma_sem() for _ in range(8)]
    ld_ctr = [0]

    def ldsem():
        ld_ctr[0] += 1
        return sem_ld[ld_ctr[0] % 8]

    sem_st = [c.new_dma_sem() for _ in range(8)]
    st_ctr = [0]

    def stsem():
        st_ctr[0] += 1
        return sem_st[st_ctr[0] % 8]

    blk = es.enter_context(nc.Block())

    c.dma("sp", sem_c, ident.t[:], ident_d, writes=ident.b)
    c.dma("sp", sem_c, mprev.t[:], mprev_d, writes=mprev.b)
    c.dma("sp", sem_c, mnext.t[:], mnext_d, writes=mnext.b)
    c.dma("sp", sem_c, esink.t[:], sink_d.partition_broadcast(128), writes=esink.b)
    c.op("act", lambda e: e.activation(out=esink.t[:], in_=esink.t[:], func=AF.Exp), reads=esink.b, writes=esink.b)
    c.op("dve", lambda e: e.tensor_copy(out=identb.t[:], in_=ident.t[:]), reads=ident.b, writes=identb.b)
    c.op("pool", lambda e: e.memset(ones.t[:], 1.0), writes=ones.b)
    c.op("pool", lambda e: e.memset(epsc.t[:, 0:1], EPS), writes=epsc.b)
    c.op("pool", lambda e: e.memset(epsc.t[:, 1:2], GEPS), writes=epsc.b)
    c.op("pool", lambda e: e.memset(epsc.t[:, 2:3], 1.0), writes=epsc.b)
    c.op("pool", lambda e: e.memset(epsc.t[:, 3:4], 0.0), writes=epsc.b)
    c.op("pool", lambda e: e.memset(onesf.t[:], 1.0), writes=onesf.b)
    c.dma("sp", sem_c, tle.t[:], tle_d, writes=tle.b)
    c.dma("sp", sem_c, tge.t[:], tge_d, writes=tge.b)
    with nc.allow_non_contiguous_dma(reason="tiny param vectors"):
        for k5 in range(5):
            c.dma("sp", sem_c, convw.t[:, :, k5], convw_d[k5].rearrange("(m p) -> p m", p=128), writes=convw.b)
        c.dma("sp", sem_c, convb.t[:], convb_d[0].rearrange("(m p) -> p m", p=128), writes=convb.b)
        c.dma("sp", sem_c, gnw.t[:], gnw_d[0].rearrange("(k p) -> p k", p=128), writes=gnw.b)
    c.dma("sp", sem_c, dtb_bc.t[:], dtb_d.partition_broadcast(128), writes=dtb_bc.b)
    c.dma("sp", sem_c, a_bc.t[:], alog_d.partition_broadcast(128), writes=a_bc.b)
    c.dma("sp", sem_c, d32.t[:], dsk_d.partition_broadcast(128), writes=d32.b)
    c.op("act", lambda e: e.activation(out=a_bc.t[:], in_=a_bc.t[:], func=AF.Exp), reads=a_bc.b, writes=a_bc.b)
    c.op("dve", lambda e: e.tensor_scalar(out=a_bc.t[:], in0=a_bc.t[:], scalar1=-1.0, scalar2=None, op0=ALU.mult), reads=a_bc.b, writes=a_bc.b)
    for li in range(2):
        for wi, src in enumerate((nmpre_d, nmpost_d, nfpre_d, nfpost_d)):
            with nc.allow_non_contiguous_dma(reason="tiny gain vectors"):
                c.dma("sp", sem_c, gvec.t[:, li * 4 + wi, :], src[li].rearrange("(k p) -> p k", p=128), writes=gvec.b)

    with ExitStack() as es0:
        NSTG = 6
        stg = [TB(es0.enter_context(nc.sbuf_tensor("stg%d" % i, [128, 2048], F32))) for i in range(NSTG)]
        stb = [TB(es0.enter_context(nc.sbuf_tensor("stb%d" % i, [128, 2048], BF16))) for i in range(NSTG)]
        cv = [0]

        tasks = []

        def conv_piece(src_ap, ncols, scale_ap, dsts):
            i = cv[0] % NSTG
            cv[0] += 1
            use_act = (cv[0] % 2 == 1)

            def load():
                c.dma("sp", ldsem(), stg[i].t[:, 0:ncols], src_ap, writes=stg[i].b)

            def cast():
                if scale_ap is None:
                    if use_act:
                        c.op("act", lambda e: e.activation(out=stb[i].t[:, 0:ncols], in_=stg[i].t[:, 0:ncols], func=AF.Copy), reads=stg[i].b, writes=stb[i].b)
                    else:
                        c.op("dve", lambda e: e.tensor_copy(out=stb[i].t[:, 0:ncols], in_=stg[i].t[:, 0:ncols]), reads=stg[i].b, writes=stb[i].b)
                else:
                    if use_act:
                        c.op("act", lambda e: e.activation(out=stb[i].t[:, 0:ncols], in_=stg[i].t[:, 0:ncols], func=AF.Copy, scale=scale_ap), reads=stg[i].b + gvec.b + gnw.b, writes=stb[i].b)
                    else:
                        c.op("dve", lambda e: e.tensor_scalar(out=stb[i].t[:, 0:ncols], in0=stg[i].t[:, 0:ncols], scalar1=scale_ap, scalar2=None, op0=ALU.mult), reads=stg[i].b + gvec.b + gnw.b, writes=stb[i].b)

            def store():
                for (dap, c0, n) in dsts:
                    c.dma("sp", stsem(), dap, stb[i].t[:, c0:c0 + n], reads=stb[i].b)
            tasks.append((load, cast, store))

        stq = [TB(es0.enter_context(nc.sbuf_tensor("stq%d" % i, [128, 2816], BF16))) for i in range(2)]

        def conv_perm(src_ap, ncols, scale_ap, pieces, outs, idx):
            i = cv[0] % NSTG
            cv[0] += 1
            q_ = stq[idx % 2]

            def load():
                c.dma("sp", ldsem(), stg[i].t[:, 0:ncols], src_ap, writes=stg[i].b)

            def cast():
                for pi_, (dc, sc, n) in enumerate(pieces):
                    if pi_ % 2 == 0:
                        c.op("act", lambda e: e.activation(out=q_.t[:, dc:dc + n], in_=stg[i].t[:, sc:sc + n], func=AF.Copy, scale=scale_ap), reads=stg[i].b + gvec.b, writes=q_.b)
                    else:
                        c.op("dve", lambda e: e.tensor_scalar(out=q_.t[:, dc:dc + n], in0=stg[i].t[:, sc:sc + n], scalar1=scale_ap, scalar2=None, op0=ALU.mult), reads=stg[i].b + gvec.b, writes=q_.b)

            def store():
                for (dap, c0, n) in outs:
                    c.dma("sp", stsem(), dap, q_.t[:, c0:c0 + n], reads=q_.b)
            tasks.append((load, cast, store))

        class _Col:
            def __init__(self):
                pass

        for k in range(8):
            rows = slice(k * 128, (k + 1) * 128)
            g = gvec.t[:, 0, k:k + 1]
            dsts = []
            for j in range(2):
                for r in range(4):
                    cch = 4 * j + r
                    blkq, pos = cch // 2, cch % 2
                    base = blkq * 512 + pos * 256
                    for e2 in range(2):
                        h = 8 * j + 4 * e2 + r
                        dsts.append((base + e2 * 64, h * 64, 64))
                        dsts.append((base + 128 + e2 * 64, h * 64 + 32, 32))
                        dsts.append((base + 128 + e2 * 64 + 32, h * 64, 32))
            for kc in range(2):
                base = 2048 + kc * 256
                for e2 in range(2):
                    gk = 2 * kc + e2
                    dsts.append((base + e2 * 64, 1024 + gk * 64, 64))
                    dsts.append((base + 128 + e2 * 64, 1024 + gk * 64 + 32, 32))
                    dsts.append((base + 128 + e2 * 64 + 32, 1024 + gk * 64, 32))
            dsts.append((2560, 1280, 256))
            conv_perm(wqkv_d[rows, :], 1536, g, dsts, [(s_qk[rows, :], 0, 2560), (s_v[rows, :], 2560, 256)], k)
            conv_piece(wo_d[rows, :], 1024, None, [(s_wo[rows, :], 0, 1024)])
            for li in range(2 if phases >= 3 else 1):
                g2 = gvec.t[:, li * 4 + 2, k:k + 1]
                for hh in range(2):
                    conv_piece(wup_d[li, rows, hh * 2048:(hh + 1) * 2048], 2048, g2, [(s_up[li][rows, hh * 2048:(hh + 1) * 2048], 0, 2048)])
        for li in range(2 if phases >= 3 else 1):
            for k in range(32):
                rows = slice(k * 128, (k + 1) * 128)
                conv_piece(wdn_d[li, rows, :], 1024, None, [(s_dn[li][rows, :], 0, 1024)])
        if phases >= 2:
            for k in range(8):
                rows = slice(k * 128, (k + 1) * 128)
                g = gvec.t[:, 4, k:k + 1]
                conv_piece(win_d[rows, 0:2048], 2048, g, [(s_wtm[rows, 0:2048], 0, 2048)])
                conv_piece(win_d[rows, 2048:4096], 2048, g, [(s_wfm[rows, 0:2048], 0, 2048)])
                conv_piece(win_d[rows, 4096:5184], 1088, g, [(s_wfm[rows, 2048:3072], 0, 1024), (s_wtm[rows, 2048:2112], 1024, 64)])
            for k in range(16):
                rows = slice(k * 128, (k + 1) * 128)
                conv_piece(wout_d[rows, :], 1024, gnw.t[:, k:k + 1], [(s_wout[rows, :], 0, 1024)])

        DEPTH = NSTG - 1
        for i_ in range(len(tasks) + DEPTH):
            if i_ < len(tasks):
                tasks[i_][0]()
            if i_ - DEPTH >= 0:
                tasks[i_ - DEPTH][1]()
                tasks[i_ - DEPTH][2]()
    c.barrier()
    tiles = []
    off = 0
    for si, L in enumerate(seq_lens):
        nt = L // T
        for ti in range(nt):
            tiles.append(dict(seq=si, pos0=ti * T, row0=off + ti * T, first=(ti == 0), last=(ti == nt - 1), idx=len(tiles)))
        off += L
    NT = len(tiles)

    def wblock(scr, kb, nb, ncols=512):
        return scr[kb * 1024:(kb + 1) * 1024, nb * ncols:(nb + 1) * ncols].rearrange("(kk p) n -> p kk n", p=128), ncols

    def sched_p1():
        for t in range(NT + 1):
            if t < NT:
                for b in range(5):
                    yield wblock(s_qk, 0, b)
                yield wblock(s_v, 0, 0, 256)
            if t >= 1:
                for b in range(2):
                    yield wblock(s_wo, 0, b)
                for b in range(8):
                    yield wblock(s_up[0], 0, b)
                for nb in range(2):
                    for kb in range(4):
                        yield wblock(s_dn[0], kb, nb)

    class WStream:
        def __init__(self, gen, ring):
            self.gen = gen
            self.ring = ring
            self.sem = [c.new_dma_sem() for _ in ring]
            self.issued = 0
            self.used = 0
            self.done = False

        def _issue(self):
            try:
                ap, ncols = next(self.gen)
            except StopIteration:
                self.done = True
                return
            s = self.issued % len(self.ring)
            r = self.ring[s]
            c.dma("sp", self.sem[s], r.t[:, :, 0:ncols], ap, writes=r.b)
            self.issued += 1

        def next(self):
            while not self.done and self.issued < self.used + len(self.ring):
                self._issue()
            r = self.ring[self.used % len(self.ring)]
            self.used += 1
            return r

    def rms_stats(src_fn, nchunk, rstd, dim, eps, sqbuf, sq_eng="act", c0=0, n=T):
        p = ps()
        for k in range(nchunk):
            ap, bufs = src_fn(k)
            c.op("act", lambda e: e.activation(out=sqbuf.t[:, k, c0:c0 + n], in_=ap, func=AF.Square), reads=bufs, writes=[sqbuf.b[k]])
        for k in range(nchunk):
            c.op("pe", lambda e: e.matmul(p.t[:, 0:n], lhsT=ones.t[:], rhs=sqbuf.t[:, k, c0:c0 + n], start=(k == 0), stop=(k == nchunk - 1)),
                 reads=[sqbuf.b[k]] + ones.b, writes=p.b, signal=(k == nchunk - 1))
        c.op("act", lambda e: e.activation(out=rstd.t[:, c0:c0 + n], in_=p.t[:, 0:n], func=AF.Ln, scale=1.0 / dim, bias=epsc.t[:, 0:1] if eps == EPS else epsc.t[:, 1:2]), reads=p.b + epsc.b, writes=rstd.b)
        c.op("act", lambda e: e.activation(out=rstd.t[:, c0:c0 + n], in_=rstd.t[:, c0:c0 + n], func=AF.Exp, scale=-0.5), reads=rstd.b, writes=rstd.b)

    def linear_g(ws, act_fn, kchunks, nblocks, kblocks, epilogue):
        for nb in range(nblocks):
            banks = [ps() for _ in range(4)]
            for kb in range(kblocks):
                w = ws.next()
                for m in range(4):
                    for kk in range(8):
                        kidx = kb * 8 + kk
                        ap, bufs = act_fn(kidx)
                        first = (kb == 0 and kk == 0)
                        last = (kb == kblocks - 1 and kk == 7)
                        c.op("pe", lambda e: e.matmul(banks[m].t[:], lhsT=w.t[:, kk, m * 128:(m + 1) * 128], rhs=ap, start=first, stop=last),
                             reads=w.b + bufs, writes=banks[m].b, signal=(kk == 7))
                if kb < kblocks - 1:
                    yield
            for m in range(4):
                epilogue(nb, m, banks[m])
            yield

    def linear(ws, act_fn, kchunks, nblocks, kblocks, epilogue):
        for _ in linear_g(ws, act_fn, kchunks, nblocks, kblocks, epilogue):
            pass

    def post_norm_residual(xT, asb, sqbuf, rstd, gidx, tmp):
        rms_stats(lambda k: (asb.t[:, k * T:(k + 1) * T], [asb.b[k]]), 8, rstd, D, EPS, sqbuf)
        dump("rstd2", rstd.t[:], [128, T], F32, rstd.b)
        for k in range(8):
            tb = tmp[k % 2]
            c.op("dve", lambda e: e.scalar_tensor_tensor(out=tb.t[:], in0=asb.t[:, k * T:(k + 1) * T], scalar=gvec.t[:, gidx, k:k + 1], in1=rstd.t[:], op0=ALU.mult, op1=ALU.mult),
                 reads=[asb.b[k]] + rstd.b + gvec.b, writes=tb.b)
            c.op("pool" if k % 2 == 0 else "dve", lambda e: e.tensor_tensor(out=xT.t[:, k, :], in0=xT.t[:, k, :], in1=tb.t[:], op=ALU.add), reads=tb.b + [xT.b[k]], writes=[xT.b[k]])

    def pre_norm(xT, hn, sqbuf, rstd):
        rms_stats(lambda k: (xT.t[:, k, :], [xT.b[k]]), 8, rstd, D, EPS, sqbuf)
        for k in range(8):
            eng = "dve" if k % 2 == 0 else "pool"
            c.op(eng, lambda e: e.tensor_tensor(out=hn.t[:, k, :], in0=xT.t[:, k, :], in1=rstd.t[:], op=ALU.mult), reads=[xT.b[k]] + rstd.b, writes=[hn.b[k]])

    def mlp_g(ws, li, xT, hn, h1T, asb, sqbuf, rstd, tmp):
        pre_norm(xT, hn, sqbuf, rstd)
        yield

        def ep_up(nb, m, p):
            j = nb * 4 + m
            tb = tmp[j % 2]
            c.op("act", lambda e: e.activation(out=tb.t[:], in_=p.t[:], func=AF.Square), reads=p.b, writes=tb.b)
            c.op("dve", lambda e: e.scalar_tensor_tensor(out=h1T.t[:, j, :], in0=p.t[:], scalar=0.0, in1=tb.t[:], op0=ALU.is_gt, op1=ALU.mult), reads=p.b + tb.b, writes=[h1T.b[j]])
        yield from linear_g(ws, lambda k: (hn.t[:, k, :], [hn.b[k]]), 8, 8, 1, ep_up)

        def ep_dn(nb, m, p):
            j = nb * 4 + m
            c.op("act", lambda e: e.activation(out=asb.t[:, j * T:(j + 1) * T], in_=p.t[:], func=AF.Copy), reads=p.b, writes=[asb.b[j]])
        yield from linear_g(ws, lambda k: (h1T.t[:, k, :], [h1T.b[k]]), 32, 2, 4, ep_dn)
        post_norm_residual(xT, asb, sqbuf, rstd, li * 4 + 3, tmp)
        yield

    def mlp(ws, li, xT, hn, h1T, asb, sqbuf, rstd, tmp):
        for _ in mlp_g(ws, li, xT, hn, h1T, asb, sqbuf, rstd, tmp):
            pass

    def store_tokmajor(xT, row0, stage):
        for b in range(4):
            for half in range(2):
                p = ps()
                for kk in range(4):
                    k = half * 4 + kk
                    c.op("pe", lambda e: e.transpose(out=p.t[:, kk * 128:(kk + 1) * 128], in_=xT.t[:, k, b * 128:(b + 1) * 128], identity=ident.t[:]),
                         reads=[xT.b[k]] + ident.b, writes=p.b, signal=(kk == 3))
                eng = "act" if half == 0 else "dve"
                if eng == "act":
                    c.op("act", lambda e: e.activation(out=stage.t[:, b, half * 512:(half + 1) * 512], in_=p.t[:], func=AF.Copy), reads=p.b, writes=[stage.b[b]])
                else:
                    c.op("dve", lambda e: e.tensor_copy(out=stage.t[:, b, half * 512:(half + 1) * 512], in_=p.t[:]), reads=p.b, writes=[stage.b[b]])
            c.dma("sp", stsem(), out_d[row0 + b * 128: row0 + (b + 1) * 128, :], stage.t[:, b, :], reads=[stage.b[b]])

    def store_tokmajor_alias(x, row0, xin_):
        for b in range(4):
            for half in range(2):
                p = ps()
                for kk in range(4):
                    k = half * 4 + kk
                    c.op("pe", lambda e: e.transpose(out=p.t[:, kk * 128:(kk + 1) * 128], in_=x.t[:, k, b * 128:(b + 1) * 128], identity=ident.t[:]),
                         reads=[x.b[k]] + ident.b, writes=p.b, signal=(kk == 3))
                bb = xin_.b[2 * b + half]
                if half == 0:
                    c.op("act", lambda e: e.activation(out=xin_.t[:, b * 1024 + half * 512: b * 1024 + (half + 1) * 512], in_=p.t[:], func=AF.Copy), reads=p.b, writes=[bb])
                else:
                    c.op("dve", lambda e: e.tensor_copy(out=xin_.t[:, b * 1024 + half * 512: b * 1024 + (half + 1) * 512], in_=p.t[:]), reads=p.b, writes=[bb])
            c.dma("sp", stsem(), out_d[row0 + b * 128: row0 + (b + 1) * 128, :], xin_.t[:, b * 1024:(b + 1) * 1024], reads=[xin_.b[2 * b], xin_.b[2 * b + 1]])


    with ExitStack() as es1:
        def sb1(name, shape, dt=F32, n=1):
            return TB(es1.enter_context(nc.sbuf_tensor(name, list(shape), dt)), n)
        ring = [sb1("ring%d" % i, [128, 8, 512], BF16) for i in range(NRING)]
        ws = WStream(sched_p1(), ring)
        xin = sb1("xin", [128, 4096], F32, 8)
        asb1 = sb1("asb1", [128, 4096], F32, 8)
        xT = [sb1("xT%d" % i, [128, 8, T], F32, 8) for i in range(2)]
        sqb = sb1("sqb", [128, 8, T], BF16, 8)
        rstd = sb1("rstd", [128, T])
        hn = sb1("hn", [128, 8, T], BF16, 8)
        qT = [sb1("qT%d" % i, [128, 8, T], BF16, 8) for i in range(2)]
        kT = [sb1("kT%d" % i, [128, 2, T], BF16, 2) for i in range(3)]
        Va = [sb1("Va%d" % i, [128, 4, 4, 66], BF16, 4) for i in range(3)]
        cs = sb1("cos", [128, T])
        sn = sb1("sin", [128, T])
        tmp = [sb1("tmp%d" % i, [128, T]) for i in range(2)]
        PT = [[sb1("PT%d_%d" % (i, j), [128, 512], BF16) for j in range(3)] for i in range(2)]
        Osb = sb1("Osb", [128, 1024], BF16)
        den = sb1("den", [128, 8])
        OT = sb1("OT", [128, 8, T], BF16, 8)
        h1T = sb1("h1T", [128, 32, T], BF16, 32)
        for v in Va:
            c.op("pool", lambda e: e.memset(v.t[:], 1.0), writes=v.b)

        def stageA(t):
            tl = tiles[t]
            x = xT[t % 2]
            q = qT[t % 2]
            kk_ = kT[t % 3]
            va = Va[t % 3]
            for b in range(4):
                c.dma("sp", ldsem(), xin.t[:, b * 1024:(b + 1) * 1024], x_d[tl["row0"] + b * 128: tl["row0"] + (b + 1) * 128, :], writes=[xin.b[2 * b], xin.b[2 * b + 1]])
            c.dma("sp", ldsem(), cs.t[:], cos_d[:, tl["pos0"]:tl["pos0"] + T], writes=cs.b)
            c.dma("sp", ldsem(), sn.t[:], sin_d[:, tl["pos0"]:tl["pos0"] + T], writes=sn.b)
            banks = [ps() for _ in range(8)]
            for k in range(8):
                for b in range(4):
                    c.op("pe", lambda e: e.transpose(out=banks[k].t[:, b * 128:(b + 1) * 128], in_=xin.t[:, b * 1024 + k * 128: b * 1024 + (k + 1) * 128], identity=ident.t[:]),
                         reads=[xin.b[2 * b], xin.b[2 * b + 1]] + ident.b, writes=banks[k].b, signal=(b == 3))
                if k % 2 == 0:
                    c.op("act", lambda e: e.activation(out=x.t[:, k, :], in_=banks[k].t[:], func=AF.Copy), reads=banks[k].b, writes=[x.b[k]])
                else:
                    c.op("dve", lambda e: e.tensor_copy(out=x.t[:, k, :], in_=banks[k].t[:]), reads=banks[k].b, writes=[x.b[k]])
            dump("ones", ones.t[:], [128, 128], BF16, ones.b)
            dump("epsc", epsc.t[:], [128, 2], F32, epsc.b)
            dump("gvec", gvec.t[:].rearrange("p a b -> p (a b)"), [128, 64], F32, gvec.b)
            dump("esink", esink.t[:], [128, 16], F32, esink.b)
            dump("xT", x.t[:].rearrange("p k n -> p (k n)"), [128, 8 * T], F32, x.b)
            pre_norm(x, hn, sqb, rstd)
            dump("rstd", rstd.t[:], [128, T], F32, rstd.b)
            dump("hn", hn.t[:].rearrange("p k n -> p (k n)"), [128, 8 * T], BF16, hn.b)

            def ep_qk(nb, m, p, hold={}):
                if m % 2 == 0:
                    hold["p"] = p
                    return
                pq = hold["p"]
                cch = nb * 2 + m // 2
                dst, dbuf = (q.t[:, cch, :], q.b[cch]) if nb < 4 else (kk_.t[:, m // 2, :], kk_.b[m // 2])
                c.op("dve", lambda e: e.tensor_tensor(out=tmp[0].t[:], in0=p.t[:], in1=sn.t[:], op=ALU.mult), reads=p.b + sn.b, writes=tmp[0].b)
                c.op("dve", lambda e: e.tensor_tensor(out=tmp[1].t[:], in0=pq.t[:], in1=cs.t[:], op=ALU.mult), reads=pq.b + cs.b, writes=tmp[1].b)
                c.op("pool", lambda e: e.tensor_tensor(out=dst, in0=tmp[0].t[:], in1=tmp[1].t[:], op=ALU.add), reads=tmp[0].b + tmp[1].b, writes=[dbuf])
            linear(ws, lambda k: (hn.t[:, k, :], [hn.b[k]]), 8, 5, 1, ep_qk)
            dump("qT", q.t[:].rearrange("p k n -> p (k n)"), [128, 8 * T], BF16, q.b)
            dump("kT", kk_.t[:].rearrange("p k n -> p (k n)"), [128, 2 * T], BF16, kk_.b)
            w = ws.next()
            for b in range(4):
                p = ps()
                for k in range(8):
                    c.op("pe", lambda e: e.matmul(p.t[:, 0:256], lhsT=hn.t[:, k, b * 128:(b + 1) * 128], rhs=w.t[:, k, 0:256], start=(k == 0), stop=(k == 7)),
                         reads=[hn.b[k]] + w.b, writes=p.b, signal=(k == 7))
                c.op("act", lambda e: e.activation(out=va.t[:, b, :, 0:64], in_=p.t[:, 0:256].rearrange("p (g d) -> p g d", g=4), func=AF.Copy), reads=p.b, writes=[va.b[b]])

        def stageB(t):
            tl = tiles[t]
            x = xT[t % 2]
            q = qT[t % 2]

            def keyblocks(b):
                kbs = []
                if b > 0:
                    kbs.append((t, b - 1, mprev))
                elif not tl["first"]:
                    kbs.append((t - 1, 3, mprev))
                kbs.append((t, b, None))
                if b < 3:
                    kbs.append((t, b + 1, mnext))
                elif not tl["last"]:
                    kbs.append((t + 1, 0, mnext))
                return kbs

            def part1(b, g):
                kbs = keyblocks(b)
                j, e2 = g // 2, g % 2
                rows = slice(e2 * 64, (e2 + 1) * 64)
                pts = PT[g % 2]
                for ci, (kt, kb, msk) in enumerate(kbs):
                    p = ps()
                    kt_ = kT[kt % 3]
                    c.op("pe", lambda e: e.matmul(p.t[:], lhsT=kt_.t[rows, j, kb * 128:(kb + 1) * 128], rhs=q.t[rows, 4 * j:4 * j + 4, b * 128:(b + 1) * 128], start=True, stop=True),
                         reads=[kt_.b[j]] + q.b[4 * j:4 * j + 4], writes=p.b)
                    c.op("act", lambda e: e.activation(out=pts[ci].t[:], in_=p.t[:], func=AF.Exp, scale=HD ** -0.5), reads=p.b, writes=pts[ci].b)
                    if msk is not None:
                        c.op("pool", lambda e: e.tensor_tensor(out=pts[ci].t[:].rearrange("p (r i) -> p r i", r=4), in0=pts[ci].t[:].rearrange("p (r i) -> p r i", r=4),
                                                               in1=msk.t[:].unsqueeze(1).to_broadcast([128, 4, 128]), op=ALU.mult), reads=pts[ci].b + msk.b, writes=pts[ci].b)

            def part2(b, g):
                kbs = keyblocks(b)
                pts = PT[g % 2]
                po = ps()
                pov = po.t[:, 0:260].rearrange("p (r d) -> p r d", r=4)
                for r in range(4):
                    for ci, (kt, kb, msk) in enumerate(kbs):
                        va = Va[kt % 3]
                        c.op("pe", lambda e: e.matmul(pov[:, r, :], lhsT=pts[ci].t[:, r * 128:(r + 1) * 128], rhs=va.t[:, kb, g, 0:65], start=(ci == 0), stop=(ci == len(kbs) - 1)),
                             reads=pts[ci].b + [va.b[kb]], writes=po.b, signal=(r == 3 and ci == len(kbs) - 1))
                c.op("dve", lambda e: e.tensor_tensor(out=den.t[:, 0:4], in0=pov[:, :, 64], in1=esink.t[:, 4 * g:4 * g + 4], op=ALU.add), reads=po.b + esink.b, writes=den.b)
                c.op("dve", lambda e: e.reciprocal(out=den.t[:, 4:8], in_=den.t[:, 0:4]), reads=den.b, writes=den.b)
                c.op("dve", lambda e: e.tensor_tensor(out=Osb.t[:, g * 256:(g + 1) * 256].rearrange("p (r d) -> p r d", r=4), in0=pov[:, :, 0:64],
                                                     in1=den.t[:, 4:8].unsqueeze(2).to_broadcast([128, 4, 64]), op=ALU.mult), reads=po.b + den.b, writes=Osb.b)

            def part3(b):
                p = ps()
                pb = p.t[:].bitcast(BF16)
                for k in range(8):
                    c.op("pe", lambda e: e.transpose(out=pb[:, k * 128:(k + 1) * 128], in_=Osb.t[:, k * 128:(k + 1) * 128], identity=identb.t[:]),
                         reads=Osb.b + identb.b, writes=p.b, signal=(k == 7))
                c.op("act", lambda e: e.activation(out=OT.t[:, :, b * 128:(b + 1) * 128], in_=pb.rearrange("p (k i) -> p k i", k=8), func=AF.Copy), reads=p.b, writes=OT.b)

            its = [(b, g) for b in range(4) for g in range(4)]
            part1(*its[0])
            for ii, (b, g) in enumerate(its):
                if ii + 1 < len(its):
                    part1(*its[ii + 1])
                part2(b, g)
                if g == 3:
                    part3(b)

            dump("Va", Va[t % 3].t[:].rearrange("p a g d -> p (a g d)"), [128, 4 * 4 * 66], BF16, Va[t % 3].b)
            dump("OT", OT.t[:].rearrange("p k n -> p (k n)"), [128, 8 * T], BF16, OT.b)

            def ep_wo(nb, m, p):
                jj = nb * 4 + m
                c.op("act", lambda e: e.activation(out=asb1.t[:, jj * T:(jj + 1) * T], in_=p.t[:], func=AF.Copy), reads=p.b, writes=[asb1.b[jj]])
            linear(ws, lambda k: (OT.t[:, k, :], OT.b), 8, 2, 1, ep_wo)
            dump("ao", asb1.t[:], [128, 4096], F32, asb1.b)
            post_norm_residual(x, asb1, sqb, rstd, 1, tmp)
            dump("x_mid", x.t[:].rearrange("p k n -> p (k n)"), [128, 8 * T], F32, x.b)
            mlp(ws, 0, x, hn, h1T, asb1, sqb, rstd, tmp)
            if phases == 1:
                store_tokmajor_alias(x, tl["row0"], asb1)
            else:
                c.dma("sp", stsem(), s_x1[t], x.t[:].rearrange("p k n -> p (k n)"), reads=x.b)
                pre_norm(x, hn, sqb, rstd)
                c.dma("sp", stsem(), s_hn[t], hn.t[:].rearrange("p k n -> p (k n)"), reads=hn.b)

        for t in range(NT + 1):
            if t < NT:
                stageA(t)
            if t >= 1:
                stageB(t - 1)

    def sched_p2():
        for t in range(NT):
            for nb in range(6):
                yield wblock(s_wfm, 0, nb)
                if nb < 4:
                    yield wblock(s_wtm, 0, nb)
                elif nb == 4:
                    yield (s_wtm[0:1024, 2048:2112].rearrange("(kk p) n -> p kk n", p=128), 64)

    if phases >= 2:
      c.barrier()
      with ExitStack() as es2:
        def sb2(name, shape, dt=F32, n=1):
            return TB(es2.enter_context(nc.sbuf_tensor(name, list(shape), dt)), n)
        ring = [sb2("ringb%d" % i, [128, 8, 512], BF16) for i in range(3)]
        ws = WStream(sched_p2(), ring)
        hn2 = sb2("hn2", [128, 8, 516], BF16, 8)
        cin = [sb2("cin%d" % i, [128, 516]) for i in range(4)]
        acc = [sb2("acc%d" % i, [128, 512]) for i in range(4)]
        xcT = [sb2("xcT%d" % i, [128, 512], BF16) for i in range(8)]
        x_tm = sb2("x_tm", [128, 4, DIN], BF16, 4)
        BT = sb2("BT", [128, 4, T], BF16, 4)
        B_tm = sb2("B_tm", [128, 4, 512], BF16, 4)
        CT = sb2("CT", [128, 4, T], BF16, 4)
        zst = [sb2("zst%d" % i, [128, 512]) for i in range(2)]
        dtx = sb2("dtx", [128, 4, 64])
        dtt = sb2("dtt", [128, 4, 64])
        dtv = sb2("dtv", [128, 4, 64])
        lndt = sb2("lndt", [128, 4, 64])
        da = sb2("da", [128, 4, 64])
        cumx = sb2("cumx", [128, 4, 64])
        biasx = sb2("biasx", [128, 4, 64])
        ecum = sb2("ecum", [128, 4, 64])
        wst = sb2("wst", [128, 4, 64])
        cdec = sb2("cdec", [128, 4, 64])
        cbm = [sb2("cbm%d" % i, [128, 512]) for i in range(2)]
        Ep = [sb2("Ep%d" % i, [128, 8, 128]) for i in range(2)]
        Mt = [sb2("Mt%d" % i, [128, 8, 128], BF16) for i in range(4)]
        Ysb = sb2("Ysb", [128, DIN], F32, 4)
        t1s = [sb2("t1_%d" % i, [128, 512]) for i in range(2)]
        t1c = [0]
        xw2 = [[sb2("xw%d_%d" % (a, i), [128, DIN], BF16, 4) for i in range(2)] for a in range(2)]
        hstf = sb2("hstf", [128, DIN], F32, 4)
        hstf_bf = sb2("hstf_bf", [128, DIN], BF16, 4)
        Sbsb = sb2("Sbsb", [128, DIN], F32, 4)
        zc = [0]
        itc = [0]
        rowsA = [sb2("rowsA%d" % i, [128, 16, 128], BF16) for i in range(1)]
        rowsB = [sb2("rowsB%d" % i, [128, 16, 128], BF16) for i in range(1)]
        spl = [[sb2("spl%d_%d" % (a, b), [64, T], BF16) for b in range(3)] for a in range(2)]
        for rr in rowsA + rowsB:
            c.op("pool", lambda e: e.memset(rr.t[:], 1.0), writes=rr.b)
        bT = sb2("bT", [64, T])
        cT = sb2("cT", [64, T])
        scrb = Buf()
        Dd = sb2("Dd", [128, NH, 128], BF16)
        c.op("dve", lambda e: e.tensor_tensor(out=Dd.t[:], in0=identb.t[:].unsqueeze(1).to_broadcast([128, NH, 128]), in1=d32.t[:].unsqueeze(2).to_broadcast([128, NH, 128]), op=ALU.mult),
             reads=d32.b + identb.b, writes=Dd.b)

        def chk(n):
            if stop == n:
                raise _Stop()

        def p2_tile(t):
            tl = tiles[t]
            ch0 = tl["row0"] // 128
            def load_hn(tt):
                tl_ = tiles[tt]
                c.dma("sp", ldsem(), hn2.t[:, :, 0:512], s_hn[tt].rearrange("p (k n) -> p k n", k=8), writes=hn2.b)
                with nc.allow_non_contiguous_dma(reason="2-token conv halos"):
                    if tl_["first"]:
                        c.op("pool", lambda e: e.memset(hn2.t[:, :, 512:514], 0.0), writes=hn2.b)
                    else:
                        c.dma("sp", ldsem(), hn2.t[:, :, 512:514], s_hn[tt - 1].rearrange("p (k n) -> p k n", k=8)[:, :, 510:512], writes=hn2.b)
                    if tl_["last"]:
                        c.op("pool", lambda e: e.memset(hn2.t[:, :, 514:516], 0.0), writes=hn2.b)
                    else:
                        c.dma("sp", ldsem(), hn2.t[:, :, 514:516], s_hn[tt + 1].rearrange("p (k n) -> p k n", k=8)[:, :, 0:2], writes=hn2.b)

            if t == 0:
                load_hn(0)
            if tl["first"]:
                c.op("pool", lambda e: e.memset(hstf.t[:], 0.0), writes=hstf.b)
                c.op("pool", lambda e: e.memset(hstf_bf.t[:], 0.0), writes=hstf_bf.b)
            chk(1)
            if t == 1:
                chk(12)
            def tm_block(nb):
                w = ws.next()
                ncol = 512 if nb < 4 else 64
                for j in range(4):
                    p = ps()
                    for kk in range(8):
                        c.op("pe", lambda e: e.matmul(p.t[:, 0:ncol], lhsT=hn2.t[:, kk, j * 128:(j + 1) * 128], rhs=w.t[:, kk, 0:ncol], start=(kk == 0), stop=(kk == 7)),
                             reads=w.b + [hn2.b[kk]], writes=p.b, signal=(kk == 7))
                    if nb < 4:
                        zb = zst[zc[0] % 2]
                        zc[0] += 1
                        if zc[0] % 2 == 0:
                            c.op("act", lambda e: e.activation(out=zb.t[:], in_=p.t[:], func=AF.Copy), reads=p.b, writes=zb.b)
                        else:
                            c.op("dve", lambda e: e.tensor_copy(out=zb.t[:], in_=p.t[:]), reads=p.b, writes=zb.b)
                        r0 = tl["row0"] + j * 128
                        c.dma("sp", stsem(), s_z[r0:r0 + 128, nb * 512:(nb + 1) * 512], zb.t[:], reads=zb.b)
                    else:
                        c.op("dve", lambda e: e.tensor_tensor(out=dtx.t[:, j, :], in0=p.t[:, 0:64], in1=dtb_bc.t[:], op=ALU.add), reads=p.b + dtb_bc.b, writes=dtx.b)

            pending_T = []
            for nb in range(6):
                w = ws.next()
                banks = [ps() for _ in range(4)]
                ph = ps()
                for m in range(4):
                    for kk in range(8):
                        c.op("pe", lambda e: e.matmul(banks[m].t[:], lhsT=w.t[:, kk, m * 128:(m + 1) * 128], rhs=hn2.t[:, kk, 0:512], start=(kk == 0), stop=(kk == 7)),
                             reads=w.b + [hn2.b[kk]], writes=banks[m].b, signal=(kk == 7))
                    for kk in range(8):
                        c.op("pe", lambda e: e.matmul(ph.t[:, m * 4:m * 4 + 4], lhsT=w.t[:, kk, m * 128:(m + 1) * 128], rhs=hn2.t[:, kk, 512:516], start=(kk == 0), stop=(kk == 7)),
                             reads=w.b + [hn2.b[kk]], writes=ph.b, signal=(kk == 7))
                mcs = [nb * 4 + m for m in range(4)]
                for m in range(4):
                    c.op("act", lambda e: e.activation(out=cin[m].t[:, 2:514], in_=banks[m].t[:], func=AF.Copy), reads=banks[m].b, writes=cin[m].b)
                for m in range(4):
                    c.op("dve", lambda e: e.tensor_copy(out=cin[m].t[:, 0:2], in_=ph.t[:, m * 4:m * 4 + 2]), reads=ph.b, writes=cin[m].b)
                    c.op("dve", lambda e: e.tensor_copy(out=cin[m].t[:, 514:516], in_=ph.t[:, m * 4 + 2:m * 4 + 4]), reads=ph.b, writes=cin[m].b)
                if nb < 5:
                    tm_block(nb)
                while pending_T:
                    pending_T.pop(0)()
                for m in range(4):
                    mc = mcs[m]
                    c.op("act", lambda e: e.activation(out=acc[m].t[:], in_=cin[m].t[:, 0:512], func=AF.Identity, scale=convw.t[:, mc, 0:1], bias=convb.t[:, mc:mc + 1]),
                         reads=cin[m].b + convw.b + convb.b, writes=acc[m].b)
                for k5 in range(1, 5):
                    for m in range(4):
                        mc = mcs[m]
                        c.op("dve", lambda e: e.scalar_tensor_tensor(out=acc[m].t[:], in0=cin[m].t[:, k5:k5 + 512], scalar=convw.t[:, mc, k5:k5 + 1], in1=acc[m].t[:], op0=ALU.mult, op1=ALU.add),
                             reads=cin[m].b + acc[m].b + convw.b, writes=acc[m].b)
                dsts = []
                for m in range(4):
                    mc = mcs[m]
                    if mc < 16:
                        dst, dbufs = xcT[(nb % 2) * 4 + m].t[:], xcT[(nb % 2) * 4 + m].b
                    elif mc < 20:
                        dst, dbufs = BT.t[:, mc - 16, :], [BT.b[mc - 16]]
                    else:
                        dst, dbufs = CT.t[:, mc - 20, :], [CT.b[mc - 20]]
                    dsts.append((dst, dbufs))
                    c.op("act", lambda e: e.activation(out=dst, in_=acc[m].t[:], func=AF.Silu), reads=acc[m].b, writes=dbufs)
                if nb < 5:
                    def do_T(dsts=dsts, mcs=mcs):
                        pps = []
                        for m in range(4):
                            dst, dbufs = dsts[m]
                            p = ps()
                            pps.append(p)
                            pb = p.t[:].bitcast(BF16)
                            for j in range(4):
                                c.op("pe", lambda e: e.transpose(out=pb[:, j * 128:(j + 1) * 128], in_=dst[:, j * 128:(j + 1) * 128], identity=identb.t[:]),
                                     reads=dbufs + identb.b, writes=p.b, signal=(j == 3))
                        for m in range(4):
                            mc = mcs[m]
                            pb = pps[m].t[:].bitcast(BF16)
                            if mc < 16:
                                c.op("act", lambda e: e.activation(out=x_tm.t[:, :, mc * 128:(mc + 1) * 128], in_=pb[:, 0:512].rearrange("p (j i) -> p j i", j=4), func=AF.Copy), reads=pps[m].b, writes=x_tm.b)
                            else:
                                g = mc - 16
                                c.op("dve", lambda e: e.tensor_copy(out=B_tm.t[:, :, g * 128:(g + 1) * 128], in_=pb[:, 0:512].rearrange("p (j i) -> p j i", j=4)), reads=pps[m].b, writes=B_tm.b)
                    pending_T.append(do_T)
            while pending_T:
                pending_T.pop(0)()
            if t + 1 < NT:
                load_hn(t + 1)
            chk(2)
            if t == 1:
                chk(14)
            c.dma("sp", stsem(), s_ct[t], CT.t[:].rearrange("p g n -> p (g n)"), reads=CT.b)
            chk(3)
            if t == 1:
                chk(15)
            c.op("act", lambda e: e.activation(out=dtt.t[:], in_=dtx.t[:], func=AF.Abs), reads=dtx.b, writes=dtt.b)
            c.op("act", lambda e: e.activation(out=dtt.t[:], in_=dtt.t[:], func=AF.Exp, scale=-1.0), reads=dtt.b, writes=dtt.b)
            c.op("act", lambda e: e.activation(out=dtt.t[:], in_=dtt.t[:], func=AF.Ln, bias=epsc.t[:, 2:3]), reads=dtt.b + epsc.b, writes=dtt.b)
            c.op("dve", lambda e: e.scalar_tensor_tensor(out=dtv.t[:], in0=dtx.t[:], scalar=0.0, in1=dtt.t[:], op0=ALU.max, op1=ALU.add), reads=dtx.b + dtt.b, writes=dtv.b)
            c.op("act", lambda e: e.activation(out=lndt.t[:], in_=dtv.t[:], func=AF.Ln), reads=dtv.b, writes=lndt.b)
            c.op("dve", lambda e: e.tensor_tensor(out=da.t[:], in0=dtv.t[:], in1=a_bc.t[:].unsqueeze(1).to_broadcast([128, 4, 64]), op=ALU.mult), reads=dtv.b + a_bc.b, writes=da.b)
            pd = ps()
            pdv = pd.t[:].rearrange("p (j c) -> p j c", j=4)
            for j in range(4):
                c.op("pe", lambda e: e.matmul(pdv[:, j, 0:32], lhsT=tle.t[:], rhs=da.t[:, j, 0:32], start=True, stop=True), reads=tle.b + da.b, writes=pd.b, signal=False)
                c.op("pe", lambda e: e.matmul(pdv[:, j, 32:64], lhsT=tge.t[:], rhs=da.t[:, j, 32:64], start=True, stop=True), reads=tge.b + da.b, writes=pd.b, signal=False)
                c.op("pe", lambda e: e.matmul(pdv[:, j, 64:128], lhsT=onesf.t[:], rhs=da.t[:, j, 0:64], start=True, stop=True), reads=onesf.b + da.b, writes=pd.b, signal=(j == 3))
            c.op("act", lambda e: e.activation(out=cumx.t[:], in_=pdv[:, :, 0:64], func=AF.Copy), reads=pd.b, writes=cumx.b)
            c.op("dve", lambda e: e.tensor_tensor(out=biasx.t[:], in0=lndt.t[:], in1=pdv[:, :, 0:64], op=ALU.subtract), reads=lndt.b + pd.b, writes=biasx.b)
            c.op("act", lambda e: e.activation(out=ecum.t[:], in_=pdv[:, :, 0:64], func=AF.Exp), reads=pd.b, writes=ecum.b)
            c.op("dve", lambda e: e.tensor_tensor(out=wst.t[:], in0=biasx.t[:], in1=pdv[:, :, 64:128], op=ALU.add), reads=biasx.b + pd.b, writes=wst.b)
            c.op("act", lambda e: e.activation(out=wst.t[:], in_=wst.t[:], func=AF.Exp), reads=wst.b, writes=wst.b)
            c.op("act", lambda e: e.activation(out=cdec.t[:], in_=pdv[:, :, 64:128], func=AF.Exp), reads=pd.b, writes=cdec.b)
            for qi_, (src_, dst_, scr_) in enumerate(((biasx, bT, s_bT), (cumx, cT, s_cT))):
                pt_ = ps()
                for j in range(4):
                    c.op("pe", lambda e: e.transpose(out=pt_.t[0:64, j * 128:(j + 1) * 128], in_=src_.t[:, j, :], identity=ident.t[:]),
                         reads=src_.b + ident.b, writes=pt_.b, signal=(j == 3))
                c.op("dve", lambda e: e.tensor_copy(out=dst_.t[:], in_=pt_.t[0:64, :]), reads=pt_.b, writes=dst_.b)
                for part in range(3):
                    sp_ = spl[qi_][part]
                    c.op("dve", lambda e: e.tensor_copy(out=sp_.t[:], in_=dst_.t[:]), reads=dst_.b, writes=sp_.b)
                    if part < 2:
                        c.op("dve", lambda e: e.tensor_tensor(out=dst_.t[:], in0=dst_.t[:], in1=sp_.t[:], op=ALU.subtract), reads=dst_.b + sp_.b, writes=dst_.b)
                    c.dma("sp", stsem(), scr_[t, part], sp_.t[:], reads=sp_.b, writes=[scrb])
            chk(4)
            r0t = tl["row0"]
            with nc.allow_non_contiguous_dma(reason="small per-token rows"):
                for j in range(4):
                    c.dma("sp", stsem(), s_ecb[r0t + j * 128:r0t + (j + 1) * 128, :], ecum.t[:, j, 32:64], reads=ecum.b)
                    c.dma("sp", stsem(), s_cdb[ch0 + j], cdec.t[:, j, 32:64], reads=cdec.b)
            chk(5)
            if t == 1:
                chk(16)
            for j in range(4):
                cols = slice(j * 128, (j + 1) * 128)
                r0 = tl["row0"] + j * 128
                pcb = psum[6]
                for g in range(4):
                    c.op("pe", lambda e: e.matmul(pcb.t[:, g * 128:(g + 1) * 128], lhsT=BT.t[:, g, cols], rhs=CT.t[:, g, cols], start=True, stop=True),
                         reads=[BT.b[g], CT.b[g]], writes=pcb.b, signal=(g == 3))
                c.op("dve", lambda e: e.tensor_tensor(out=cbm[0].t[:].rearrange("p (g l) -> p g l", g=4), in0=pcb.t[:].rearrange("p (g l) -> p g l", g=4),
                                                     in1=tle.t[:].unsqueeze(1).to_broadcast([128, 4, 128]), op=ALU.mult), reads=pcb.b + tle.b, writes=cbm[0].b)
                c.op("dve", lambda e: e.tensor_tensor(out=cbm[1].t[:].rearrange("p (g l) -> p g l", g=4), in0=pcb.t[:].rearrange("p (g l) -> p g l", g=4),
                                                     in1=tge.t[:].unsqueeze(1).to_broadcast([128, 4, 128]), op=ALU.mult), reads=pcb.b + tge.b, writes=cbm[1].b)
                xw = xw2[j % 2]
                for d in range(2):
                    xwd = xw[d]
                    for g in range(4):
                        c.op("pool", lambda e: e.tensor_tensor(out=xwd.t[:, g * 512:(g + 1) * 512].rearrange("p (h d) -> p h d", h=8), in0=x_tm.t[:, j, g * 512:(g + 1) * 512].rearrange("p (h d) -> p h d", h=8),
                                                              in1=wst.t[:, j, d * 32 + g * 8:d * 32 + (g + 1) * 8].unsqueeze(2).to_broadcast([128, 8, 64]), op=ALU.mult),
                             reads=[x_tm.b[j]] + wst.b, writes=[xwd.b[g]])
                chk(6)
                rA, rB = rowsA[0], rowsB[0]
                for q in range(4):
                    c.dma("sp", ldsem(), rA.t[32 * q + 3:32 * q + 6, :, :], s_bT[t][:, 16 * q:16 * q + 16, cols], reads=[scrb], writes=rA.b)
                    c.dma("sp", ldsem(), rB.t[32 * q:32 * q + 3, :, :], s_cT[t][:, 16 * q:16 * q + 16, cols], reads=[scrb], writes=rB.b)

                def emit_T(g):
                    info = []
                    for d in range(2):
                        it = itc[0]
                        itc[0] += 1
                        rb = [psum[4 + 2 * (it % 2)], psum[5 + 2 * (it % 2)]]
                        for hh in range(8):
                            col = d * 32 + g * 8 + hh
                            q, h16 = col // 16, col % 16
                            pr = rb[hh // 4]
                            c.op("pe", lambda e: e.matmul(pr.t[:, (hh % 4) * 128:(hh % 4 + 1) * 128], lhsT=rA.t[32 * q:32 * q + 6, h16, :], rhs=rB.t[32 * q:32 * q + 6, h16, :],
                                                          start=True, stop=True, tile_position=(32 * q, 0)),
                                 reads=rA.b + rB.b, writes=pr.b, signal=(hh % 4 == 3))
                        info.append((it, rb))
                    return info

                nxt = emit_T(0)
                for g in range(4):
                    cur = nxt
                    mts = []
                    for d in range(2):
                        it, rb = cur[d]
                        ep = Ep[it % 2]
                        mt = Mt[it % 4]
                        mts.append(mt)
                        for hb in range(2):
                            pr = rb[hb]
                            c.op("act", lambda e: e.activation(out=ep.t[:, hb * 4:(hb + 1) * 4, :], in_=pr.t[:].rearrange("p (h l) -> p h l", h=4), func=AF.Exp), reads=pr.b, writes=ep.b)
                        c.op("dve", lambda e: e.scalar_tensor_tensor(out=mt.t[:], in0=ep.t[:], scalar=1e30, in1=cbm[d].t[:, g * 128:(g + 1) * 128].unsqueeze(1).to_broadcast([128, 8, 128]),
                                                                    op0=ALU.min, op1=ALU.mult), reads=ep.b + cbm[d].b, writes=mt.b)
                    if g < 3:
                        nxt = emit_T(g + 1)
                    for hh in range(8):
                        h = g * 8 + hh
                        xs_ = x_tm.t[:, j, h * 64:(h + 1) * 64]
                        c.op("pe", lambda e: e.matmul(psum[g].t[:, hh * 64:(hh + 1) * 64], lhsT=mts[0].t[:, hh, :], rhs=xs_, start=True, stop=False),
                             reads=mts[0].b + [x_tm.b[j]], writes=psum[g].b, signal=False)
                        c.op("pe", lambda e: e.matmul(psum[g].t[:, hh * 64:(hh + 1) * 64], lhsT=mts[1].t[:, hh, :], rhs=xs_, start=False, stop=False),
                             reads=mts[1].b + [x_tm.b[j]], writes=psum[g].b, signal=False)
                        c.op("pe", lambda e: e.matmul(psum[g].t[:, hh * 64:(hh + 1) * 64], lhsT=Dd.t[:, h, :], rhs=xs_, start=False, stop=True),
                             reads=Dd.b + [x_tm.b[j]], writes=psum[g].b, signal=(hh == 7))
                chk(7)
                for g in range(4):
                    c.op("pe", lambda e: e.matmul(psum[4 + g].t[:], lhsT=CT.t[:, g, cols], rhs=hstf_bf.t[:, g * 512:(g + 1) * 512], start=True, stop=True),
                         reads=[CT.b[g], hstf_bf.b[g]], writes=psum[4 + g].b)
                for g in range(4):
                    t1 = t1s[t1c[0] % 2]
                    t1c[0] += 1
                    c.op("dve", lambda e: e.tensor_tensor(out=t1.t[:].rearrange("p (h d) -> p h d", h=8), in0=psum[4 + g].t[:].rearrange("p (h d) -> p h d", h=8),
                                                         in1=ecum.t[:, j, g * 8:(g + 1) * 8].unsqueeze(2).to_broadcast([128, 8, 64]), op=ALU.mult), reads=psum[4 + g].b + ecum.b, writes=t1.b)
                    c.op("dve", lambda e: e.tensor_tensor(out=Ysb.t[:, g * 512:(g + 1) * 512], in0=psum[g].t[:], in1=t1.t[:], op=ALU.add),
                         reads=t1.b + psum[g].b, writes=[Ysb.b[g]])
                c.dma("sp", stsem(), s_yp[r0:r0 + 128, :], Ysb.t[:], reads=Ysb.b)
                chk(8)
                for g in range(4):
                    c.op("pe", lambda e: e.matmul(psum[g].t[:], lhsT=B_tm.t[:, j, g * 128:(g + 1) * 128], rhs=xw[0].t[:, g * 512:(g + 1) * 512], start=True, stop=True),
                         reads=[B_tm.b[j], xw[0].b[g]], writes=psum[g].b)
                for g in range(4):
                    c.op("dve", lambda e: e.tensor_tensor(out=hstf.t[:, g * 512:(g + 1) * 512].rearrange("p (h d) -> p h d", h=8), in0=hstf.t[:, g * 512:(g + 1) * 512].rearrange("p (h d) -> p h d", h=8),
                                                         in1=cdec.t[:, j, g * 8:(g + 1) * 8].unsqueeze(2).to_broadcast([128, 8, 64]), op=ALU.mult), reads=[hstf.b[g]] + cdec.b, writes=[hstf.b[g]])
                for g in range(4):
                    c.op("dve", lambda e: e.tensor_tensor(out=hstf.t[:, g * 512:(g + 1) * 512], in0=hstf.t[:, g * 512:(g + 1) * 512], in1=psum[g].t[:], op=ALU.add),
                         reads=[hstf.b[g]] + psum[g].b, writes=[hstf.b[g]])
                for g in range(4):
                    c.op("act", lambda e: e.activation(out=hstf_bf.t[:, g * 512:(g + 1) * 512], in_=hstf.t[:, g * 512:(g + 1) * 512], func=AF.Copy), reads=[hstf.b[g]], writes=[hstf_bf.b[g]])
                for g in range(4):
                    c.op("pe", lambda e: e.matmul(psum[g].t[:], lhsT=B_tm.t[:, j, g * 128:(g + 1) * 128], rhs=xw[1].t[:, g * 512:(g + 1) * 512], start=True, stop=True),
                         reads=[B_tm.b[j], xw[1].b[g]], writes=psum[g].b)
                for g in range(4):
                    c.op("act", lambda e: e.activation(out=Sbsb.t[:, g * 512:(g + 1) * 512], in_=psum[g].t[:], func=AF.Copy), reads=psum[g].b, writes=[Sbsb.b[g]])
                c.dma("sp", stsem(), s_sb[ch0 + j], Sbsb.t[:], reads=Sbsb.b)
                chk(10)
                if j == 3:
                    chk(11)

        try:
            for t in range(NT):
                p2_tile(t)
        except _Stop:
            pass

    def sched_p3():
        for t in range(NT):
            for nb in range(2):
                for kb in range(2):
                    yield wblock(s_wout, kb, nb)
            for b in range(8):
                yield wblock(s_up[1], 0, b)
            for nb in range(2):
                for kb in range(4):
                    yield wblock(s_dn[1], kb, nb)

    if phases >= 3:
      c.barrier()
      with ExitStack() as es3:
        def sb3(name, shape, dt=F32, n=1):
            return TB(es3.enter_context(nc.sbuf_tensor(name, list(shape), dt)), n)
        ring = [sb3("ringc%d" % i, [128, 8, 512], BF16) for i in range(NRING)]
        ws = WStream(sched_p3(), ring)
        x1T = sb3("x1T", [128, 8, T], F32, 8)
        CT3 = sb3("CT3", [128, 4, T], BF16)
        yps = [sb3("yp%d" % i, [128, DIN], F32, 4) for i in range(2)]
        zzs = [sb3("zz%d" % i, [128, DIN], F32, 4) for i in range(2)]
        ecbs = [sb3("ecb%d" % i, [128, NH]) for i in range(2)]
        Sb3s = [sb3("Sb3_%d" % i, [128, DIN], F32, 4) for i in range(2)]
        cdbs = [sb3("cdb%d" % i, [128, NH]) for i in range(2)]
        hstb = sb3("hstb", [128, DIN], F32, 4)
        hstb_bf = sb3("hstb_bf", [128, DIN], BF16, 4)
        t3s = [sb3("t3_%d" % i, [128, 512]) for i in range(2)]
        ss = sb3("ss", [128, 8])
        mhalf = sb3("mhalf", [128, 1])
        c.op("pool", lambda e: e.memset(mhalf.t[:], -0.5), writes=mhalf.b)
        Gn = sb3("Gn", [128, DIN], BF16)
        gT = sb3("gT", [128, 16, T], BF16, 16)
        asb3 = sb3("asb3", [128, 4096], F32, 8)
        rstd3 = sb3("rstd3", [128, T])
        hn3 = sb3("hn3", [128, 8, T], BF16, 8)
        h1T3 = sb3("h1T3", [128, 32, T], BF16, 32)
        sqb3 = TB(h1T3.t[:, 0:8, :], 0)
        sqb3.b = h1T3.b[0:8]
        tmp3 = [sb3("tmp3_%d" % i, [128, T]) for i in range(2)]
        order = []
        for si in range(len(seq_lens)):
            ts_ = [tt for tt in range(NT) if tiles[tt]["seq"] == si]
            order += ts_[::-1]
        chunks = [(t, j) for t in order for j in (3, 2, 1, 0)]

        def prefetch(i):
            if i >= len(chunks):
                return
            t, j = chunks[i]
            tl = tiles[t]
            r0 = tl["row0"] + j * 128
            chn = tl["row0"] // 128 + j
            pi = i % 2
            c.dma("sp", ldsem(), yps[pi].t[:], s_yp[r0:r0 + 128, :], writes=yps[pi].b)
            c.dma("sp", ldsem(), zzs[pi].t[:], s_z[r0:r0 + 128, :], writes=zzs[pi].b)
            c.dma("sp", ldsem(), Sb3s[pi].t[:], s_sb[chn], writes=Sb3s[pi].b)
            with nc.allow_non_contiguous_dma(reason="small per-token rows"):
                c.dma("sp", ldsem(), ecbs[pi].t[:], s_ecb[r0:r0 + 128, :], writes=ecbs[pi].b)
                c.dma("sp", ldsem(), cdbs[pi].t[:], s_cdb[chn], writes=cdbs[pi].b)

        prefetch(0)
        ci = [0]
        t3c = [0]

        def CL(t):
            tl = tiles[t]
            c.dma("sp", ldsem(), CT3.t[:], s_ct[t].rearrange("p (g n) -> p g n", g=4), writes=CT3.b)
            if tl["last"]:
                c.op("pool", lambda e: e.memset(hstb.t[:], 0.0), writes=hstb.b)
                c.op("pool", lambda e: e.memset(hstb_bf.t[:], 0.0), writes=hstb_bf.b)
            for j in (3, 2, 1, 0):
                cols = slice(j * 128, (j + 1) * 128)
                i = ci[0]
                ci[0] += 1
                assert chunks[i] == (t, j)
                prefetch(i + 1)
                yp, zz, ecb, Sb3, cdb = yps[i % 2], zzs[i % 2], ecbs[i % 2], Sb3s[i % 2], cdbs[i % 2]
                G4 = [slice(g * 512, (g + 1) * 512) for g in range(4)]
                for g in range(4):
                    t3 = t3s[t3c[0] % 2]
                    t3c[0] += 1
                    py = ps()
                    c.op("pe", lambda e: e.matmul(py.t[:], lhsT=CT3.t[:, g, cols], rhs=hstb_bf.t[:, G4[g]], start=True, stop=True), reads=CT3.b + [hstb_bf.b[g]], writes=py.b)
                    c.op("dve", lambda e: e.tensor_tensor(out=t3.t[:].rearrange("p (h d) -> p h d", h=8), in0=py.t[:].rearrange("p (h d) -> p h d", h=8),
                                                         in1=ecb.t[:, g * 8:(g + 1) * 8].unsqueeze(2).to_broadcast([128, 8, 64]), op=ALU.mult), reads=py.b + ecb.b, writes=t3.b)
                    c.op("pool", lambda e: e.tensor_tensor(out=yp.t[:, G4[g]], in0=yp.t[:, G4[g]], in1=t3.t[:], op=ALU.add), reads=t3.b + [yp.b[g]], writes=[yp.b[g]])
                yield
                for g in range(4):
                    c.op("dve", lambda e: e.tensor_tensor(out=hstb.t[:, G4[g]].rearrange("p (h d) -> p h d", h=8), in0=hstb.t[:, G4[g]].rearrange("p (h d) -> p h d", h=8),
                                                         in1=cdb.t[:, g * 8:(g + 1) * 8].unsqueeze(2).to_broadcast([128, 8, 64]), op=ALU.mult), reads=[hstb.b[g]] + cdb.b, writes=[hstb.b[g]])
                for g in range(4):
                    eng = "pool" if g < 2 else "dve"
                    c.op(eng, lambda e: e.tensor_tensor(out=hstb.t[:, G4[g]], in0=hstb.t[:, G4[g]], in1=Sb3.t[:, G4[g]], op=ALU.add), reads=[hstb.b[g], Sb3.b[g]], writes=[hstb.b[g]])
                for g in range(4):
                    c.op("pool", lambda e: e.tensor_copy(out=hstb_bf.t[:, G4[g]], in_=hstb.t[:, G4[g]]), reads=[hstb.b[g]], writes=[hstb_bf.b[g]])
                yield
                for g in range(4):
                    c.op("act", lambda e: e.activation(out=zz.t[:, G4[g]], in_=zz.t[:, G4[g]], func=AF.Silu), reads=[zz.b[g]], writes=[zz.b[g]])
                for g in range(4):
                    c.op("dve", lambda e: e.tensor_tensor(out=yp.t[:, G4[g]], in0=yp.t[:, G4[g]], in1=zz.t[:, G4[g]], op=ALU.mult), reads=[yp.b[g], zz.b[g]], writes=[yp.b[g]])
                for g in range(4):
                    c.op("act", lambda e: e.activation(out=zz.t[:, G4[g]], in_=yp.t[:, G4[g]], func=AF.Square, accum_out=ss.t[:, g:g + 1]), reads=[yp.b[g]], writes=[zz.b[g]] + ss.b)
                yield
                c.op("dve", lambda e: e.tensor_reduce(out=ss.t[:, 4:5], in_=ss.t[:, 0:4], op=ALU.add, axis=mybir.AxisListType.X), reads=ss.b, writes=ss.b)
                c.op("pool", lambda e: e.tensor_scalar(out=ss.t[:, 5:6], in0=ss.t[:, 4:5], scalar1=1.0 / DIN, scalar2=GEPS, op0=ALU.mult, op1=ALU.add), reads=ss.b, writes=ss.b)
                c.op("pool", lambda e: e.tensor_tensor(out=ss.t[:, 4:5], in0=ss.t[:, 5:6], in1=mhalf.t[:, 0:1], op=ALU.pow), reads=ss.b + mhalf.b, writes=ss.b)
                c.op("dve", lambda e: e.tensor_scalar(out=Gn.t[:], in0=yp.t[:], scalar1=ss.t[:, 4:5], scalar2=None, op0=ALU.mult), reads=yp.b + ss.b, writes=Gn.b)
                yield
                for half in range(2):
                    p = ps()
                    pb = p.t[:].bitcast(BF16)
                    for kk in range(8):
                        k = half * 8 + kk
                        c.op("pe", lambda e: e.transpose(out=pb[:, kk * 128:(kk + 1) * 128], in_=Gn.t[:, k * 128:(k + 1) * 128], identity=identb.t[:]),
                             reads=Gn.b + identb.b, writes=p.b, signal=(kk == 7))
                    c.op("act", lambda e: e.activation(out=gT.t[:, half * 8:(half + 1) * 8, cols], in_=pb.rearrange("p (k i) -> p k i", k=8), func=AF.Copy), reads=p.b, writes=gT.b[half * 8:(half + 1) * 8])
                yield

        def HEAD(t):
            tl = tiles[t]
            c.dma("sp", ldsem(), x1T.t[:], s_x1[t].rearrange("p (k n) -> p k n", k=8), writes=x1T.b)

            def ep_out(nb, m, p):
                jj = nb * 4 + m
                c.op("act", lambda e: e.activation(out=asb3.t[:, jj * T:(jj + 1) * T], in_=p.t[:], func=AF.Copy), reads=p.b, writes=[asb3.b[jj]])
            yield from linear_g(ws, lambda k: (gT.t[:, k, :], [gT.b[k]]), 16, 2, 2, ep_out)
            post_norm_residual(x1T, asb3, sqb3, rstd3, 5, tmp3)
            yield
            yield from mlp_g(ws, 1, x1T, hn3, h1T3, asb3, sqb3, rstd3, tmp3)
            store_tokmajor_alias(x1T, tl["row0"], asb3)
            yield

        def drive(gh, gc, lead):
            nh = 0
            dh = gh is None
            dc = gc is None
            while not (dh and dc):
                if not dh:
                    cur_pool[0] = "A"
                    try:
                        next(gh)
                    except StopIteration:
                        dh = True
                    nh += 1
                if not dc and (dh or nh > lead):
                    cur_pool[0] = "B"
                    try:
                        next(gc)
                    except StopIteration:
                        dc = True
            cur_pool[0] = "all"

        drive(None, CL(order[0]), 0)
        for oi, t in enumerate(order):
            nxt = CL(order[oi + 1]) if oi + 1 < len(order) else None
            drive(HEAD(t), nxt, 4)

    c.final_wait("sp")
    es.close()
    return nc, c


WNAMES = ["attn_w_qkv", "attn_w_o", "attn_sink", "ssm_w_in", "ssm_conv_w", "ssm_conv_b", "ssm_dt_bias", "ssm_a_log",
          "ssm_d", "ssm_norm_w", "ssm_w_out", "norm_mix_pre", "norm_mix_post", "norm_ffn_pre", "norm_ffn_post",
          "mlp_w_up", "mlp_w_down"]


def _wmap(inputs, lmax):
    m = {}
    f = lambda a: np.ascontiguousarray(np.asarray(a, dtype=np.float32))
    m["attn_w_qkv"] = f(inputs["attn_w_qkv"])[0]
    m["attn_w_o"] = f(inputs["attn_w_o"])[0]
    m["attn_sink"] = f(inputs["attn_sink"]).reshape(1, NQH)
    m["ssm_w_in"] = f(inputs["ssm_w_in"])[0]
    m["ssm_conv_w"] = f(inputs["ssm_conv_w"])[0]
    m["ssm_conv_b"] = f(inputs["ssm_conv_b"]).reshape(1, CONVD)
    m["ssm_dt_bias"] = f(inputs["ssm_dt_bias"]).reshape(1, 64)
    m["ssm_a_log"] = f(inputs["ssm_a_log"]).reshape(1, 64)
    m["ssm_d"] = f(inputs["ssm_d"]).reshape(1, NH)
    m["ssm_norm_w"] = f(inputs["ssm_norm_w"]).reshape(1, DIN)
    m["ssm_w_out"] = f(inputs["ssm_w_out"])[0]
    for n in ("norm_mix_pre", "norm_mix_post", "norm_ffn_pre", "norm_ffn_post", "mlp_w_up", "mlp_w_down"):
        m[n] = f(inputs[n])
    m.update(_consts(lmax))
    return m


def kernel(**inputs):
    xp = np.asarray(inputs["x_prompt"], dtype=np.float32)
    xs = np.asarray(inputs["x_sample"], dtype=np.float32)
    B, L, _ = xp.shape
    Bs, Ls, _ = xs.shape
    nc, _ = build([L, Ls])
    wm = _wmap(inputs, max(L, Ls))
    in_maps = []
    for i in range(8):
        m = dict(wm)
        m["x"] = np.ascontiguousarray(np.concatenate([xp[i], xs[i % Bs]], axis=0))
        in_maps.append(m)
    res = run_bass_kernel_spmd(nc, in_maps, core_ids=list(range(8)))
    yp = np.stack([res.results[i]["y"][:L] for i in range(8)], 0)
    ys = np.stack([res.results[i]["y"][L:] for i in range(Bs)], 0)
    return (yp.astype(np.float32), ys.astype(np.float32))
```

```python
import math
from contextlib import ExitStack
import numpy as np
import ml_dtypes
import concourse.bass as bass
import concourse.mybir as mybir
from concourse.bass_utils import run_bass_kernel_spmd

F32 = mybir.dt.float32
BF16 = mybir.dt.bfloat16
ALU = mybir.AluOpType
AF = mybir.ActivationFunctionType

D = 1024
T = 512
NQH, NKV, HD = 16, 4, 64
DFF = 4096
DIN = 2048
NH = 32
NG = 4
DS = 128
CONVD = 3072
INP = 5184
EPS = 1e-6
GEPS = 1e-5
NRING = 4


class Buf:
    __slots__ = ("name", "w", "r", "excl")

    def __init__(self, name=""):
        self.name = name
        self.w = None
        self.r = {}
        self.excl = False


class Ctx:
    def __init__(self, nc):
        self.nc = nc
        self.eng = {"pe": nc.tensor, "act": nc.scalar, "dve": nc.vector,
                    "pool": nc.gpsimd, "sp": nc.sync}
        self.sems = {}
        self.cnt = {}
        for k in self.eng:
            self.sems[k] = nc.alloc_semaphore("s_" + k)
            self.cnt[k] = 0
        self.waited = {}
        self.n_dma_sem = 0
        self.ninst = 0
        self.nwait = 0

    def new_dma_sem(self):
        k = "dma%d" % self.n_dma_sem
        self.n_dma_sem += 1
        self.sems[k] = self.nc.alloc_semaphore("s_" + k)
        self.cnt[k] = 0
        return k

    def _wait(self, e, dep):
        if dep is None:
            return
        key, val = dep
        if self.waited.get((e, key), 0) >= val:
            return
        if key == e and e == "pe":
            return
        self.eng[e].wait_ge(self.sems[key], val)
        self.waited[(e, key)] = val
        self.nwait += 1

    def deps(self, e, reads, writes):
        for b in reads:
            self._wait(e, b.w)
            if b.excl:
                for k, v in list(b.r.items()):
                    if k != e:
                        self._wait(e, (k, v))
        for b in writes:
            self._wait(e, b.w)
            for k, v in list(b.r.items()):
                self._wait(e, (k, v))

    def op(self, e, fn, reads=(), writes=(), signal=True):
        self.deps(e, reads, writes)
        ins = fn(self.eng[e])
        self.ninst += 1
        if signal:
            self.cnt[e] += 1
            ins.then_inc(self.sems[e], 1)
            v = self.cnt[e]
        else:
            v = self.cnt[e] + 1
        for b in writes:
            b.w = (e, v)
            b.r = {}
        for b in reads:
            if b.r.get(e, 0) < v:
                b.r[e] = v
        return ins

    def dma(self, q, semk, out, in_, reads=(), writes=()):
        self.deps(q, reads, writes)
        if self.cnt[semk] > 0:
            self._wait(q, (semk, self.cnt[semk]))
        ins = self.eng[q].dma_start(out=out, in_=in_)
        self.cnt[semk] += 16
        ins.then_inc(self.sems[semk], 16)
        v = self.cnt[semk]
        self.ninst += 1
        for b in writes:
            b.w = (semk, v)
            b.r = {}
        for b in reads:
            if b.r.get(semk, 0) < v:
                b.r[semk] = v
        return ins

    def barrier(self):
        for e in self.eng:
            for k, v in self.cnt.items():
                if v > 0 and k != e:
                    self._wait(e, (k, v))

    def final_wait(self, e="sp"):
        for k, v in self.cnt.items():
            if v > 0 and k != e:
                self._wait(e, (k, v))


class TB:
    def __init__(self, t, n=1):
        self.t = t
        self.b = [Buf() for _ in range(n)]


def _consts(lmax):
    inv = 1.0 / (10000.0 ** (np.arange(0, HD, 2, dtype=np.float32) / HD))
    ang = np.arange(lmax, dtype=np.float32)[:, None] * inv[None, :].astype(np.float32)
    ang = ang.astype(np.float32)
    cos = np.cos(ang).astype(np.float32).T
    sin = np.sin(ang).astype(np.float32).T
    cosT = np.concatenate([cos, cos, cos, cos], 0)
    sinT = np.concatenate([-sin, sin, -sin, sin], 0)
    j = np.arange(128)[:, None]
    i = np.arange(128)[None, :]
    mprev = (j >= i).astype(np.float32)
    mnext = (j <= i).astype(np.float32)
    return {
        "c_cos": np.ascontiguousarray(cosT), "c_sin": np.ascontiguousarray(sinT),
        "c_ident": np.eye(128, dtype=np.float32),
        "c_mprev": mprev.astype(ml_dtypes.bfloat16), "c_mnext": mnext.astype(ml_dtypes.bfloat16),
        "c_tle": mnext.astype(np.float32), "c_tge": mprev.astype(np.float32),
    }


class _Stop(Exception):
    pass


def build(seq_lens, phases=3, debug=False, stop=0):
    ntok = sum(seq_lens)
    lmax = max(seq_lens)
    nc = bass.Bass("TRN2", target_bir_lowering=False)
    c = Ctx(nc)

    def din(name, shape, dt=F32):
        return nc.dram_tensor(name, list(shape), dt, kind="ExternalInput").ap()

    def dscr(name, shape, dt):
        return nc.dram_tensor(name, list(shape), dt, kind="Internal").ap()

    x_d = din("x", [ntok, D])
    out_d = nc.dram_tensor("y", [ntok, D], F32, kind="ExternalOutput").ap()
    wqkv_d = din("attn_w_qkv", [D, 1536])
    wo_d = din("attn_w_o", [D, D])
    sink_d = din("attn_sink", [1, NQH])
    win_d = din("ssm_w_in", [D, INP])
    convw_d = din("ssm_conv_w", [5, CONVD])
    convb_d = din("ssm_conv_b", [1, CONVD])
    dtb_d = din("ssm_dt_bias", [1, 64])
    alog_d = din("ssm_a_log", [1, 64])
    dsk_d = din("ssm_d", [1, NH])
    gnw_d = din("ssm_norm_w", [1, DIN])
    wout_d = din("ssm_w_out", [DIN, D])
    nmpre_d = din("norm_mix_pre", [2, D])
    nmpost_d = din("norm_mix_post", [2, D])
    nfpre_d = din("norm_ffn_pre", [2, D])
    nfpost_d = din("norm_ffn_post", [2, D])
    wup_d = din("mlp_w_up", [2, D, DFF])
    wdn_d = din("mlp_w_down", [2, DFF, D])
    cos_d = din("c_cos", [128, lmax])
    sin_d = din("c_sin", [128, lmax])
    ident_d = din("c_ident", [128, 128])
    mprev_d = din("c_mprev", [128, 128], BF16)
    mnext_d = din("c_mnext", [128, 128], BF16)
    tle_d = din("c_tle", [128, 128])
    tge_d = din("c_tge", [128, 128])

    s_qk = dscr("s_qk", [D, 2560], BF16)
    s_v = dscr("s_v", [D, 256], BF16)
    s_wo = dscr("s_wo", [D, D], BF16)
    s_up = [dscr("s_up%d" % i, [D, DFF], BF16) for i in range(2)]
    s_dn = [dscr("s_dn%d" % i, [DFF, D], BF16) for i in range(2)]
    s_x1 = dscr("s_x1", [ntok // T, 128, 8 * T], F32)
    s_hn = dscr("s_hn", [ntok // T, 128, 8 * T], BF16)
    s_bT = dscr("s_bT", [ntok // T, 3, 64, T], BF16)
    s_cT = dscr("s_cT", [ntok // T, 3, 64, T], BF16)
    s_wfm = dscr("s_wfm", [D, CONVD], BF16)
    s_wtm = dscr("s_wtm", [D, 2560], BF16)
    s_wout = dscr("s_wout", [DIN, D], BF16)
    NCH = ntok // 128
    s_yp = dscr("s_yp", [ntok, DIN], F32)
    s_z = dscr("s_z", [ntok, DIN], F32)
    s_ct = dscr("s_ct", [ntok // T, 128, 4 * T], BF16)
    s_ecb = dscr("s_ecb", [ntok, NH], F32)
    s_sb = dscr("s_sb", [NCH, 128, DIN], F32)
    s_cdb = dscr("s_cdb", [NCH, 128, NH], F32)

    es = ExitStack()

    def sb(name, shape, dt=F32, n=1):
        return TB(es.enter_context(nc.sbuf_tensor(name, list(shape), dt)), n)

    ident = sb("ident", [128, 128])
    identb = sb("identb", [128, 128], BF16)
    ones = sb("ones", [128, 128], BF16)
    mprev = sb("mprev", [128, 128], BF16)
    mnext = sb("mnext", [128, 128], BF16)
    esink = sb("esink", [128, NQH])
    gvec = sb("gvec", [128, 8, 8])
    epsc = sb("epsc", [128, 4])
    tle = sb("tle", [128, 128])
    tge = sb("tge", [128, 128])
    onesf = sb("onesf", [128, 128])
    convw = sb("convw", [128, 24, 5])
    convb = sb("convb", [128, 24])
    dtb_bc = sb("dtb_bc", [128, 64])
    a_bc = sb("a_bc", [128, 64])
    d32 = sb("d32", [128, NH])
    gnw = sb("gnw", [128, 16])
    psum = [TB(es.enter_context(nc.psum_tensor("ps%d" % i, [128, 512], F32))) for i in range(8)]
    for p_ in psum:
        p_.b[0].excl = True
    pools = {"all": list(range(8)), "A": [0, 1, 2, 3, 4, 5], "B": [6, 7]}
    cur_pool = ["all"]
    pctrs = {"all": 0, "A": 0, "B": 0}

    def ps():
        k = cur_pool[0]
        lst = pools[k]
        p = psum[lst[pctrs[k] % len(lst)]]
        pctrs[k] += 1
        return p

    dumped = {}

    def dump(name, ap, shape, dt, bufs):
        if not debug or name in dumped:
            return
        d = nc.dram_tensor("dbg_" + name, list(shape), dt, kind="ExternalOutput").ap()
        dumped[name] = d
        c.dma("sp", sem_dbg, d, ap, reads=bufs)

    sem_dbg = c.new_dma_sem()
    sem_c = c.new_dma_sem()
    sem_ld = [c.new_dma_sem() for _ in range(8)]
    ld_ctr = [0]

    def ldsem():
        ld_ctr[0] += 1
        return sem_ld[ld_ctr[0] % 8]

    sem_st = [c.new_dma_sem() for _ in range(8)]
    st_ctr = [0]

    def stsem():
        st_ctr[0] += 1
        return sem_st[st_ctr[0] % 8]

    blk = es.enter_context(nc.Block())

    c.dma("sp", sem_c, ident.t[:], ident_d, writes=ident.b)
    c.dma("sp", sem_c, mprev.t[:], mprev_d, writes=mprev.b)
    c.dma("sp", sem_c, mnext.t[:], mnext_d, writes=mnext.b)
    c.dma("sp", sem_c, esink.t[:], sink_d.partition_broadcast(128), writes=esink.b)
    c.op("act", lambda e: e.activation(out=esink.t[:], in_=esink.t[:], func=AF.Exp), reads=esink.b, writes=esink.b)
    c.op("dve", lambda e: e.tensor_copy(out=identb.t[:], in_=ident.t[:]), reads=ident.b, writes=identb.b)
    c.op("pool", lambda e: e.memset(ones.t[:], 1.0), writes=ones.b)
    c.op("pool", lambda e: e.memset(epsc.t[:, 0:1], EPS), writes=epsc.b)
    c.op("pool", lambda e: e.memset(epsc.t[:, 1:2], GEPS), writes=epsc.b)
    c.op("pool", lambda e: e.memset(epsc.t[:, 2:3], 1.0), writes=epsc.b)
    c.op("pool", lambda e: e.memset(epsc.t[:, 3:4], 0.0), writes=epsc.b)
    c.op("pool", lambda e: e.memset(onesf.t[:], 1.0), writes=onesf.b)
    c.dma("sp", sem_c, tle.t[:], tle_d, writes=tle.b)
    c.dma("sp", sem_c, tge.t[:], tge_d, writes=tge.b)
    with nc.allow_non_contiguous_dma(reason="tiny param vectors"):
        for k5 in range(5):
            c.dma("sp", sem_c, convw.t[:, :, k5], convw_d[k5].rearrange("(m p) -> p m", p=128), writes=convw.b)
        c.dma("sp", sem_c, convb.t[:], convb_d[0].rearrange("(m p) -> p m", p=128), writes=convb.b)
        c.dma("sp", sem_c, gnw.t[:], gnw_d[0].rearrange("(k p) -> p k", p=128), writes=gnw.b)
    c.dma("sp", sem_c, dtb_bc.t[:], dtb_d.partition_broadcast(128), writes=dtb_bc.b)
    c.dma("sp", sem_c, a_bc.t[:], alog_d.partition_broadcast(128), writes=a_bc.b)
    c.dma("sp", sem_c, d32.t[:], dsk_d.partition_broadcast(128), writes=d32.b)
    c.op("act", lambda e: e.activation(out=a_bc.t[:], in_=a_bc.t[:], func=AF.Exp), reads=a_bc.b, writes=a_bc.b)
    c.op("dve", lambda e: e.tensor_scalar(out=a_bc.t[:], in0=a_bc.t[:], scalar1=-1.0, scalar2=None, op0=ALU.mult), reads=a_bc.b, writes=a_bc.b)
    for li in range(2):
        for wi, src in enumerate((nmpre_d, nmpost_d, nfpre_d, nfpost_d)):
            with nc.allow_non_contiguous_dma(reason="tiny gain vectors"):
                c.dma("sp", sem_c, gvec.t[:, li * 4 + wi, :], src[li].rearrange("(k p) -> p k", p=128), writes=gvec.b)

    with ExitStack() as es0:
        NSTG = 6
        stg = [TB(es0.enter_context(nc.sbuf_tensor("stg%d" % i, [128, 2048], F32))) for i in range(NSTG)]
        stb = [TB(es0.enter_context(nc.sbuf_tensor("stb%d" % i, [128, 2048], BF16))) for i in range(NSTG)]
        cv = [0]

        tasks = []

        def conv_piece(src_ap, ncols, scale_ap, dsts):
            i = cv[0] % NSTG
            cv[0] += 1
            use_act = (cv[0] % 2 == 1)

            def load():
                c.dma("sp", ldsem(), stg[i].t[:, 0:ncols], src_ap, writes=stg[i].b)

            def cast():
                if scale_ap is None:
                    if use_act:
                        c.op("act", lambda e: e.activation(out=stb[i].t[:, 0:ncols], in_=stg[i].t[:, 0:ncols], func=AF.Copy), reads=stg[i].b, writes=stb[i].b)
                    else:
                        c.op("dve", lambda e: e.tensor_copy(out=stb[i].t[:, 0:ncols], in_=stg[i].t[:, 0:ncols]), reads=stg[i].b, writes=stb[i].b)
                else:
                    if use_act:
                        c.op("act", lambda e: e.activation(out=stb[i].t[:, 0:ncols], in_=stg[i].t[:, 0:ncols], func=AF.Copy, scale=scale_ap), reads=stg[i].b + gvec.b + gnw.b, writes=stb[i].b)
                    else:
                        c.op("dve", lambda e: e.tensor_scalar(out=stb[i].t[:, 0:ncols], in0=stg[i].t[:, 0:ncols], scalar1=scale_ap, scalar2=None, op0=ALU.mult), reads=stg[i].b + gvec.b + gnw.b, writes=stb[i].b)

            def store():
                for (dap, c0, n) in dsts:
                    c.dma("sp", stsem(), dap, stb[i].t[:, c0:c0 + n], reads=stb[i].b)
            tasks.append((load, cast, store))

        stq = [TB(es0.enter_context(nc.sbuf_tensor("stq%d" % i, [128, 2816], BF16))) for i in range(2)]

        def conv_perm(src_ap, ncols, scale_ap, pieces, outs, idx):
            i = cv[0] % NSTG
            cv[0] += 1
            q_ = stq[idx % 2]

            def load():
                c.dma("sp", ldsem(), stg[i].t[:, 0:ncols], src_ap, writes=stg[i].b)

            def cast():
                for pi_, (dc, sc, n) in enumerate(pieces):
                    if pi_ % 2 == 0:
                        c.op("act", lambda e: e.activation(out=q_.t[:, dc:dc + n], in_=stg[i].t[:, sc:sc + n], func=AF.Copy, scale=scale_ap), reads=stg[i].b + gvec.b, writes=q_.b)
                    else:
                        c.op("dve", lambda e: e.tensor_scalar(out=q_.t[:, dc:dc + n], in0=stg[i].t[:, sc:sc + n], scalar1=scale_ap, scalar2=None, op0=ALU.mult), reads=stg[i].b + gvec.b, writes=q_.b)

            def store():
                for (dap, c0, n) in outs:
                    c.dma("sp", stsem(), dap, q_.t[:, c0:c0 + n], reads=q_.b)
            tasks.append((load, cast, store))

        class _Col:
            def __init__(self):
                pass

        for k in range(8):
            rows = slice(k * 128, (k + 1) * 128)
            g = gvec.t[:, 0, k:k + 1]
            dsts = []
            for j in range(2):
                for r in range(4):
                    cch = 4 * j + r
                    blkq, pos = cch // 2, cch % 2
                    base = blkq * 512 + pos * 256
                    for e2 in range(2):
                        h = 8 * j + 4 * e2 + r
                        dsts.append((base + e2 * 64, h * 64, 64))
                        dsts.append((base + 128 + e2 * 64, h * 64 + 32, 32))
                        dsts.append((base + 128 + e2 * 64 + 32, h * 64, 32))
            for kc in range(2):
                base = 2048 + kc * 256
                for e2 in range(2):
                    gk = 2 * kc + e2
                    dsts.append((base + e2 * 64, 1024 + gk * 64, 64))
                    dsts.append((base + 128 + e2 * 64, 1024 + gk * 64 + 32, 32))
                    dsts.append((base + 128 + e2 * 64 + 32, 1024 + gk * 64, 32))
            dsts.append((2560, 1280, 256))
            conv_perm(wqkv_d[rows, :], 1536, g, dsts, [(s_qk[rows, :], 0, 2560), (s_v[rows, :], 2560, 256)], k)
            conv_piece(wo_d[rows, :], 1024, None, [(s_wo[rows, :], 0, 1024)])
            for li in range(2 if phases >= 3 else 1):
                g2 = gvec.t[:, li * 4 + 2, k:k + 1]
                for hh in range(2):
                    conv_piece(wup_d[li, rows, hh * 2048:(hh + 1) * 2048], 2048, g2, [(s_up[li][rows, hh * 2048:(hh + 1) * 2048], 0, 2048)])
        for li in range(2 if phases >= 3 else 1):
            for k in range(32):
                rows = slice(k * 128, (k + 1) * 128)
                conv_piece(wdn_d[li, rows, :], 1024, None, [(s_dn[li][rows, :], 0, 1024)])
        if phases >= 2:
            for k in range(8):
                rows = slice(k * 128, (k + 1) * 128)
                g = gvec.t[:, 4, k:k + 1]
                conv_piece(win_d[rows, 0:2048], 2048, g, [(s_wtm[rows, 0:2048], 0, 2048)])
                conv_piece(win_d[rows, 2048:4096], 2048, g, [(s_wfm[rows, 0:2048], 0, 2048)])
                conv_piece(win_d[rows, 4096:5184], 1088, g, [(s_wfm[rows, 2048:3072], 0, 1024), (s_wtm[rows, 2048:2112], 1024, 64)])
            for k in range(16):
                rows = slice(k * 128, (k + 1) * 128)
                conv_piece(wout_d[rows, :], 1024, gnw.t[:, k:k + 1], [(s_wout[rows, :], 0, 1024)])

        DEPTH = NSTG - 1
        for i_ in range(len(tasks) + DEPTH):
            if i_ < len(tasks):
                tasks[i_][0]()
            if i_ - DEPTH >= 0:
                tasks[i_ - DEPTH][1]()
                tasks[i_ - DEPTH][2]()
    c.barrier()
    tiles = []
    off = 0
    for si, L in enumerate(seq_lens):
        nt = L // T
        for ti in range(nt):
            tiles.append(dict(seq=si, pos0=ti * T, row0=off + ti * T, first=(ti == 0), last=(ti == nt - 1), idx=len(tiles)))
        off += L
    NT = len(tiles)

    def wblock(scr, kb, nb, ncols=512):
        return scr[kb * 1024:(kb + 1) * 1024, nb * ncols:(nb + 1) * ncols].rearrange("(kk p) n -> p kk n", p=128), ncols

    def sched_p1():
        for t in range(NT + 1):
            if t < NT:
                for b in range(5):
                    yield wblock(s_qk, 0, b)
                yield wblock(s_v, 0, 0, 256)
            if t >= 1:
                for b in range(2):
                    yield wblock(s_wo, 0, b)
                for b in range(8):
                    yield wblock(s_up[0], 0, b)
                for nb in range(2):
                    for kb in range(4):
                        yield wblock(s_dn[0], kb, nb)

    class WStream:
        def __init__(self, gen, ring):
            self.gen = gen
            self.ring = ring
            self.sem = [c.new_dma_sem() for _ in ring]
            self.issued = 0
            self.used = 0
            self.done = False

        def _issue(self):
            try:
                ap, ncols = next(self.gen)
            except StopIteration:
                self.done = True
                return
            s = self.issued % len(self.ring)
            r = self.ring[s]
            c.dma("sp", self.sem[s], r.t[:, :, 0:ncols], ap, writes=r.b)
            self.issued += 1

        def next(self):
            while not self.done and self.issued < self.used + len(self.ring):
                self._issue()
            r = self.ring[self.used % len(self.ring)]
            self.used += 1
            return r

    def rms_stats(src_fn, nchunk, rstd, dim, eps, sqbuf, sq_eng="act", c0=0, n=T):
        p = ps()
        for k in range(nchunk):
            ap, bufs = src_fn(k)
            c.op("act", lambda e: e.activation(out=sqbuf.t[:, k, c0:c0 + n], in_=ap, func=AF.Square), reads=bufs, writes=[sqbuf.b[k]])
        for k in range(nchunk):
            c.op("pe", lambda e: e.matmul(p.t[:, 0:n], lhsT=ones.t[:], rhs=sqbuf.t[:, k, c0:c0 + n], start=(k == 0), stop=(k == nchunk - 1)),
                 reads=[sqbuf.b[k]] + ones.b, writes=p.b, signal=(k == nchunk - 1))
        c.op("act", lambda e: e.activation(out=rstd.t[:, c0:c0 + n], in_=p.t[:, 0:n], func=AF.Ln, scale=1.0 / dim, bias=epsc.t[:, 0:1] if eps == EPS else epsc.t[:, 1:2]), reads=p.b + epsc.b, writes=rstd.b)
        c.op("act", lambda e: e.activation(out=rstd.t[:, c0:c0 + n], in_=rstd.t[:, c0:c0 + n], func=AF.Exp, scale=-0.5), reads=rstd.b, writes=rstd.b)

    def linear_g(ws, act_fn, kchunks, nblocks, kblocks, epilogue):
        for nb in range(nblocks):
            banks = [ps() for _ in range(4)]
            for kb in range(kblocks):
                w = ws.next()
                for m in range(4):
                    for kk in range(8):
                        kidx = kb * 8 + kk
                        ap, bufs = act_fn(kidx)
                        first = (kb == 0 and kk == 0)
                        last = (kb == kblocks - 1 and kk == 7)
                        c.op("pe", lambda e: e.matmul(banks[m].t[:], lhsT=w.t[:, kk, m * 128:(m + 1) * 128], rhs=ap, start=first, stop=last),
                             reads=w.b + bufs, writes=banks[m].b, signal=(kk == 7))
                if kb < kblocks - 1:
                    yield
            for m in range(4):
                epilogue(nb, m, banks[m])
            yield

    def linear(ws, act_fn, kchunks, nblocks, kblocks, epilogue):
        for _ in linear_g(ws, act_fn, kchunks, nblocks, kblocks, epilogue):
            pass

    def post_norm_residual(xT, asb, sqbuf, rstd, gidx, tmp):
        rms_stats(lambda k: (asb.t[:, k * T:(k + 1) * T], [asb.b[k]]), 8, rstd, D, EPS, sqbuf)
        dump("rstd2", rstd.t[:], [128, T], F32, rstd.b)
        for k in range(8):
            tb = tmp[k % 2]
            c.op("dve", lambda e: e.scalar_tensor_tensor(out=tb.t[:], in0=asb.t[:, k * T:(k + 1) * T], scalar=gvec.t[:, gidx, k:k + 1], in1=rstd.t[:], op0=ALU.mult, op1=ALU.mult),
                 reads=[asb.b[k]] + rstd.b + gvec.b, writes=tb.b)
            c.op("pool" if k % 2 == 0 else "dve", lambda e: e.tensor_tensor(out=xT.t[:, k, :], in0=xT.t[:, k, :], in1=tb.t[:], op=ALU.add), reads=tb.b + [xT.b[k]], writes=[xT.b[k]])

    def pre_norm(xT, hn, sqbuf, rstd):
        rms_stats(lambda k: (xT.t[:, k, :], [xT.b[k]]), 8, rstd, D, EPS, sqbuf)
        for k in range(8):
            eng = "dve" if k % 2 == 0 else "pool"
            c.op(eng, lambda e: e.tensor_tensor(out=hn.t[:, k, :], in0=xT.t[:, k, :], in1=rstd.t[:], op=ALU.mult), reads=[xT.b[k]] + rstd.b, writes=[hn.b[k]])

    def mlp_g(ws, li, xT, hn, h1T, asb, sqbuf, rstd, tmp):
        pre_norm(xT, hn, sqbuf, rstd)
        yield

        def ep_up(nb, m, p):
            j = nb * 4 + m
            tb = tmp[j % 2]
            c.op("act", lambda e: e.activation(out=tb.t[:], in_=p.t[:], func=AF.Square), reads=p.b, writes=tb.b)
            c.op("dve", lambda e: e.scalar_tensor_tensor(out=h1T.t[:, j, :], in0=p.t[:], scalar=0.0, in1=tb.t[:], op0=ALU.is_gt, op1=ALU.mult), reads=p.b + tb.b, writes=[h1T.b[j]])
        yield from linear_g(ws, lambda k: (hn.t[:, k, :], [hn.b[k]]), 8, 8, 1, ep_up)

        def ep_dn(nb, m, p):
            j = nb * 4 + m
            c.op("act", lambda e: e.activation(out=asb.t[:, j * T:(j + 1) * T], in_=p.t[:], func=AF.Copy), reads=p.b, writes=[asb.b[j]])
        yield from linear_g(ws, lambda k: (h1T.t[:, k, :], [h1T.b[k]]), 32, 2, 4, ep_dn)
        post_norm_residual(xT, asb, sqbuf, rstd, li * 4 + 3, tmp)
        yield

    def mlp(ws, li, xT, hn, h1T, asb, sqbuf, rstd, tmp):
        for _ in mlp_g(ws, li, xT, hn, h1T, asb, sqbuf, rstd, tmp):
            pass

    def store_tokmajor(xT, row0, stage):
        for b in range(4):
            for half in range(2):
                p = ps()
                for kk in range(4):
                    k = half * 4 + kk
                    c.op("pe", lambda e: e.transpose(out=p.t[:, kk * 128:(kk + 1) * 128], in_=xT.t[:, k, b * 128:(b + 1) * 128], identity=ident.t[:]),
                         reads=[xT.b[k]] + ident.b, writes=p.b, signal=(kk == 3))
                eng = "act" if half == 0 else "dve"
                if eng == "act":
                    c.op("act", lambda e: e.activation(out=stage.t[:, b, half * 512:(half + 1) * 512], in_=p.t[:], func=AF.Copy), reads=p.b, writes=[stage.b[b]])
                else:
                    c.op("dve", lambda e: e.tensor_copy(out=stage.t[:, b, half * 512:(half + 1) * 512], in_=p.t[:]), reads=p.b, writes=[stage.b[b]])
            c.dma("sp", stsem(), out_d[row0 + b * 128: row0 + (b + 1) * 128, :], stage.t[:, b, :], reads=[stage.b[b]])

    def store_tokmajor_alias(x, row0, xin_):
        for b in range(4):
            for half in range(2):
                p = ps()
                for kk in range(4):
                    k = half * 4 + kk
                    c.op("pe", lambda e: e.transpose(out=p.t[:, kk * 128:(kk + 1) * 128], in_=x.t[:, k, b * 128:(b + 1) * 128], identity=ident.t[:]),
                         reads=[x.b[k]] + ident.b, writes=p.b, signal=(kk == 3))
                bb = xin_.b[2 * b + half]
                if half == 0:
                    c.op("act", lambda e: e.activation(out=xin_.t[:, b * 1024 + half * 512: b * 1024 + (half + 1) * 512], in_=p.t[:], func=AF.Copy), reads=p.b, writes=[bb])
                else:
                    c.op("dve", lambda e: e.tensor_copy(out=xin_.t[:, b * 1024 + half * 512: b * 1024 + (half + 1) * 512], in_=p.t[:]), reads=p.b, writes=[bb])
            c.dma("sp", stsem(), out_d[row0 + b * 128: row0 + (b + 1) * 128, :], xin_.t[:, b * 1024:(b + 1) * 1024], reads=[xin_.b[2 * b], xin_.b[2 * b + 1]])


    with ExitStack() as es1:
        def sb1(name, shape, dt=F32, n=1):
            return TB(es1.enter_context(nc.sbuf_tensor(name, list(shape), dt)), n)
        ring = [sb1("ring%d" % i, [128, 8, 512], BF16) for i in range(NRING)]
        ws = WStream(sched_p1(), ring)
        xin = sb1("xin", [128, 4096], F32, 8)
        asb1 = sb1("asb1", [128, 4096], F32, 8)
        xT = [sb1("xT%d" % i, [128, 8, T], F32, 8) for i in range(2)]
        sqb = sb1("sqb", [128, 8, T], BF16, 8)
        rstd = sb1("rstd", [128, T])
        hn = sb1("hn", [128, 8, T], BF16, 8)
        qT = [sb1("qT%d" % i, [128, 8, T], BF16, 8) for i in range(2)]
        kT = [sb1("kT%d" % i, [128, 2, T], BF16, 2) for i in range(3)]
        Va = [sb1("Va%d" % i, [128, 4, 4, 66], BF16, 4) for i in range(3)]
        cs = sb1("cos", [128, T])
        sn = sb1("sin", [128, T])
        tmp = [sb1("tmp%d" % i, [128, T]) for i in range(2)]
        PT = [[sb1("PT%d_%d" % (i, j), [128, 512], BF16) for j in range(3)] for i in range(2)]
        Osb = sb1("Osb", [128, 1024], BF16)
        den = sb1("den", [128, 8])
        OT = sb1("OT", [128, 8, T], BF16, 8)
        h1T = sb1("h1T", [128, 32, T], BF16, 32)
        for v in Va:
            c.op("pool", lambda e: e.memset(v.t[:], 1.0), writes=v.b)

        def stageA(t):
            tl = tiles[t]
            x = xT[t % 2]
            q = qT[t % 2]
            kk_ = kT[t % 3]
            va = Va[t % 3]
            for b in range(4):
                c.dma("sp", ldsem(), xin.t[:, b * 1024:(b + 1) * 1024], x_d[tl["row0"] + b * 128: tl["row0"] + (b + 1) * 128, :], writes=[xin.b[2 * b], xin.b[2 * b + 1]])
            c.dma("sp", ldsem(), cs.t[:], cos_d[:, tl["pos0"]:tl["pos0"] + T], writes=cs.b)
            c.dma("sp", ldsem(), sn.t[:], sin_d[:, tl["pos0"]:tl["pos0"] + T], writes=sn.b)
            banks = [ps() for _ in range(8)]
            for k in range(8):
                for b in range(4):
                    c.op("pe", lambda e: e.transpose(out=banks[k].t[:, b * 128:(b + 1) * 128], in_=xin.t[:, b * 1024 + k * 128: b * 1024 + (k + 1) * 128], identity=ident.t[:]),
                         reads=[xin.b[2 * b], xin.b[2 * b + 1]] + ident.b, writes=banks[k].b, signal=(b == 3))
                if k % 2 == 0:
                    c.op("act", lambda e: e.activation(out=x.t[:, k, :], in_=banks[k].t[:], func=AF.Copy), reads=banks[k].b, writes=[x.b[k]])
                else:
                    c.op("dve", lambda e: e.tensor_copy(out=x.t[:, k, :], in_=banks[k].t[:]), reads=banks[k].b, writes=[x.b[k]])
            dump("ones", ones.t[:], [128, 128], BF16, ones.b)
            dump("epsc", epsc.t[:], [128, 2], F32, epsc.b)
            dump("gvec", gvec.t[:].rearrange("p a b -> p (a b)"), [128, 64], F32, gvec.b)
            dump("esink", esink.t[:], [128, 16], F32, esink.b)
            dump("xT", x.t[:].rearrange("p k n -> p (k n)"), [128, 8 * T], F32, x.b)
            pre_norm(x, hn, sqb, rstd)
            dump("rstd", rstd.t[:], [128, T], F32, rstd.b)
            dump("hn", hn.t[:].rearrange("p k n -> p (k n)"), [128, 8 * T], BF16, hn.b)

            def ep_qk(nb, m, p, hold={}):
                if m % 2 == 0:
                    hold["p"] = p
                    return
                pq = hold["p"]
                cch = nb * 2 + m // 2
                dst, dbuf = (q.t[:, cch, :], q.b[cch]) if nb < 4 else (kk_.t[:, m // 2, :], kk_.b[m // 2])
                c.op("dve", lambda e: e.tensor_tensor(out=tmp[0].t[:], in0=p.t[:], in1=sn.t[:], op=ALU.mult), reads=p.b + sn.b, writes=tmp[0].b)
                c.op("dve", lambda e: e.tensor_tensor(out=tmp[1].t[:], in0=pq.t[:], in1=cs.t[:], op=ALU.mult), reads=pq.b + cs.b, writes=tmp[1].b)
                c.op("pool", lambda e: e.tensor_tensor(out=dst, in0=tmp[0].t[:], in1=tmp[1].t[:], op=ALU.add), reads=tmp[0].b + tmp[1].b, writes=[dbuf])
            linear(ws, lambda k: (hn.t[:, k, :], [hn.b[k]]), 8, 5, 1, ep_qk)
            dump("qT", q.t[:].rearrange("p k n -> p (k n)"), [128, 8 * T], BF16, q.b)
            dump("kT", kk_.t[:].rearrange("p k n -> p (k n)"), [128, 2 * T], BF16, kk_.b)
            w = ws.next()
            for b in range(4):
                p = ps()
                for k in range(8):
                    c.op("pe", lambda e: e.matmul(p.t[:, 0:256], lhsT=hn.t[:, k, b * 128:(b + 1) * 128], rhs=w.t[:, k, 0:256], start=(k == 0), stop=(k == 7)),
                         reads=[hn.b[k]] + w.b, writes=p.b, signal=(k == 7))
                c.op("act", lambda e: e.activation(out=va.t[:, b, :, 0:64], in_=p.t[:, 0:256].rearrange("p (g d) -> p g d", g=4), func=AF.Copy), reads=p.b, writes=[va.b[b]])

        def stageB(t):
            tl = tiles[t]
            x = xT[t % 2]
            q = qT[t % 2]

            def keyblocks(b):
                kbs = []
                if b > 0:
                    kbs.append((t, b - 1, mprev))
                elif not tl["first"]:
                    kbs.append((t - 1, 3, mprev))
                kbs.append((t, b, None))
                if b < 3:
                    kbs.append((t, b + 1, mnext))
                elif not tl["last"]:
                    kbs.append((t + 1, 0, mnext))
                return kbs

            def part1(b, g):
                kbs = keyblocks(b)
                j, e2 = g // 2, g % 2
                rows = slice(e2 * 64, (e2 + 1) * 64)
                pts = PT[g % 2]
                for ci, (kt, kb, msk) in enumerate(kbs):
                    p = ps()
                    kt_ = kT[kt % 3]
                    c.op("pe", lambda e: e.matmul(p.t[:], lhsT=kt_.t[rows, j, kb * 128:(kb + 1) * 128], rhs=q.t[rows, 4 * j:4 * j + 4, b * 128:(b + 1) * 128], start=True, stop=True),
                         reads=[kt_.b[j]] + q.b[4 * j:4 * j + 4], writes=p.b)
                    c.op("act", lambda e: e.activation(out=pts[ci].t[:], in_=p.t[:], func=AF.Exp, scale=HD ** -0.5), reads=p.b, writes=pts[ci].b)
                    if msk is not None:
                        c.op("pool", lambda e: e.tensor_tensor(out=pts[ci].t[:].rearrange("p (r i) -> p r i", r=4), in0=pts[ci].t[:].rearrange("p (r i) -> p r i", r=4),
                                                               in1=msk.t[:].unsqueeze(1).to_broadcast([128, 4, 128]), op=ALU.mult), reads=pts[ci].b + msk.b, writes=pts[ci].b)

            def part2(b, g):
                kbs = keyblocks(b)
                pts = PT[g % 2]
                po = ps()
                pov = po.t[:, 0:260].rearrange("p (r d) -> p r d", r=4)
                for r in range(4):
                    for ci, (kt, kb, msk) in enumerate(kbs):
                        va = Va[kt % 3]
                        c.op("pe", lambda e: e.matmul(pov[:, r, :], lhsT=pts[ci].t[:, r * 128:(r + 1) * 128], rhs=va.t[:, kb, g, 0:65], start=(ci == 0), stop=(ci == len(kbs) - 1)),
                             reads=pts[ci].b + [va.b[kb]], writes=po.b, signal=(r == 3 and ci == len(kbs) - 1))
                c.op("dve", lambda e: e.tensor_tensor(out=den.t[:, 0:4], in0=pov[:, :, 64], in1=esink.t[:, 4 * g:4 * g + 4], op=ALU.add), reads=po.b + esink.b, writes=den.b)
                c.op("dve", lambda e: e.reciprocal(out=den.t[:, 4:8], in_=den.t[:, 0:4]), reads=den.b, writes=den.b)
                c.op("dve", lambda e: e.tensor_tensor(out=Osb.t[:, g * 256:(g + 1) * 256].rearrange("p (r d) -> p r d", r=4), in0=pov[:, :, 0:64],
                                                     in1=den.t[:, 4:8].unsqueeze(2).to_broadcast([128, 4, 64]), op=ALU.mult), reads=po.b + den.b, writes=Osb.b)

            def part3(b):
                p = ps()
                pb = p.t[:].bitcast(BF16)
                for k in range(8):
                    c.op("pe", lambda e: e.transpose(out=pb[:, k * 128:(k + 1) * 128], in_=Osb.t[:, k * 128:(k + 1) * 128], identity=identb.t[:]),
                         reads=Osb.b + identb.b, writes=p.b, signal=(k == 7))
                c.op("act", lambda e: e.activation(out=OT.t[:, :, b * 128:(b + 1) * 128], in_=pb.rearrange("p (k i) -> p k i", k=8), func=AF.Copy), reads=p.b, writes=OT.b)

            its = [(b, g) for b in range(4) for g in range(4)]
            part1(*its[0])
            for ii, (b, g) in enumerate(its):
                if ii + 1 < len(its):
                    part1(*its[ii + 1])
                part2(b, g)
                if g == 3:
                    part3(b)

            dump("Va", Va[t % 3].t[:].rearrange("p a g d -> p (a g d)"), [128, 4 * 4 * 66], BF16, Va[t % 3].b)
            dump("OT", OT.t[:].rearrange("p k n -> p (k n)"), [128, 8 * T], BF16, OT.b)

            def ep_wo(nb, m, p):
                jj = nb * 4 + m
                c.op("act", lambda e: e.activation(out=asb1.t[:, jj * T:(jj + 1) * T], in_=p.t[:], func=AF.Copy), reads=p.b, writes=[asb1.b[jj]])
            linear(ws, lambda k: (OT.t[:, k, :], OT.b), 8, 2, 1, ep_wo)
            dump("ao", asb1.t[:], [128, 4096], F32, asb1.b)
            post_norm_residual(x, asb1, sqb, rstd, 1, tmp)
            dump("x_mid", x.t[:].rearrange("p k n -> p (k n)"), [128, 8 * T], F32, x.b)
            mlp(ws, 0, x, hn, h1T, asb1, sqb, rstd, tmp)
            if phases == 1:
                store_tokmajor_alias(x, tl["row0"], asb1)
            else:
                c.dma("sp", stsem(), s_x1[t], x.t[:].rearrange("p k n -> p (k n)"), reads=x.b)
                pre_norm(x, hn, sqb, rstd)
                c.dma("sp", stsem(), s_hn[t], hn.t[:].rearrange("p k n -> p (k n)"), reads=hn.b)

        for t in range(NT + 1):
            if t < NT:
                stageA(t)
            if t >= 1:
                stageB(t - 1)

    def sched_p2():
        for t in range(NT):
            for nb in range(6):
                yield wblock(s_wfm, 0, nb)
                if nb < 4:
                    yield wblock(s_wtm, 0, nb)
                elif nb == 4:
                    yield (s_wtm[0:1024, 2048:2112].rearrange("(kk p) n -> p kk n", p=128), 64)

    if phases >= 2:
      c.barrier()
      with ExitStack() as es2:
        def sb2(name, shape, dt=F32, n=1):
            return TB(es2.enter_context(nc.sbuf_tensor(name, list(shape), dt)), n)
        ring = [sb2("ringb%d" % i, [128, 8, 512], BF16) for i in range(3)]
        ws = WStream(sched_p2(), ring)
        hn2 = sb2("hn2", [128, 8, 516], BF16, 8)
        cin = [sb2("cin%d" % i, [128, 516]) for i in range(4)]
        acc = [sb2("acc%d" % i, [128, 512]) for i in range(4)]
        xcT = [sb2("xcT%d" % i, [128, 512], BF16) for i in range(8)]
        x_tm = sb2("x_tm", [128, 4, DIN], BF16, 4)
        BT = sb2("BT", [128, 4, T], BF16, 4)
        B_tm = sb2("B_tm", [128, 4, 512], BF16, 4)
        CT = sb2("CT", [128, 4, T], BF16, 4)
        zst = [sb2("zst%d" % i, [128, 512]) for i in range(2)]
        dtx = sb2("dtx", [128, 4, 64])
        dtt = sb2("dtt", [128, 4, 64])
        dtv = sb2("dtv", [128, 4, 64])
        lndt = sb2("lndt", [128, 4, 64])
        da = sb2("da", [128, 4, 64])
        cumx = sb2("cumx", [128, 4, 64])
        biasx = sb2("biasx", [128, 4, 64])
        ecum = sb2("ecum", [128, 4, 64])
        wst = sb2("wst", [128, 4, 64])
        cdec = sb2("cdec", [128, 4, 64])
        cbm = [sb2("cbm%d" % i, [128, 512]) for i in range(2)]
        Ep = [sb2("Ep%d" % i, [128, 8, 128]) for i in range(2)]
        Mt = [sb2("Mt%d" % i, [128, 8, 128], BF16) for i in range(4)]
        Ysb = sb2("Ysb", [128, DIN], F32, 4)
        t1s = [sb2("t1_%d" % i, [128, 512]) for i in range(2)]
        t1c = [0]
        xw2 = [[sb2("xw%d_%d" % (a, i), [128, DIN], BF16, 4) for i in range(2)] for a in range(2)]
        hstf = sb2("hstf", [128, DIN], F32, 4)
        hstf_bf = sb2("hstf_bf", [128, DIN], BF16, 4)
        Sbsb = sb2("Sbsb", [128, DIN], F32, 4)
        zc = [0]
        itc = [0]
        rowsA = [sb2("rowsA%d" % i, [128, 16, 128], BF16) for i in range(1)]
        rowsB = [sb2("rowsB%d" % i, [128, 16, 128], BF16) for i in range(1)]
        spl = [[sb2("spl%d_%d" % (a, b), [64, T], BF16) for b in range(3)] for a in range(2)]
        for rr in rowsA + rowsB:
            c.op("pool", lambda e: e.memset(rr.t[:], 1.0), writes=rr.b)
        bT = sb2("bT", [64, T])
        cT = sb2("cT", [64, T])
        scrb = Buf()
        Dd = sb2("Dd", [128, NH, 128], BF16)
        c.op("dve", lambda e: e.tensor_tensor(out=Dd.t[:], in0=identb.t[:].unsqueeze(1).to_broadcast([128, NH, 128]), in1=d32.t[:].unsqueeze(2).to_broadcast([128, NH, 128]), op=ALU.mult),
             reads=d32.b + identb.b, writes=Dd.b)

        def chk(n):
            if stop == n:
                raise _Stop()

        def p2_tile(t):
            tl = tiles[t]
            ch0 = tl["row0"] // 128
            def load_hn(tt):
                tl_ = tiles[tt]
                c.dma("sp", ldsem(), hn2.t[:, :, 0:512], s_hn[tt].rearrange("p (k n) -> p k n", k=8), writes=hn2.b)
                with nc.allow_non_contiguous_dma(reason="2-token conv halos"):
                    if tl_["first"]:
                        c.op("pool", lambda e: e.memset(hn2.t[:, :, 512:514], 0.0), writes=hn2.b)
                    else:
                        c.dma("sp", ldsem(), hn2.t[:, :, 512:514], s_hn[tt - 1].rearrange("p (k n) -> p k n", k=8)[:, :, 510:512], writes=hn2.b)
                    if tl_["last"]:
                        c.op("pool", lambda e: e.memset(hn2.t[:, :, 514:516], 0.0), writes=hn2.b)
                    else:
                        c.dma("sp", ldsem(), hn2.t[:, :, 514:516], s_hn[tt + 1].rearrange("p (k n) -> p k n", k=8)[:, :, 0:2], writes=hn2.b)

            if t == 0:
                load_hn(0)
            if tl["first"]:
                c.op("pool", lambda e: e.memset(hstf.t[:], 0.0), writes=hstf.b)
                c.op("pool", lambda e: e.memset(hstf_bf.t[:], 0.0), writes=hstf_bf.b)
            chk(1)
            if t == 1:
                chk(12)
            def tm_block(nb):
                w = ws.next()
                ncol = 512 if nb < 4 else 64
                for j in range(4):
                    p = ps()
                    for kk in range(8):
                        c.op("pe", lambda e: e.matmul(p.t[:, 0:ncol], lhsT=hn2.t[:, kk, j * 128:(j + 1) * 128], rhs=w.t[:, kk, 0:ncol], start=(kk == 0), stop=(kk == 7)),
                             reads=w.b + [hn2.b[kk]], writes=p.b, signal=(kk == 7))
                    if nb < 4:
                        zb = zst[zc[0] % 2]
                        zc[0] += 1
                        if zc[0] % 2 == 0:
                            c.op("act", lambda e: e.activation(out=zb.t[:], in_=p.t[:], func=AF.Copy), reads=p.b, writes=zb.b)
                        else:
                            c.op("dve", lambda e: e.tensor_copy(out=zb.t[:], in_=p.t[:]), reads=p.b, writes=zb.b)
                        r0 = tl["row0"] + j * 128
                        c.dma("sp", stsem(), s_z[r0:r0 + 128, nb * 512:(nb + 1) * 512], zb.t[:], reads=zb.b)
                    else:
                        c.op("dve", lambda e: e.tensor_tensor(out=dtx.t[:, j, :], in0=p.t[:, 0:64], in1=dtb_bc.t[:], op=ALU.add), reads=p.b + dtb_bc.b, writes=dtx.b)

            pending_T = []
            for nb in range(6):
                w = ws.next()
                banks = [ps() for _ in range(4)]
                ph = ps()
                for m in range(4):
                    for kk in range(8):
                        c.op("pe", lambda e: e.matmul(banks[m].t[:], lhsT=w.t[:, kk, m * 128:(m + 1) * 128], rhs=hn2.t[:, kk, 0:512], start=(kk == 0), stop=(kk == 7)),
                             reads=w.b + [hn2.b[kk]], writes=banks[m].b, signal=(kk == 7))
                    for kk in range(8):
                        c.op("pe", lambda e: e.matmul(ph.t[:, m * 4:m * 4 + 4], lhsT=w.t[:, kk, m * 128:(m + 1) * 128], rhs=hn2.t[:, kk, 512:516], start=(kk == 0), stop=(kk == 7)),
                             reads=w.b + [hn2.b[kk]], writes=ph.b, signal=(kk == 7))
                mcs = [nb * 4 + m for m in range(4)]
                for m in range(4):
                    c.op("act", lambda e: e.activation(out=cin[m].t[:, 2:514], in_=banks[m].t[:], func=AF.Copy), reads=banks[m].b, writes=cin[m].b)
                for m in range(4):
                    c.op("dve", lambda e: e.tensor_copy(out=cin[m].t[:, 0:2], in_=ph.t[:, m * 4:m * 4 + 2]), reads=ph.b, writes=cin[m].b)
                    c.op("dve", lambda e: e.tensor_copy(out=cin[m].t[:, 514:516], in_=ph.t[:, m * 4 + 2:m * 4 + 4]), reads=ph.b, writes=cin[m].b)
                if nb < 5:
                    tm_block(nb)
                while pending_T:
                    pending_T.pop(0)()
                for m in range(4):
                    mc = mcs[m]
                    c.op("act", lambda e: e.activation(out=acc[m].t[:], in_=cin[m].t[:, 0:512], func=AF.Identity, scale=convw.t[:, mc, 0:1], bias=convb.t[:, mc:mc + 1]),
                         reads=cin[m].b + convw.b + convb.b, writes=acc[m].b)
                for k5 in range(1, 5):
                    for m in range(4):
                        mc = mcs[m]
                        c.op("dve", lambda e: e.scalar_tensor_tensor(out=acc[m].t[:], in0=cin[m].t[:, k5:k5 + 512], scalar=convw.t[:, mc, k5:k5 + 1], in1=acc[m].t[:], op0=ALU.mult, op1=ALU.add),
                             reads=cin[m].b + acc[m].b + convw.b, writes=acc[m].b)
                dsts = []
                for m in range(4):
                    mc = mcs[m]
                    if mc < 16:
                        dst, dbufs = xcT[(nb % 2) * 4 + m].t[:], xcT[(nb % 2) * 4 + m].b
                    elif mc < 20:
                        dst, dbufs = BT.t[:, mc - 16, :], [BT.b[mc - 16]]
                    else:
                        dst, dbufs = CT.t[:, mc - 20, :], [CT.b[mc - 20]]
                    dsts.append((dst, dbufs))
                    c.op("act", lambda e: e.activation(out=dst, in_=acc[m].t[:], func=AF.Silu), reads=acc[m].b, writes=dbufs)
                if nb < 5:
                    def do_T(dsts=dsts, mcs=mcs):
                        pps = []
                        for m in range(4):
                            dst, dbufs = dsts[m]
                            p = ps()
                            pps.append(p)
                            pb = p.t[:].bitcast(BF16)
                            for j in range(4):
                                c.op("pe", lambda e: e.transpose(out=pb[:, j * 128:(j + 1) * 128], in_=dst[:, j * 128:(j + 1) * 128], identity=identb.t[:]),
                                     reads=dbufs + identb.b, writes=p.b, signal=(j == 3))
                        for m in range(4):
                            mc = mcs[m]
                            pb = pps[m].t[:].bitcast(BF16)
                            if mc < 16:
                                c.op("act", lambda e: e.activation(out=x_tm.t[:, :, mc * 128:(mc + 1) * 128], in_=pb[:, 0:512].rearrange("p (j i) -> p j i", j=4), func=AF.Copy), reads=pps[m].b, writes=x_tm.b)
                            else:
                                g = mc - 16
                                c.op("dve", lambda e: e.tensor_copy(out=B_tm.t[:, :, g * 128:(g + 1) * 128], in_=pb[:, 0:512].rearrange("p (j i) -> p j i", j=4)), reads=pps[m].b, writes=B_tm.b)
                    pending_T.append(do_T)
            while pending_T:
                pending_T.pop(0)()
            if t + 1 < NT:
                load_hn(t + 1)
            chk(2)
            if t == 1:
                chk(14)
            c.dma("sp", stsem(), s_ct[t], CT.t[:].rearrange("p g n -> p (g n)"), reads=CT.b)
            chk(3)
            if t == 1:
                chk(15)
            c.op("act", lambda e: e.activation(out=dtt.t[:], in_=dtx.t[:], func=AF.Abs), reads=dtx.b, writes=dtt.b)
            c.op("act", lambda e: e.activation(out=dtt.t[:], in_=dtt.t[:], func=AF.Exp, scale=-1.0), reads=dtt.b, writes=dtt.b)
            c.op("act", lambda e: e.activation(out=dtt.t[:], in_=dtt.t[:], func=AF.Ln, bias=epsc.t[:, 2:3]), reads=dtt.b + epsc.b, writes=dtt.b)
            c.op("dve", lambda e: e.scalar_tensor_tensor(out=dtv.t[:], in0=dtx.t[:], scalar=0.0, in1=dtt.t[:], op0=ALU.max, op1=ALU.add), reads=dtx.b + dtt.b, writes=dtv.b)
            c.op("act", lambda e: e.activation(out=lndt.t[:], in_=dtv.t[:], func=AF.Ln), reads=dtv.b, writes=lndt.b)
            c.op("dve", lambda e: e.tensor_tensor(out=da.t[:], in0=dtv.t[:], in1=a_bc.t[:].unsqueeze(1).to_broadcast([128, 4, 64]), op=ALU.mult), reads=dtv.b + a_bc.b, writes=da.b)
            pd = ps()
            pdv = pd.t[:].rearrange("p (j c) -> p j c", j=4)
            for j in range(4):
                c.op("pe", lambda e: e.matmul(pdv[:, j, 0:32], lhsT=tle.t[:], rhs=da.t[:, j, 0:32], start=True, stop=True), reads=tle.b + da.b, writes=pd.b, signal=False)
                c.op("pe", lambda e: e.matmul(pdv[:, j, 32:64], lhsT=tge.t[:], rhs=da.t[:, j, 32:64], start=True, stop=True), reads=tge.b + da.b, writes=pd.b, signal=False)
                c.op("pe", lambda e: e.matmul(pdv[:, j, 64:128], lhsT=onesf.t[:], rhs=da.t[:, j, 0:64], start=True, stop=True), reads=onesf.b + da.b, writes=pd.b, signal=(j == 3))
            c.op("act", lambda e: e.activation(out=cumx.t[:], in_=pdv[:, :, 0:64], func=AF.Copy), reads=pd.b, writes=cumx.b)
            c.op("dve", lambda e: e.tensor_tensor(out=biasx.t[:], in0=lndt.t[:], in1=pdv[:, :, 0:64], op=ALU.subtract), reads=lndt.b + pd.b, writes=biasx.b)
            c.op("act", lambda e: e.activation(out=ecum.t[:], in_=pdv[:, :, 0:64], func=AF.Exp), reads=pd.b, writes=ecum.b)
            c.op("dve", lambda e: e.tensor_tensor(out=wst.t[:], in0=biasx.t[:], in1=pdv[:, :, 64:128], op=ALU.add), reads=biasx.b + pd.b, writes=wst.b)
            c.op("act", lambda e: e.activation(out=wst.t[:], in_=wst.t[:], func=AF.Exp), reads=wst.b, writes=wst.b)
            c.op("act", lambda e: e.activation(out=cdec.t[:], in_=pdv[:, :, 64:128], func=AF.Exp), reads=pd.b, writes=cdec.b)
            for qi_, (src_, dst_, scr_) in enumerate(((biasx, bT, s_bT), (cumx, cT, s_cT))):
                pt_ = ps()
                for j in range(4):
                    c.op("pe", lambda e: e.transpose(out=pt_.t[0:64, j * 128:(j + 1) * 128], in_=src_.t[:, j, :], identity=ident.t[:]),
                         reads=src_.b + ident.b, writes=pt_.b, signal=(j == 3))
                c.op("dve", lambda e: e.tensor_copy(out=dst_.t[:], in_=pt_.t[0:64, :]), reads=pt_.b, writes=dst_.b)
                for part in range(3):
                    sp_ = spl[qi_][part]
                    c.op("dve", lambda e: e.tensor_copy(out=sp_.t[:], in_=dst_.t[:]), reads=dst_.b, writes=sp_.b)
                    if part < 2:
                        c.op("dve", lambda e: e.tensor_tensor(out=dst_.t[:], in0=dst_.t[:], in1=sp_.t[:], op=ALU.subtract), reads=dst_.b + sp_.b, writes=dst_.b)
                    c.dma("sp", stsem(), scr_[t, part], sp_.t[:], reads=sp_.b, writes=[scrb])
            chk(4)
            r0t = tl["row0"]
            with nc.allow_non_contiguous_dma(reason="small per-token rows"):
                for j in range(4):
                    c.dma("sp", stsem(), s_ecb[r0t + j * 128:r0t + (j + 1) * 128, :], ecum.t[:, j, 32:64], reads=ecum.b)
                    c.dma("sp", stsem(), s_cdb[ch0 + j], cdec.t[:, j, 32:64], reads=cdec.b)
            chk(5)
            if t == 1:
                chk(16)
            for j in range(4):
                cols = slice(j * 128, (j + 1) * 128)
                r0 = tl["row0"] + j * 128
                pcb = psum[6]
                for g in range(4):
                    c.op("pe", lambda e: e.matmul(pcb.t[:, g * 128:(g + 1) * 128], lhsT=BT.t[:, g, cols], rhs=CT.t[:, g, cols], start=True, stop=True),
                         reads=[BT.b[g], CT.b[g]], writes=pcb.b, signal=(g == 3))
                c.op("dve", lambda e: e.tensor_tensor(out=cbm[0].t[:].rearrange("p (g l) -> p g l", g=4), in0=pcb.t[:].rearrange("p (g l) -> p g l", g=4),
                                                     in1=tle.t[:].unsqueeze(1).to_broadcast([128, 4, 128]), op=ALU.mult), reads=pcb.b + tle.b, writes=cbm[0].b)
                c.op("dve", lambda e: e.tensor_tensor(out=cbm[1].t[:].rearrange("p (g l) -> p g l", g=4), in0=pcb.t[:].rearrange("p (g l) -> p g l", g=4),
                                                     in1=tge.t[:].unsqueeze(1).to_broadcast([128, 4, 128]), op=ALU.mult), reads=pcb.b + tge.b, writes=cbm[1].b)
                xw = xw2[j % 2]
                for d in range(2):
                    xwd = xw[d]
                    for g in range(4):
                        c.op("pool", lambda e: e.tensor_tensor(out=xwd.t[:, g * 512:(g + 1) * 512].rearrange("p (h d) -> p h d", h=8), in0=x_tm.t[:, j, g * 512:(g + 1) * 512].rearrange("p (h d) -> p h d", h=8),
                                                              in1=wst.t[:, j, d * 32 + g * 8:d * 32 + (g + 1) * 8].unsqueeze(2).to_broadcast([128, 8, 64]), op=ALU.mult),
                             reads=[x_tm.b[j]] + wst.b, writes=[xwd.b[g]])
                chk(6)
                rA, rB = rowsA[0], rowsB[0]
                for q in range(4):
                    c.dma("sp", ldsem(), rA.t[32 * q + 3:32 * q + 6, :, :], s_bT[t][:, 16 * q:16 * q + 16, cols], reads=[scrb], writes=rA.b)
                    c.dma("sp", ldsem(), rB.t[32 * q:32 * q + 3, :, :], s_cT[t][:, 16 * q:16 * q + 16, cols], reads=[scrb], writes=rB.b)

                def emit_T(g):
                    info = []
                    for d in range(2):
                        it = itc[0]
                        itc[0] += 1
                        rb = [psum[4 + 2 * (it % 2)], psum[5 + 2 * (it % 2)]]
                        for hh in range(8):
                            col = d * 32 + g * 8 + hh
                            q, h16 = col // 16, col % 16
                            pr = rb[hh // 4]
                            c.op("pe", lambda e: e.matmul(pr.t[:, (hh % 4) * 128:(hh % 4 + 1) * 128], lhsT=rA.t[32 * q:32 * q + 6, h16, :], rhs=rB.t[32 * q:32 * q + 6, h16, :],
                                                          start=True, stop=True, tile_position=(32 * q, 0)),
                                 reads=rA.b + rB.b, writes=pr.b, signal=(hh % 4 == 3))
                        info.append((it, rb))
                    return info

                nxt = emit_T(0)
                for g in range(4):
                    cur = nxt
                    mts = []
                    for d in range(2):
                        it, rb = cur[d]
                        ep = Ep[it % 2]
                        mt = Mt[it % 4]
                        mts.append(mt)
                        for hb in range(2):
                            pr = rb[hb]
                            c.op("act", lambda e: e.activation(out=ep.t[:, hb * 4:(hb + 1) * 4, :], in_=pr.t[:].rearrange("p (h l) -> p h l", h=4), func=AF.Exp), reads=pr.b, writes=ep.b)
                        c.op("dve", lambda e: e.scalar_tensor_tensor(out=mt.t[:], in0=ep.t[:], scalar=1e30, in1=cbm[d].t[:, g * 128:(g + 1) * 128].unsqueeze(1).to_broadcast([128, 8, 128]),
                                                                    op0=ALU.min, op1=ALU.mult), reads=ep.b + cbm[d].b, writes=mt.b)
                    if g < 3:
                        nxt = emit_T(g + 1)
                    for hh in range(8):
                        h = g * 8 + hh
                        xs_ = x_tm.t[:, j, h * 64:(h + 1) * 64]
                        c.op("pe", lambda e: e.matmul(psum[g].t[:, hh * 64:(hh + 1) * 64], lhsT=mts[0].t[:, hh, :], rhs=xs_, start=True, stop=False),
                             reads=mts[0].b + [x_tm.b[j]], writes=psum[g].b, signal=False)
                        c.op("pe", lambda e: e.matmul(psum[g].t[:, hh * 64:(hh + 1) * 64], lhsT=mts[1].t[:, hh, :], rhs=xs_, start=False, stop=False),
                             reads=mts[1].b + [x_tm.b[j]], writes=psum[g].b, signal=False)
                        c.op("pe", lambda e: e.matmul(psum[g].t[:, hh * 64:(hh + 1) * 64], lhsT=Dd.t[:, h, :], rhs=xs_, start=False, stop=True),
                             reads=Dd.b + [x_tm.b[j]], writes=psum[g].b, signal=(hh == 7))
                chk(7)
                for g in range(4):
                    c.op("pe", lambda e: e.matmul(psum[4 + g].t[:], lhsT=CT.t[:, g, cols], rhs=hstf_bf.t[:, g * 512:(g + 1) * 512], start=True, stop=True),
                         reads=[CT.b[g], hstf_bf.b[g]], writes=psum[4 + g].b)
                for g in range(4):
                    t1 = t1s[t1c[0] % 2]
                    t1c[0] += 1
                    c.op("dve", lambda e: e.tensor_tensor(out=t1.t[:].rearrange("p (h d) -> p h d", h=8), in0=psum[4 + g].t[:].rearrange("p (h d) -> p h d", h=8),
                                                         in1=ecum.t[:, j, g * 8:(g + 1) * 8].unsqueeze(2).to_broadcast([128, 8, 64]), op=ALU.mult), reads=psum[4 + g].b + ecum.b, writes=t1.b)
                    c.op("dve", lambda e: e.tensor_tensor(out=Ysb.t[:, g * 512:(g + 1) * 512], in0=psum[g].t[:], in1=t1.t[:], op=ALU.add),
                         reads=t1.b + psum[g].b, writes=[Ysb.b[g]])
                c.dma("sp", stsem(), s_yp[r0:r0 + 128, :], Ysb.t[:], reads=Ysb.b)
                chk(8)
                for g in range(4):
                    c.op("pe", lambda e: e.matmul(psum[g].t[:], lhsT=B_tm.t[:, j, g * 128:(g + 1) * 128], rhs=xw[0].t[:, g * 512:(g + 1) * 512], start=True, stop=True),
                         reads=[B_tm.b[j], xw[0].b[g]], writes=psum[g].b)
                for g in range(4):
                    c.op("pool", lambda e: e.tensor_tensor(out=hstf.t[:, g * 512:(g + 1) * 512].rearrange("p (h d) -> p h d", h=8), in0=hstf.t[:, g * 512:(g + 1) * 512].rearrange("p (h d) -> p h d", h=8),
                                                          in1=cdec.t[:, j, g * 8:(g + 1) * 8].unsqueeze(2).to_broadcast([128, 8, 64]), op=ALU.mult), reads=[hstf.b[g]] + cdec.b, writes=[hstf.b[g]])
                for g in range(4):
                    c.op("dve", lambda e: e.tensor_tensor(out=hstf.t[:, g * 512:(g + 1) * 512], in0=hstf.t[:, g * 512:(g + 1) * 512], in1=psum[g].t[:], op=ALU.add),
                         reads=[hstf.b[g]] + psum[g].b, writes=[hstf.b[g]])
                for g in range(4):
                    c.op("act", lambda e: e.activation(out=hstf_bf.t[:, g * 512:(g + 1) * 512], in_=hstf.t[:, g * 512:(g + 1) * 512], func=AF.Copy), reads=[hstf.b[g]], writes=[hstf_bf.b[g]])
                for g in range(4):
                    c.op("pe", lambda e: e.matmul(psum[g].t[:], lhsT=B_tm.t[:, j, g * 128:(g + 1) * 128], rhs=xw[1].t[:, g * 512:(g + 1) * 512], start=True, stop=True),
                         reads=[B_tm.b[j], xw[1].b[g]], writes=psum[g].b)
                for g in range(4):
                    c.op("act", lambda e: e.activation(out=Sbsb.t[:, g * 512:(g + 1) * 512], in_=psum[g].t[:], func=AF.Copy), reads=psum[g].b, writes=[Sbsb.b[g]])
                c.dma("sp", stsem(), s_sb[ch0 + j], Sbsb.t[:], reads=Sbsb.b)
                chk(10)
                if j == 3:
                    chk(11)

        try:
            for t in range(NT):
                p2_tile(t)
        except _Stop:
            pass

    def sched_p3():
        for t in range(NT):
            for nb in range(2):
                for kb in range(2):
                    yield wblock(s_wout, kb, nb)
            for b in range(8):
                yield wblock(s_up[1], 0, b)
            for nb in range(2):
                for kb in range(4):
                    yield wblock(s_dn[1], kb, nb)

    if phases >= 3:
      c.barrier()
      with ExitStack() as es3:
        def sb3(name, shape, dt=F32, n=1):
            return TB(es3.enter_context(nc.sbuf_tensor(name, list(shape), dt)), n)
        ring = [sb3("ringc%d" % i, [128, 8, 512], BF16) for i in range(NRING)]
        ws = WStream(sched_p3(), ring)
        x1T = sb3("x1T", [128, 8, T], F32, 8)
        CT3 = sb3("CT3", [128, 4, T], BF16)
        yps = [sb3("yp%d" % i, [128, DIN], F32, 4) for i in range(2)]
        zzs = [sb3("zz%d" % i, [128, DIN], F32, 4) for i in range(2)]
        ecbs = [sb3("ecb%d" % i, [128, NH]) for i in range(2)]
        Sb3s = [sb3("Sb3_%d" % i, [128, DIN], F32, 4) for i in range(2)]
        cdbs = [sb3("cdb%d" % i, [128, NH]) for i in range(2)]
        hstb = sb3("hstb", [128, DIN], F32, 4)
        hstb_bf = sb3("hstb_bf", [128, DIN], BF16, 4)
        t3s = [sb3("t3_%d" % i, [128, 512]) for i in range(2)]
        ss = sb3("ss", [128, 8])
        mhalf = sb3("mhalf", [128, 1])
        c.op("pool", lambda e: e.memset(mhalf.t[:], -0.5), writes=mhalf.b)
        Gn = sb3("Gn", [128, DIN], BF16)
        gT = sb3("gT", [128, 16, T], BF16, 16)
        asb3 = sb3("asb3", [128, 4096], F32, 8)
        rstd3 = sb3("rstd3", [128, T])
        hn3 = sb3("hn3", [128, 8, T], BF16, 8)
        h1T3 = sb3("h1T3", [128, 32, T], BF16, 32)
        sqb3 = TB(h1T3.t[:, 0:8, :], 0)
        sqb3.b = h1T3.b[0:8]
        tmp3 = [sb3("tmp3_%d" % i, [128, T]) for i in range(2)]
        order = []
        for si in range(len(seq_lens)):
            ts_ = [tt for tt in range(NT) if tiles[tt]["seq"] == si]
            order += ts_[::-1]
        chunks = [(t, j) for t in order for j in (3, 2, 1, 0)]

        def prefetch(i):
            if i >= len(chunks):
                return
            t, j = chunks[i]
            tl = tiles[t]
            r0 = tl["row0"] + j * 128
            chn = tl["row0"] // 128 + j
            pi = i % 2
            c.dma("sp", ldsem(), yps[pi].t[:], s_yp[r0:r0 + 128, :], writes=yps[pi].b)
            c.dma("sp", ldsem(), zzs[pi].t[:], s_z[r0:r0 + 128, :], writes=zzs[pi].b)
            c.dma("sp", ldsem(), Sb3s[pi].t[:], s_sb[chn], writes=Sb3s[pi].b)
            with nc.allow_non_contiguous_dma(reason="small per-token rows"):
                c.dma("sp", ldsem(), ecbs[pi].t[:], s_ecb[r0:r0 + 128, :], writes=ecbs[pi].b)
                c.dma("sp", ldsem(), cdbs[pi].t[:], s_cdb[chn], writes=cdbs[pi].b)

        prefetch(0)
        ci = [0]
        t3c = [0]

        def CL(t):
            tl = tiles[t]
            c.dma("sp", ldsem(), CT3.t[:], s_ct[t].rearrange("p (g n) -> p g n", g=4), writes=CT3.b)
            if tl["last"]:
                c.op("pool", lambda e: e.memset(hstb.t[:], 0.0), writes=hstb.b)
                c.op("pool", lambda e: e.memset(hstb_bf.t[:], 0.0), writes=hstb_bf.b)
            for j in (3, 2, 1, 0):
                cols = slice(j * 128, (j + 1) * 128)
                i = ci[0]
                ci[0] += 1
                assert chunks[i] == (t, j)
                prefetch(i + 1)
                yp, zz, ecb, Sb3, cdb = yps[i % 2], zzs[i % 2], ecbs[i % 2], Sb3s[i % 2], cdbs[i % 2]
                G4 = [slice(g * 512, (g + 1) * 512) for g in range(4)]
                for g in range(4):
                    t3 = t3s[t3c[0] % 2]
                    t3c[0] += 1
                    py = ps()
                    c.op("pe", lambda e: e.matmul(py.t[:], lhsT=CT3.t[:, g, cols], rhs=hstb_bf.t[:, G4[g]], start=True, stop=True), reads=CT3.b + [hstb_bf.b[g]], writes=py.b)
                    c.op("dve", lambda e: e.tensor_tensor(out=t3.t[:].rearrange("p (h d) -> p h d", h=8), in0=py.t[:].rearrange("p (h d) -> p h d", h=8),
                                                         in1=ecb.t[:, g * 8:(g + 1) * 8].unsqueeze(2).to_broadcast([128, 8, 64]), op=ALU.mult), reads=py.b + ecb.b, writes=t3.b)
                    c.op("pool", lambda e: e.tensor_tensor(out=yp.t[:, G4[g]], in0=yp.t[:, G4[g]], in1=t3.t[:], op=ALU.add), reads=t3.b + [yp.b[g]], writes=[yp.b[g]])
                yield
                for g in range(4):
                    c.op("dve", lambda e: e.tensor_tensor(out=hstb.t[:, G4[g]].rearrange("p (h d) -> p h d", h=8), in0=hstb.t[:, G4[g]].rearrange("p (h d) -> p h d", h=8),
                                                         in1=cdb.t[:, g * 8:(g + 1) * 8].unsqueeze(2).to_broadcast([128, 8, 64]), op=ALU.mult), reads=[hstb.b[g]] + cdb.b, writes=[hstb.b[g]])
                for g in range(4):
                    eng = "pool" if g < 2 else "dve"
                    c.op(eng, lambda e: e.tensor_tensor(out=hstb.t[:, G4[g]], in0=hstb.t[:, G4[g]], in1=Sb3.t[:, G4[g]], op=ALU.add), reads=[hstb.b[g], Sb3.b[g]], writes=[hstb.b[g]])
                for g in range(4):
                    c.op("pool", lambda e: e.tensor_copy(out=hstb_bf.t[:, G4[g]], in_=hstb.t[:, G4[g]]), reads=[hstb.b[g]], writes=[hstb_bf.b[g]])
                yield
                for g in range(4):
                    c.op("act", lambda e: e.activation(out=zz.t[:, G4[g]], in_=zz.t[:, G4[g]], func=AF.Silu), reads=[zz.b[g]], writes=[zz.b[g]])
                for g in range(4):
                    c.op("dve", lambda e: e.tensor_tensor(out=yp.t[:, G4[g]], in0=yp.t[:, G4[g]], in1=zz.t[:, G4[g]], op=ALU.mult), reads=[yp.b[g], zz.b[g]], writes=[yp.b[g]])
                for g in range(4):
                    c.op("act", lambda e: e.activation(out=zz.t[:, G4[g]], in_=yp.t[:, G4[g]], func=AF.Square, accum_out=ss.t[:, g:g + 1]), reads=[yp.b[g]], writes=[zz.b[g]] + ss.b)
                yield
                c.op("dve", lambda e: e.tensor_reduce(out=ss.t[:, 4:5], in_=ss.t[:, 0:4], op=ALU.add, axis=mybir.AxisListType.X), reads=ss.b, writes=ss.b)
                c.op("pool", lambda e: e.tensor_scalar(out=ss.t[:, 5:6], in0=ss.t[:, 4:5], scalar1=1.0 / DIN, scalar2=GEPS, op0=ALU.mult, op1=ALU.add), reads=ss.b, writes=ss.b)
                c.op("pool", lambda e: e.tensor_tensor(out=ss.t[:, 4:5], in0=ss.t[:, 5:6], in1=mhalf.t[:, 0:1], op=ALU.pow), reads=ss.b + mhalf.b, writes=ss.b)
                c.op("dve", lambda e: e.tensor_scalar(out=Gn.t[:], in0=yp.t[:], scalar1=ss.t[:, 4:5], scalar2=None, op0=ALU.mult), reads=yp.b + ss.b, writes=Gn.b)
                yield
                for half in range(2):
                    p = ps()
                    pb = p.t[:].bitcast(BF16)
                    for kk in range(8):
                        k = half * 8 + kk
                        c.op("pe", lambda e: e.transpose(out=pb[:, kk * 128:(kk + 1) * 128], in_=Gn.t[:, k * 128:(k + 1) * 128], identity=identb.t[:]),
                             reads=Gn.b + identb.b, writes=p.b, signal=(kk == 7))
                    c.op("act", lambda e: e.activation(out=gT.t[:, half * 8:(half + 1) * 8, cols], in_=pb.rearrange("p (k i) -> p k i", k=8), func=AF.Copy), reads=p.b, writes=gT.b[half * 8:(half + 1) * 8])
                yield

        def HEAD(t):
            tl = tiles[t]
            c.dma("sp", ldsem(), x1T.t[:], s_x1[t].rearrange("p (k n) -> p k n", k=8), writes=x1T.b)

            def ep_out(nb, m, p):
                jj = nb * 4 + m
                c.op("act", lambda e: e.activation(out=asb3.t[:, jj * T:(jj + 1) * T], in_=p.t[:], func=AF.Copy), reads=p.b, writes=[asb3.b[jj]])
            yield from linear_g(ws, lambda k: (gT.t[:, k, :], [gT.b[k]]), 16, 2, 2, ep_out)
            post_norm_residual(x1T, asb3, sqb3, rstd3, 5, tmp3)
            yield
            yield from mlp_g(ws, 1, x1T, hn3, h1T3, asb3, sqb3, rstd3, tmp3)
            store_tokmajor_alias(x1T, tl["row0"], asb3)
            yield

        def drive(gh, gc, lead):
            nh = 0
            dh = gh is None
            dc = gc is None
            while not (dh and dc):
                if not dh:
                    cur_pool[0] = "A"
                    try:
                        next(gh)
                    except StopIteration:
                        dh = True
                    nh += 1
                if not dc and (dh or nh > lead):
                    cur_pool[0] = "B"
                    try:
                        next(gc)
                    except StopIteration:
                        dc = True
            cur_pool[0] = "all"

        drive(None, CL(order[0]), 0)
        for oi, t in enumerate(order):
            nxt = CL(order[oi + 1]) if oi + 1 < len(order) else None
            drive(HEAD(t), nxt, 4)

    c.final_wait("sp")
    es.close()
    return nc, c


WNAMES = ["attn_w_qkv", "attn_w_o", "attn_sink", "ssm_w_in", "ssm_conv_w", "ssm_conv_b", "ssm_dt_bias", "ssm_a_log",
          "ssm_d", "ssm_norm_w", "ssm_w_out", "norm_mix_pre", "norm_mix_post", "norm_ffn_pre", "norm_ffn_post",
          "mlp_w_up", "mlp_w_down"]


def _wmap(inputs, lmax):
    m = {}
    f = lambda a: np.ascontiguousarray(np.asarray(a, dtype=np.float32))
    m["attn_w_qkv"] = f(inputs["attn_w_qkv"])[0]
    m["attn_w_o"] = f(inputs["attn_w_o"])[0]
    m["attn_sink"] = f(inputs["attn_sink"]).reshape(1, NQH)
    m["ssm_w_in"] = f(inputs["ssm_w_in"])[0]
    m["ssm_conv_w"] = f(inputs["ssm_conv_w"])[0]
    m["ssm_conv_b"] = f(inputs["ssm_conv_b"]).reshape(1, CONVD)
    m["ssm_dt_bias"] = f(inputs["ssm_dt_bias"]).reshape(1, 64)
    m["ssm_a_log"] = f(inputs["ssm_a_log"]).reshape(1, 64)
    m["ssm_d"] = f(inputs["ssm_d"]).reshape(1, NH)
    m["ssm_norm_w"] = f(inputs["ssm_norm_w"]).reshape(1, DIN)
    m["ssm_w_out"] = f(inputs["ssm_w_out"])[0]
    for n in ("norm_mix_pre", "norm_mix_post", "norm_ffn_pre", "norm_ffn_post", "mlp_w_up", "mlp_w_down"):
        m[n] = f(inputs[n])
    m.update(_consts(lmax))
    return m


def kernel(**inputs):
    xp = np.asarray(inputs["x_prompt"], dtype=np.float32)
    xs = np.asarray(inputs["x_sample"], dtype=np.float32)
    B, L, _ = xp.shape
    Bs, Ls, _ = xs.shape
    nc, _ = build([L, Ls])
    wm = _wmap(inputs, max(L, Ls))
    in_maps = []
    for i in range(8):
        m = dict(wm)
        m["x"] = np.ascontiguousarray(np.concatenate([xp[i], xs[i % Bs]], axis=0))
        in_maps.append(m)
    res = run_bass_kernel_spmd(nc, in_maps, core_ids=list(range(8)))
    yp = np.stack([res.results[i]["y"][:L] for i in range(8)], 0)
    ys = np.stack([res.results[i]["y"][L:] for i in range(Bs)], 0)
    return (yp.astype(np.float32), ys.astype(np.float32))
```
